# Optimizing a Trainium2 kernel written in Bass

```python
import jax, jax.numpy as jnp
from jax import lax
import numpy as np

D_MODEL = 1024
BATCH = 4
SEQ = 4096
DEPTH = 4
DEC_BATCH = 128
DEC_SEQ = 4
PAST_LEN = 2048
PAGE_SIZE = 128

N_A = DEPTH // 2
N_B = DEPTH - N_A
D_FF = 2816
PLE_DIM = 256
CHUNK = 128
D_GMLP = D_MODEL
GMLP_GROUPS = 8
GMLP_GROUP_DIM = D_GMLP // GMLP_GROUPS
N_HEADS = 8
HEAD_DIM = D_MODEL // N_HEADS
D_ATT = N_HEADS * HEAD_DIM
Q_BLOCK = 128
EPS = 1e-6

kernel_name = 'yoco_gmlp_fox_macaron_decoder_step'


def rmsnorm(x, g):
    xf = x.astype(jnp.float32)
    y = xf * lax.rsqrt(jnp.mean(xf * xf, axis=-1, keepdims=True) + EPS)
    return (y * g.astype(jnp.float32)).astype(x.dtype)


def swiglu(x, w1, w3, w2):
    return (jax.nn.silu(x @ w1) * (x @ w3)) @ w2


def gmlp_mix(xn, w_in, v_norm, w_s, b_s, w_out):
    bsz, L, _ = xn.shape
    u, v = jnp.split(xn @ w_in, 2, axis=-1)
    v = rmsnorm(v, v_norm)
    c = min(L, CHUNK)
    n_ch = -(-L // c)
    pad = n_ch * c - L
    vp = jnp.pad(v, ((0, 0), (0, pad), (0, 0)))
    vg = vp.reshape(bsz, n_ch, c, GMLP_GROUPS, GMLP_GROUP_DIM)
    mask = jnp.tril(jnp.ones((c, c), dtype=bool)).astype(w_s.dtype)
    ws = w_s[:, :c, :c] * mask[None]
    s = jnp.einsum('gts,bnsgd->bntgd', ws, vg)
    s = s + b_s[:, :c].T[None, None, :, :, None]
    s = s.reshape(bsz, n_ch * c, D_GMLP)[:, :L]
    return (u * s) @ w_out, v


def shared_kv(h, kv_norm, w_kvf, b_f, k_norm):
    bsz, L, _ = h.shape
    xs = rmsnorm(h, kv_norm)
    kvf = xs @ w_kvf
    k = rmsnorm(kvf[..., :D_ATT].reshape(bsz, L, N_HEADS, HEAD_DIM), k_norm)
    v = kvf[..., D_ATT:2 * D_ATT].reshape(bsz, L, N_HEADS, HEAD_DIM)
    logf = jax.nn.log_sigmoid((kvf[..., 2 * D_ATT:] + b_f).astype(jnp.float32))
    return k, v, logf


def fox_prompt(q, k, v, logf):
    bsz, S, _, _ = q.shape
    scale = HEAD_DIM ** -0.5
    csum = jnp.cumsum(logf.astype(jnp.float32), axis=1).transpose(0, 2, 1)
    kpos = jnp.arange(S)

    def block(i):
        qs = i * Q_BLOCK
        qb = lax.dynamic_slice_in_dim(q, qs, Q_BLOCK, axis=1)
        cq = lax.dynamic_slice_in_dim(csum, qs, Q_BLOCK, axis=2)
        s = jnp.einsum('bqhd,bkhd->bhqk', qb, k).astype(jnp.float32) * scale
        s = s + cq[..., :, None] - csum[..., None, :]
        qpos = qs + jnp.arange(Q_BLOCK)
        causal = kpos[None, :] <= qpos[:, None]
        s = jnp.where(causal[None, None], s, -jnp.inf)
        p = jax.nn.softmax(s, axis=-1).astype(v.dtype)
        return jnp.einsum('bhqk,bkhd->bqhd', p, v)

    out = lax.map(block, jnp.arange(S // Q_BLOCK))
    return out.transpose(1, 0, 2, 3, 4).reshape(bsz, S, N_HEADS, HEAD_DIM)


def fox_sample(q, k_new, v_new, lf_new, k_past, v_past, lf_past):
    T = q.shape[1]
    P = k_past.shape[1]
    scale = HEAD_DIM ** -0.5
    c_new = jnp.cumsum(lf_new.astype(jnp.float32), axis=1).transpose(0, 2, 1)
    c_past = jnp.cumsum(lf_past.astype(jnp.float32), axis=1)
    tail = (c_past[:, -1:] - c_past).transpose(0, 2, 1)
    s_past = jnp.einsum('bqhd,bkhd->bhqk', q, k_past).astype(jnp.float32) * scale
    s_past = s_past + c_new[..., :, None] + tail[..., None, :]
    s_new = jnp.einsum('bqhd,bkhd->bhqk', q, k_new).astype(jnp.float32) * scale
    s_new = s_new + c_new[..., :, None] - c_new[..., None, :]
    causal = jnp.tril(jnp.ones((T, T), dtype=bool))
    s_new = jnp.where(causal[None, None], s_new, -jnp.inf)
    p = jax.nn.softmax(jnp.concatenate([s_past, s_new], axis=-1), axis=-1).astype(v_new.dtype)
    return (jnp.einsum('bhqk,bkhd->bqhd', p[..., :P], v_past)
            + jnp.einsum('bhqk,bkhd->bqhd', p[..., P:], v_new))


def setup_inputs(seed: int = 0) -> dict:
    key = jax.random.key(seed)
    ks = iter(jax.random.split(key, 48))
    f32 = jnp.float32

    def nrm(shape, scale):
        return jax.random.normal(next(ks), shape, f32) * scale

    def gain(shape):
        return 1.0 + 0.05 * jax.random.normal(next(ks), shape, f32)

    n_pages = PAST_LEN // PAGE_SIZE
    n_used = DEC_BATCH * n_pages
    n_pool = n_used + n_used // 4
    page_table = jax.random.permutation(next(ks), n_pool)[:n_used].reshape(DEC_BATCH, n_pages).astype(jnp.int32)
    d = D_MODEL
    return {
        'x_prompt': nrm((BATCH, SEQ, d), 1.0),
        'x_sample': nrm((DEC_BATCH, DEC_SEQ, d), 1.0),
        'p_prompt': nrm((DEPTH, BATCH, SEQ, PLE_DIM), 1.0),
        'p_sample': nrm((DEPTH, DEC_BATCH, DEC_SEQ, PLE_DIM), 1.0),
        'cache_k': nrm((n_pool, PAGE_SIZE, N_HEADS, HEAD_DIM), 1.0),
        'cache_v': nrm((n_pool, PAGE_SIZE, N_HEADS, HEAD_DIM), 1.0),
        'cache_logf': jax.nn.log_sigmoid(3.0 + nrm((n_pool, PAGE_SIZE, N_HEADS), 1.0)),
        'page_table': page_table,
        'ffn1_norm': gain((DEPTH, d)),
        'ffn1_w1': nrm((DEPTH, d, D_FF), d ** -0.5),
        'ffn1_w3': nrm((DEPTH, d, D_FF), d ** -0.5),
        'ffn1_w2': nrm((DEPTH, D_FF, d), D_FF ** -0.5),
        'mix_norm': gain((DEPTH, d)),
        'ffn2_norm': gain((DEPTH, d)),
        'ffn2_w1': nrm((DEPTH, d, D_FF), d ** -0.5),
        'ffn2_w3': nrm((DEPTH, d, D_FF), d ** -0.5),
        'ffn2_w2': nrm((DEPTH, D_FF, d), D_FF ** -0.5),
        'ple_norm': gain((DEPTH, d)),
        'ple_w_gate': nrm((DEPTH, d, d), d ** -0.5),
        'ple_w_proj': nrm((DEPTH, PLE_DIM, d), PLE_DIM ** -0.5),
        'gmlp_w_in': nrm((N_A, d, 2 * D_GMLP), d ** -0.5),
        'gmlp_v_norm': gain((N_A, D_GMLP)),
        'gmlp_w_s': nrm((N_A, GMLP_GROUPS, CHUNK, CHUNK), CHUNK ** -0.5),
        'gmlp_b_s': 1.0 + 0.1 * jax.random.normal(next(ks), (N_A, GMLP_GROUPS, CHUNK), f32),
        'gmlp_w_out': nrm((N_A, D_GMLP, d), D_GMLP ** -0.5),
        'kv_norm': gain((d,)),
        'w_kvf': nrm((d, 2 * D_ATT + N_HEADS), d ** -0.5),
        'b_f': 3.0 + 0.5 * jax.random.normal(next(ks), (N_HEADS,), f32),
        'k_norm': gain((HEAD_DIM,)),
        'att_w_q': nrm((N_B, d, D_ATT), d ** -0.5),
        'q_norm': gain((N_B, HEAD_DIM)),
        'att_w_o': nrm((N_B, D_ATT, d), D_ATT ** -0.5),
    }


def reference(x_prompt, x_sample, p_prompt, p_sample, cache_k, cache_v, cache_logf, page_table,
              ffn1_norm, ffn1_w1, ffn1_w3, ffn1_w2, mix_norm, ffn2_norm, ffn2_w1, ffn2_w3, ffn2_w2,
              ple_norm, ple_w_gate, ple_w_proj, gmlp_w_in, gmlp_v_norm, gmlp_w_s, gmlp_b_s, gmlp_w_out,
              kv_norm, w_kvf, b_f, k_norm, att_w_q, q_norm, att_w_o):

    def trunk(x, pe, past):
        h = x
        bsz, L, _ = x.shape
        kv = None
        gmlp_vs = []
        for i in range(DEPTH):
            h = h + 0.5 * swiglu(rmsnorm(h, ffn1_norm[i]), ffn1_w1[i], ffn1_w3[i], ffn1_w2[i])
            xn = rmsnorm(h, mix_norm[i])
            if i < N_A:
                mix, v_rows = gmlp_mix(xn, gmlp_w_in[i], gmlp_v_norm[i], gmlp_w_s[i], gmlp_b_s[i], gmlp_w_out[i])
                gmlp_vs.append(v_rows)
            else:
                j = i - N_A
                q = rmsnorm((xn @ att_w_q[j]).reshape(bsz, L, N_HEADS, HEAD_DIM), q_norm[j])
                k, v, lf = kv
                if past is None:
                    o = fox_prompt(q, k, v, lf)
                else:
                    o = fox_sample(q, k, v, lf, past[0], past[1], past[2])
                mix = o.reshape(bsz, L, D_ATT) @ att_w_o[j]
            h = h + mix
            h = h + 0.5 * swiglu(rmsnorm(h, ffn2_norm[i]), ffn2_w1[i], ffn2_w3[i], ffn2_w2[i])
            gate = jax.nn.sigmoid(rmsnorm(h, ple_norm[i]) @ ple_w_gate[i])
            h = h + gate * (pe[i] @ ple_w_proj[i])
            if i == N_A - 1:
                kv = shared_kv(h, kv_norm, w_kvf, b_f, k_norm)
        return h, kv, jnp.stack(gmlp_vs)

    y_prompt, (k_prompt, v_prompt, logf_prompt), _ = trunk(x_prompt, p_prompt, None)

    bd, n_pages = page_table.shape
    past_len = n_pages * PAGE_SIZE
    k_past = cache_k[page_table].reshape(bd, past_len, N_HEADS, HEAD_DIM)
    v_past = cache_v[page_table].reshape(bd, past_len, N_HEADS, HEAD_DIM)
    lf_past = cache_logf[page_table].reshape(bd, past_len, N_HEADS)
    y_sample, (k_sample, v_sample, logf_sample), gmlp_v_sample = trunk(x_sample, p_sample, (k_past, v_past, lf_past))

    return (y_prompt, y_sample, k_prompt, v_prompt, logf_prompt, k_sample, v_sample, logf_sample, gmlp_v_sample)
```

```python
import numpy as np
import concourse.bass as bass
import concourse.mybir as mybir
from concourse.bass_utils import run_bass_kernel_spmd

F32 = mybir.dt.float32
BF16 = mybir.dt.bfloat16
I32 = mybir.dt.int32
AF = mybir.ActivationFunctionType
ALU = mybir.AluOpType
AX = mybir.AxisListType

ENGS = ("pe", "act", "dve", "pool", "sp")

D = 1024
NCH = 8
DFF = 2816
NFC = 22
PLE = 256
DEPTH = 4
N_A = 2
NH = 8
HD = 128
TILE = 512
NPT = 2048
NSQ = 16
NST = NSQ * 4
NT = NPT + NST
MAIN_TILES = [(0, 512), (512, 512), (1024, 512), (1536, 512), (2048, NST)]
PREV_TILES = [(0, 512), (512, 512), (1024, 512), (1536, 512)]
NBLK = 17
NPG = 16
NPOOL = 2560
SCALE = HD ** -0.5
NEG = -30000.0
PGG = 4
BGD = 3
EPS = 1e-6
SLOT_EL = 12288
FFN_GROUPS = [(0, 4), (4, 4), (8, 4), (12, 4), (16, 4), (20, 2)]


class Trk:
    __slots__ = ("name", "last_w", "reads")

    def __init__(self, name=""):
        self.name = name
        self.last_w = None
        self.reads = []


class Chan:
    def __init__(self, ctx, name):
        self.key = "dma_" + name
        self.sem = ctx.nc.alloc_semaphore(name=self.key)
        self.count = 0
        ctx.semh[self.key] = self.sem


class Ctx:
    def __init__(self, nc, same_engine_sync=True):
        self.nc = nc
        self.eng = {"pe": nc.tensor, "act": nc.scalar, "dve": nc.vector,
                    "pool": nc.gpsimd, "sp": nc.sync}
        self.semh = {}
        self.cnt = {}
        for e in ENGS:
            self.semh[e] = nc.alloc_semaphore(name="sem_" + e)
            self.cnt[e] = 0
        self.seen = {e: {} for e in ENGS}
        self.same_engine_sync = same_engine_sync
        self.chans = []
        self.n_ins = 0
        self.all_trk = []

    def trk(self, name=""):
        t = Trk(name)
        self.all_trk.append(t)
        return t

    def chan(self, name):
        c = Chan(self, name)
        self.chans.append(c)
        return c

    def _wait(self, e, events):
        need = {}
        for ev in events:
            if ev is None:
                continue
            k, v = ev
            if k == e and (e == "pe" or not self.same_engine_sync):
                continue
            if need.get(k, 0) < v:
                need[k] = v
        seen = self.seen[e]
        for k, v in need.items():
            if seen.get(k, 0) < v:
                self.eng[e].wait_ge(self.semh[k], v)
                seen[k] = v

    @staticmethod
    def _deps(reads, writes):
        evs = []
        for t in reads:
            evs.append(t.last_w)
        for t in writes:
            evs.append(t.last_w)
            evs.extend(t.reads)
        return evs

    @staticmethod
    def _commit(ev, reads, writes):
        for t in reads:
            t.reads.append(ev)
            if len(t.reads) > 32:
                mx = {}
                for k, v in t.reads:
                    if mx.get(k, 0) < v:
                        mx[k] = v
                t.reads = list(mx.items())
        for t in writes:
            t.last_w = ev
            t.reads = []

    def op(self, e, fn, reads=(), writes=()):
        self._wait(e, self._deps(reads, writes))
        ins = fn()
        self.cnt[e] += 1
        ins.then_inc(self.semh[e], 1)
        ev = (e, self.cnt[e])
        self._commit(ev, reads, writes)
        self.n_ins += 1
        return ev

    def mm_group(self, out_ap, pairs, reads=(), writes=()):
        e = "pe"
        self._wait(e, self._deps(reads, writes))
        n = len(pairs)
        ins = None
        for i, (l, r) in enumerate(pairs):
            ins = self.nc.tensor.matmul(out_ap, l, r, start=(i == 0), stop=(i == n - 1))
        self.n_ins += n
        self.cnt[e] += 1
        ins.then_inc(self.semh[e], 1)
        ev = (e, self.cnt[e])
        self._commit(ev, reads, writes)
        return ev

    def mm(self, out_ap, lhsT, rhs, start, stop, reads=(), writes=()):
        e = "pe"
        self._wait(e, self._deps(reads, writes))
        ins = self.nc.tensor.matmul(out_ap, lhsT, rhs, start=start, stop=stop)
        self.n_ins += 1
        self.cnt[e] += 1
        ins.then_inc(self.semh[e], 1)
        ev = (e, self.cnt[e])
        self._commit(ev, reads, writes)
        return ev

    def idma(self, chan, out, in_, idx_ap, reads=(), writes=()):
        q = "pool"
        self._wait(q, self._deps(reads, writes))
        ins = self.nc.gpsimd.indirect_dma_start(
            out=out, out_offset=None, in_=in_,
            in_offset=bass.IndirectOffsetOnAxis(ap=idx_ap, axis=0))
        chan.count += 1
        ins.then_inc(chan.sem, 16)
        ev = (chan.key, 16 * chan.count)
        self._commit(ev, reads, writes)
        self.n_ins += 1
        return ev

    def dma(self, q, chan, out, in_, reads=(), writes=()):
        self._wait(q, self._deps(reads, writes))
        ins = self.eng[q].dma_start(out=out, in_=in_)
        chan.count += 1
        ins.then_inc(chan.sem, 16)
        ev = (chan.key, 16 * chan.count)
        self._commit(ev, reads, writes)
        self.n_ins += 1
        return ev

    def barrier_events(self):
        evs = [(e, self.cnt[e]) for e in ENGS if self.cnt[e]]
        evs += [(c.key, 16 * c.count) for c in self.chans if c.count and not c.key.startswith("dma_w")]
        return evs

    def finish(self, e="sp"):
        self._wait(e, self.barrier_events())


class SB:
    def __init__(self, big_ap, base, limit):
        self.big = big_ap
        self.base = base
        self.off = base
        self.limit = limit

    def reset(self):
        self.off = self.base

    def alloc(self, shape, dtype, align=64):
        esz = 4 if dtype in (F32, I32) else 2
        n = int(np.prod(shape[1:]))
        nbytes = esz * n
        self.off = (self.off + align - 1) // align * align
        a = self.big[:, self.off // 2:(self.off + nbytes) // 2]
        if dtype != BF16:
            a = a.bitcast(dtype)
        if shape[0] != 128:
            a = a[0:shape[0]]
        if len(shape) == 3:
            a = a.rearrange("p (a b) -> p a b", a=shape[1])
        elif len(shape) == 4:
            a = a.rearrange("p (a b c) -> p a b c", a=shape[1], b=shape[2])
        self.off += nbytes
        assert self.off <= self.limit, ("SBUF region overflow", self.off, self.limit)
        return a


class Builder:
    def __init__(self, nc, n_layers=DEPTH, do_attn=True, do_sample_attn=True):
        self.nc = nc
        self.n_layers = n_layers
        self.do_attn = do_attn
        self.do_sample_attn = do_sample_attn
        self.cx = Ctx(nc)
        self.tiles = MAIN_TILES
        self.npool = NPOOL if do_sample_attn else 1
        self.declare_dram()
        self.alloc_onchip()

    def declare_dram(self):
        nc = self.nc

        def inp(name, shape, dt=F32):
            return nc.dram_tensor(name, list(shape), dt, kind="ExternalInput").ap()

        def outp(name, shape, dt=F32):
            return nc.dram_tensor(name, list(shape), dt, kind="ExternalOutput").ap()

        def scr(name, shape, dt=BF16):
            return nc.dram_tensor(name, list(shape), dt, kind="Internal").ap()

        self.xin = inp("xin", [NT, D])
        self.pin = inp("pin", [DEPTH, NT, PLE])
        self.xprev = inp("xprev", [NPT, D])
        self.pprev = inp("pprev", [N_A, NPT, PLE])
        self.flag = inp("flag", [1, 1])
        self.ptab = inp("ptab", [1, NSQ * NPG], I32)
        self.cache_k = inp("cache_k", [self.npool * 128, D])
        self.cache_v = inp("cache_v", [self.npool * 128, D])
        self.cache_lf = inp("cache_lf", [self.npool * 128, NH])
        self.norms = inp("norms", [17, D])
        self.w_ffn = {}
        for which in (1, 2):
            for nm in ("w1", "w3", "w2"):
                shp = [DEPTH, D, DFF] if nm != "w2" else [DEPTH, DFF, D]
                self.w_ffn[(which, nm)] = inp(f"ffn{which}_{nm}", shp)
        self.ple_w_gate = inp("ple_w_gate", [DEPTH, D, D])
        self.ple_w_proj = inp("ple_w_proj", [DEPTH, PLE, D])
        self.gmlp_w_in = inp("gmlp_w_in", [N_A, D, 2 * D])
        self.gmlp_v_norm = inp("gmlp_v_norm", [N_A, D])
        self.gmlp_w_s = inp("gmlp_w_s", [N_A, 8, 128, 128])
        self.gmlp_b_s = inp("gmlp_b_s", [N_A, 8, 128])
        self.gmlp_w_out = inp("gmlp_w_out", [N_A, D, D])
        self.w_kvf = inp("w_kvf", [D, 2 * D + NH])
        self.b_f = inp("b_f", [1, NH])
        self.k_norm = inp("k_norm", [1, HD])
        self.att_w_q = inp("att_w_q", [2, D, D])
        self.q_norm = inp("q_norm", [2, HD])
        self.att_w_o = inp("att_w_o", [2, D, D])
        self.c_ident = inp("c_ident", [128, 128])
        self.c_tri = inp("c_tri", [128, 128])
        self.c_bdm = inp("c_bdm", [128, 128])
        self.c_rep = inp("c_rep", [128, 128])
        self.c_su = inp("c_su", [128, 128])
        self.c_sel = inp("c_sel", [128, 128])
        self.c_iota = inp("c_iota", [128, 1], I32)
        self.y_out = outp("y_out", [NT, D])
        self.k_out = outp("k_out", [NT, D])
        self.v_out = outp("v_out", [NT, D])
        self.lf_out = outp("lf_out", [NT, NH])
        self.gv_out = outp("gv_out", [N_A, NST, D])
        self.KTprev = scr("KTprev", [NH, HD, NPT])
        self.Vprev = scr("Vprev", [NPT, D])
        self.KTown = scr("KTown", [NH, HD, NT])
        self.Vown = scr("Vown", [NT, D])
        self.KTpast = scr("KTpast", [NSQ * NPG, HD, D])
        self.Vpast = scr("Vpast", [NSQ * NPG, 128, D])

    def alloc_onchip(self):
        nc, cx = self.nc, self.cx
        total = nc.sbuf_bytes_remaining
        total = (total - 256) // 64 * 64
        big = nc.alloc_sbuf_tensor("big", [128, total // 2], BF16).ap()
        self.big = big
        P = SB(big, 0, total)
        self.h = P.alloc([128, NCH, NT], F32)
        self.h_trk = [[cx.trk(f"h{t}_{c}") for c in range(NCH)] for t in range(len(MAIN_TILES))]
        self.wslot = [P.alloc([128, SLOT_EL], BF16) for _ in range(2)]
        self.wslot_trk = [cx.trk("wslot0"), cx.trk("wslot1")]
        self.ident = P.alloc([128, 128], F32)
        self.identb = P.alloc([128, 128], BF16)
        self.ones_mean = P.alloc([128, 128], BF16)
        self.ones_hd = P.alloc([128, 128], BF16)
        self.ones_bf = P.alloc([128, 128], BF16)
        self.ones_f = P.alloc([128, 128], F32)
        self.tri = P.alloc([128, 128], F32)
        self.bdm = P.alloc([128, 128], F32)
        self.rep = P.alloc([128, 128], F32)
        self.su = P.alloc([128, 128], F32)
        self.sel = P.alloc([128, 128], F32)
        self.gcol = P.alloc([128, 17 * NCH], F32)
        self.gqcol = P.alloc([128, 2], F32)
        self.epsb = P.alloc([128, 1], F32)
        self.oneb = P.alloc([128, 1], F32)
        self.flagB = P.alloc([128, 1], F32)
        self.iota = P.alloc([128, 1], I32)
        self.lf_all = P.alloc([128, NBLK, NH], F32)
        self.lf_prev = P.alloc([128, 16, NH], F32)
        self.csum_own = P.alloc([128, 16, NH], F32)
        self.tail_prev = P.alloc([128, 16, NH], F32)
        self.Rt = P.alloc([128, 4, NH], F32)
        self.ncn = P.alloc([128, NH], F32)
        self.idx_all = P.alloc([128, NSQ * NPG], I32)
        self.tail_all = P.alloc([128, NSQ, NPG, NH], F32)
        self.bfB = P.alloc([128, NH], F32)
        self.gkB = P.alloc([128, HD], F32)
        self.const_trk = cx.trk("const")
        self.lf_trk = [cx.trk("lf") for b in range(NBLK)]
        self.lfp_trk = [cx.trk("lfp") for b in range(16)]
        self.cs_trk = cx.trk("csum")
        self.idx_trk = cx.trk("idx")
        self.tail_trk = cx.trk("tailall")
        self.RT = SB(big, P.off, total)
        self.rt_bytes = total - P.off
        print(f"[kernel] SBUF persistent={P.off} phase_region={self.rt_bytes}")
        self.pp = [nc.alloc_psum_tensor(f"pp{i}", [128, 1024], F32).ap() for i in range(4)]
        self.bank = []
        for i in range(4):
            self.bank += [self.pp[i][:, 0:512], self.pp[i][:, 512:1024]]
        self.bank_trk = [cx.trk(f"bank{i}") for i in range(8)]
        self.ch_w = [cx.chan("w0"), cx.chan("w1")]
        self.ch_const = cx.chan("const")
        self.ch_by_name = {}

    def phase(self):
        self.RT.reset()
        self._phase_barrier = self.cx.barrier_events()

    def ptrk(self, name=""):
        t = self.cx.trk(name)
        t.reads = list(self._phase_barrier)
        return t

    def palloc(self, shape, dtype, name=""):
        return self.RT.alloc(shape, dtype), self.ptrk(name)

    def plan_weights(self):
        pieces = []

        def chunked(ap2d):
            return ap2d.rearrange("(c p) f -> p c f", p=128)

        def ffn_pieces(l, which, tag):
            for (f0, gc) in FFN_GROUPS:
                w1 = self.w_ffn[(which, "w1")][l]
                w3 = self.w_ffn[(which, "w3")][l]
                w2 = self.w_ffn[(which, "w2")][l]
                n1 = NCH * gc * 128
                pieces.append((f"{tag}ffn{which}_{l}_{f0}", [
                    ((0, n1, [NCH, gc * 128]), chunked(w1[:, f0 * 128:(f0 + gc) * 128])),
                    ((4096, n1, [NCH, gc * 128]), chunked(w3[:, f0 * 128:(f0 + gc) * 128])),
                    ((8192, gc * D, [gc, D]), chunked(w2[f0 * 128:(f0 + gc) * 128, :])),
                ]))

        def gmlp_pieces(l, tag):
            wi = self.gmlp_w_in[l]
            pieces.append((f"{tag}gmlpA_{l}", [
                ((0, 8192, [NCH, D]), chunked(wi[:, 0:D])),
                ((8192, 4096, [4, D]), chunked(wi[0:512, D:2 * D])),
            ]))
            pieces.append((f"{tag}gmlpB_{l}", [
                ((0, 4096, [4, D]), chunked(wi[512:1024, D:2 * D])),
                ((4096, 8192, [NCH, D]), chunked(self.gmlp_w_out[l])),
            ]))

        def ple_piece(l, tag):
            pieces.append((f"{tag}ple_{l}", [
                ((0, 8192, [NCH, D]), chunked(self.ple_w_gate[l])),
                ((8192, 2048, [2, D]), chunked(self.ple_w_proj[l])),
            ]))

        def kv_pieces(tag):
            pieces.append((f"{tag}kvA", [((0, 8192, [NCH, D]), chunked(self.w_kvf[:, 0:D]))]))
            pieces.append((f"{tag}kvB", [((0, NCH * (D + NH), [NCH, D + NH]),
                                          chunked(self.w_kvf[:, D:2 * D + NH]))]))

        if self.do_attn and self.n_layers > N_A:
            for l in range(N_A):
                ffn_pieces(l, 1, "p")
                gmlp_pieces(l, "p")
                ffn_pieces(l, 2, "p")
                ple_piece(l, "p")
            kv_pieces("p")
        for l in range(self.n_layers):
            ffn_pieces(l, 1, "")
            if l < N_A:
                gmlp_pieces(l, "")
            elif self.do_attn:
                j = l - N_A
                pieces.append((f"attA_{l}", [((0, 8192, [NCH, D]), chunked(self.att_w_q[j]))]))
                pieces.append((f"attB_{l}", [((0, 8192, [NCH, D]), chunked(self.att_w_o[j]))]))
            ffn_pieces(l, 2, "")
            ple_piece(l, "")
            if l == N_A - 1:
                kv_pieces("")
        self.pieces = pieces
        self.w_cur = 0
        self.w_issued = 0

    def w_view(self, slot, off, n, shp):
        return self.wslot[slot][:, off:off + n].rearrange("p (a b) -> p a b", a=shp[0])

    def w_issue(self):
        i = self.w_issued
        if i >= len(self.pieces):
            return
        slot = i % 2
        name, loads = self.pieces[i]
        for (off, n, shp), src in loads:
            self.cx.dma("pool", self.ch_w[slot], self.w_view(slot, off, n, shp), src,
                        writes=[self.wslot_trk[slot]])
        self.w_issued += 1

    def w_acquire(self, name):
        i = self.w_cur
        assert self.pieces[i][0] == name, (self.pieces[i][0], name)
        assert self.w_issued > i
        self.w_cur += 1
        return i % 2

    def w_release(self):
        self.w_issue()

    def tsl(self, t):
        off, w = self.tiles[t]
        return slice(off, off + w)

    def blocks(self, t):
        off, w = self.tiles[t]
        return [(off + 128 * j, min(128, w - 128 * j)) for j in range((w + 127) // 128)]

    def chan_for(self, trk, kind):
        key = kind + "_" + trk.name
        if key not in self.ch_by_name:
            self.ch_by_name[key] = self.cx.chan(key)
        return self.ch_by_name[key]

    def store(self, out_ap, in_ap, reads, writes=()):
        return self.cx.dma("sp", self.chan_for(reads[0], "st"), out_ap, in_ap, reads=reads, writes=writes)

    def load(self, out_ap, in_ap, writes, reads=(), q="sp"):
        return self.cx.dma(q, self.chan_for(writes[0], "ld"), out_ap, in_ap, reads=reads, writes=writes)

    def setup_consts(self):
        nc, cx = self.nc, self.cx
        ct = self.const_trk
        for dst, src in ((self.ident, self.c_ident), (self.tri, self.c_tri), (self.bdm, self.c_bdm),
                         (self.rep, self.c_rep), (self.su, self.c_su), (self.sel, self.c_sel),
                         (self.iota, self.c_iota)):
            cx.dma("sp", self.ch_const, dst, src, writes=[ct])
        cx.dma("sp", self.ch_const, self.bfB, self.b_f[0].partition_broadcast(128), writes=[ct])
        cx.dma("sp", self.ch_const, self.gkB, self.k_norm[0].partition_broadcast(128), writes=[ct])
        cx.dma("sp", self.ch_const, self.flagB, self.flag[0].partition_broadcast(128), writes=[ct])
        with nc.allow_non_contiguous_dma(reason="256-element gain transpose"):
            cx.dma("sp", self.ch_const, self.gqcol, self.q_norm.rearrange("j d -> d j"), writes=[ct])
        cx.op("dve", lambda: nc.vector.memset(self.ones_mean, 1.0 / D), writes=[ct])
        cx.op("dve", lambda: nc.vector.memset(self.ones_hd, 1.0 / HD), writes=[ct])
        cx.op("dve", lambda: nc.vector.memset(self.ones_bf, 1.0), writes=[ct])
        cx.op("dve", lambda: nc.vector.memset(self.ones_f, 1.0), writes=[ct])
        cx.op("dve", lambda: nc.vector.memset(self.epsb, EPS), writes=[ct])
        cx.op("dve", lambda: nc.vector.memset(self.oneb, 1.0), writes=[ct])
        cx.op("dve", lambda: nc.vector.tensor_copy(out=self.identb, in_=self.ident), reads=[ct], writes=[ct])
        self.phase()
        ptab_b, ptab_trk = self.palloc([128, NSQ * NPG], I32, "ptab_b")
        self.load(ptab_b, self.ptab[0].partition_broadcast(128), writes=[ptab_trk])
        cx.op("dve", lambda: nc.vector.tensor_scalar(out=self.idx_all, in0=ptab_b, scalar1=128, scalar2=self.iota[:, 0:1],
                                                     op0=ALU.mult, op1=ALU.add),
              reads=[ptab_trk, ct], writes=[self.idx_trk])
        rows, rt = self.palloc([17, D], F32, "normrows")
        self.load(rows, self.norms, writes=[rt])
        bt = self.bank_trk[0]
        for c in range(NCH):
            cx.op("pe", lambda: nc.tensor.transpose(self.bank[0][:, c * 17:(c + 1) * 17],
                                                    rows[:, c * 128:(c + 1) * 128], self.ident[0:17, 0:17]),
                  reads=[rt, ct], writes=[bt])
        cx.op("act", lambda: nc.scalar.activation(
            out=self.gcol.rearrange("p (n c) -> p c n", c=NCH),
            in_=self.bank[0][:, 0:NCH * 17].rearrange("p (c n) -> p c n", n=17), func=AF.Copy),
            reads=[bt], writes=[ct])

    def load_x(self, src):
        nc, cx = self.nc, self.cx
        self.phase()
        xr = [self.palloc([128, D], F32, f"xrow{i}") for i in range(2)]
        b = 0
        for t in range(len(self.tiles)):
            for (tok0, bp) in self.blocks(t):
                row, rtrk = xr[b % 2]
                self.load(row[0:bp], src[tok0:tok0 + bp, :], writes=[rtrk])
                for j in range(2):
                    bi = (2 * b + j) % 4
                    for cc in range(4):
                        c = 4 * j + cc
                        cx.op("pe", lambda: nc.tensor.transpose(self.bank[bi][:, cc * 128:cc * 128 + bp],
                                                                row[0:bp, c * 128:(c + 1) * 128],
                                                                self.ident[0:bp, 0:bp]),
                              reads=[rtrk, self.const_trk], writes=[self.bank_trk[bi]])
                    dst = self.h[:, 4 * j:4 * j + 4, tok0:tok0 + bp]
                    srcp = self.bank[bi].rearrange("p (a b) -> p a b", a=4)[:, :, 0:bp]
                    if j == 0:
                        fn = lambda: nc.scalar.activation(out=dst, in_=srcp, func=AF.Copy)
                    else:
                        fn = lambda: nc.vector.tensor_copy(out=dst, in_=srcp)
                    cx.op("act" if j == 0 else "dve", fn, reads=[self.bank_trk[bi]],
                          writes=self.h_trk[t][4 * j:4 * j + 4])
                b += 1

    def store_y(self, dst):
        nc, cx = self.nc, self.cx
        self.phase()
        yr = [self.palloc([128, D], F32, f"yrow{i}") for i in range(2)]
        b = 0
        for t in range(len(self.tiles)):
            for (tok0, bp) in self.blocks(t):
                row, rtrk = yr[b % 2]
                for j in range(2):
                    bi = (2 * b + j) % 4
                    for cc in range(4):
                        c = 4 * j + cc
                        cx.op("pe", lambda: nc.tensor.transpose(self.bank[bi][0:bp, cc * 128:(cc + 1) * 128],
                                                                self.h[:, c, tok0:tok0 + bp], self.ident),
                              reads=[self.h_trk[t][c], self.const_trk], writes=[self.bank_trk[bi]])
                    d_ = row[0:bp, 512 * j:512 * (j + 1)]
                    s_ = self.bank[bi][0:bp, :]
                    if j == 0:
                        fn = lambda: nc.scalar.activation(out=d_, in_=s_, func=AF.Copy)
                    else:
                        fn = lambda: nc.vector.tensor_copy(out=d_, in_=s_)
                    cx.op("act" if j == 0 else "dve", fn, reads=[self.bank_trk[bi]], writes=[rtrk])
                self.store(dst[tok0:tok0 + bp, :], row[0:bp], reads=[rtrk])
                b += 1

    def norm_tile(self, t, gi, xn_dst, xn_trks, tmp):
        nc, cx = self.nc, self.cx
        sq, sq_trk, rs, rs_trk = tmp
        st_bank, st_trk = self.bank[6], self.bank_trk[6]
        ts = self.tsl(t)
        W = self.tiles[t][1]
        for hf in range(2):
            cx.op("act", lambda: nc.scalar.activation(out=sq[hf][:, :, 0:W], in_=self.h[:, 4 * hf:4 * hf + 4, ts],
                                                      func=AF.Square),
                  reads=self.h_trk[t][4 * hf:4 * hf + 4], writes=[sq_trk[hf]])
        cx.mm_group(st_bank[:, 0:W], [(self.ones_mean, sq[c // 4][:, c % 4, 0:W]) for c in range(NCH)],
                    reads=[self.const_trk, sq_trk[0], sq_trk[1]], writes=[st_trk])
        cx.op("act", lambda: nc.scalar.activation(out=rs[:, 0:W], in_=st_bank[:, 0:W], func=AF.Ln,
                                                  bias=self.epsb, scale=1.0),
              reads=[st_trk, self.const_trk], writes=[rs_trk])
        cx.op("act", lambda: nc.scalar.activation(out=rs[:, 0:W], in_=rs[:, 0:W], func=AF.Exp, scale=-0.5),
              reads=[rs_trk], writes=[rs_trk])
        for c in range(NCH):
            cx.op("dve", lambda: nc.vector.scalar_tensor_tensor(
                out=xn_dst[:, c, 0:W], in0=self.h[:, c, ts], scalar=self.gcol[:, gi * NCH + c:gi * NCH + c + 1],
                in1=rs[:, 0:W], op0=ALU.mult, op1=ALU.mult),
                reads=[self.h_trk[t][c], self.const_trk, rs_trk], writes=[xn_trks[c]])

    def norm_tmp(self):
        sq, sq_trk = [], []
        for i in range(2):
            a, tk = self.palloc([128, 4, TILE], BF16, f"sq{i}")
            sq.append(a)
            sq_trk.append(tk)
        rs, rs_trk = self.palloc([128, TILE], F32, "rstd")
        return (sq, sq_trk, rs, rs_trk)

    def bg_alloc(self):
        self.bg_bufs = None
        if self.bg_nC >= self.bg_total:
            return
        kbf = [self.palloc([128, D], BF16, f"bgk{i}") for i in range(BGD)]
        vbf = [self.palloc([128, D], BF16, f"bgv{i}") for i in range(BGD)]
        ktp = [self.palloc([128, D], BF16, f"bgt{i}") for i in range(2)]
        self.bg_bufs = (kbf, vbf, ktp)

    def bg_A(self, p):
        cx = self.cx
        kbf, vbf, ktp = self.bg_bufs
        i, pg = divmod(p, NPG)
        kb, kb_trk = kbf[p % BGD]
        vb, vb_trk = vbf[p % BGD]
        ix = self.idx_all[:, p:p + 1]
        cx.idma(self.chan_for(kb_trk, "ld"), kb, self.cache_k, ix, reads=[self.idx_trk], writes=[kb_trk])
        cx.idma(self.chan_for(vb_trk, "ld"), vb, self.cache_v, ix, reads=[self.idx_trk], writes=[vb_trk])
        cx.idma(self.chan_for(self.tail_trk, "ld"), self.tail_all[:, i, pg, :], self.cache_lf, ix,
                reads=[self.idx_trk], writes=[self.tail_trk])

    def bg_C(self, p):
        nc, cx = self.nc, self.cx
        kbf, vbf, ktp = self.bg_bufs
        kb, kb_trk = kbf[p % BGD]
        vb, vb_trk = vbf[p % BGD]
        kp, kp_trk = ktp[p % 2]
        ptb = self.bank[7].bitcast(BF16)
        ptb_trk = self.bank_trk[7]
        for hh in range(NH):
            cx.op("pe", lambda: nc.tensor.transpose(ptb[:, hh * 128:(hh + 1) * 128], kb[:, hh * 128:(hh + 1) * 128],
                                                    self.identb),
                  reads=[kb_trk, self.const_trk], writes=[ptb_trk])
        if p % 2 == 0:
            cx.op("act", lambda: nc.scalar.activation(out=kp, in_=ptb, func=AF.Copy), reads=[ptb_trk], writes=[kp_trk])
        else:
            cx.op("dve", lambda: nc.vector.tensor_copy(out=kp, in_=ptb), reads=[ptb_trk], writes=[kp_trk])
        self.store(self.KTpast[p], kp, reads=[kp_trk])
        self.store(self.Vpast[p], vb, reads=[vb_trk])

    def bg_hook(self):
        if not self.bg_bufs:
            return
        if self.bg_nA - self.bg_nC >= BGD - 1 or (self.bg_nA >= self.bg_total and self.bg_nC < self.bg_nA):
            self.bg_C(self.bg_nC)
            self.bg_nC += 1
        if self.bg_nA < self.bg_total and self.bg_nA - self.bg_nC < BGD:
            self.bg_A(self.bg_nA)
            self.bg_nA += 1

    def bg_flush(self):
        if not self.bg_bufs:
            return
        while self.bg_nC < self.bg_nA:
            self.bg_C(self.bg_nC)
            self.bg_nC += 1
        self.bg_bufs = None

    def ffn(self, l, which, tag=""):
        nc, cx = self.nc, self.cx
        self.phase()
        ntl = len(self.tiles)
        ntok = self.tiles[-1][0] + self.tiles[-1][1]
        gi = 4 * l + (0 if which == 1 else 2)
        xn, _ = self.palloc([128, NCH, ntok], BF16, "xn_all")
        xn_trk = [[self.ptrk(f"xn{t}_{c}") for c in range(NCH)] for t in range(ntl)]
        tmp = self.norm_tmp()
        hmid = [self.palloc([128, 4, TILE], BF16, f"hmid{i}") for i in range(2)]
        sa = [self.palloc([128, TILE], F32, f"sa{i}") for i in range(2)]
        self.bg_alloc()
        self.norm_tile(0, gi, xn[:, :, self.tsl(0)], xn_trk[0], tmp)
        PA = [(self.bank[0], self.bank_trk[0]), (self.bank[1], self.bank_trk[1])]
        PB = [(self.bank[2], self.bank_trk[2]), (self.bank[3], self.bank_trk[3])]
        PY = [(self.bank[4], self.bank_trk[4]), (self.bank[5], self.bank_trk[5])]
        steps = [(g, t) for g in range(len(FFN_GROUPS)) for t in range(ntl)]
        state = {"k": 0, "j": 0, "slot": {}}

        def emit_ab(i):
            g, t = steps[i]
            f0, gc = FFN_GROUPS[g]
            if t == 0:
                state["slot"][g] = self.w_acquire(f"{tag}ffn{which}_{l}_{f0}")
            slot = state["slot"][g]
            strk = self.wslot_trk[slot]
            w1g = self.w_view(slot, 0, NCH * gc * 128, [NCH, gc * 128])
            w3g = self.w_view(slot, 4096, NCH * gc * 128, [NCH, gc * 128])
            hm, hm_trk = hmid[i % 2]
            ts = self.tsl(t)
            W = self.tiles[t][1]
            for fc in range(gc):
                k = state["k"]
                state["k"] += 1
                (a_ps, a_trk), (b_ps, b_trk) = PA[k % 2], PB[k % 2]
                s_sb, s_trk = sa[k % 2]
                fs = slice(fc * 128, (fc + 1) * 128)
                cx.mm_group(a_ps[:, 0:W], [(w1g[:, c, fs], xn[:, c, ts]) for c in range(NCH)],
                            reads=[strk] + xn_trk[t], writes=[a_trk])
                cx.mm_group(b_ps[:, 0:W], [(w3g[:, c, fs], xn[:, c, ts]) for c in range(NCH)],
                            reads=[strk] + xn_trk[t], writes=[b_trk])
                cx.op("act", lambda: nc.scalar.activation(out=s_sb[:, 0:W], in_=a_ps[:, 0:W], func=AF.Silu),
                      reads=[a_trk], writes=[s_trk])
                cx.op("dve", lambda: nc.vector.tensor_tensor(out=hm[:, fc, 0:W], in0=s_sb[:, 0:W], in1=b_ps[:, 0:W],
                                                             op=ALU.mult),
                      reads=[s_trk, b_trk], writes=[hm_trk])

        def emit_y(i):
            g, t = steps[i]
            f0, gc = FFN_GROUPS[g]
            slot = state["slot"][g]
            strk = self.wslot_trk[slot]
            w2g = self.w_view(slot, 8192, gc * D, [gc, D])
            hm, hm_trk = hmid[i % 2]
            ts = self.tsl(t)
            W = self.tiles[t][1]
            for c in range(NCH):
                j = state["j"]
                state["j"] += 1
                y_ps, y_trk = PY[j % 2]
                cx.mm_group(y_ps[:, 0:W], [(w2g[:, fc, c * 128:(c + 1) * 128], hm[:, fc, 0:W]) for fc in range(gc)],
                            reads=[strk, hm_trk], writes=[y_trk])
                cx.op("dve", lambda: nc.vector.scalar_tensor_tensor(
                    out=self.h[:, c, ts], in0=y_ps[:, 0:W], scalar=0.5, in1=self.h[:, c, ts],
                    op0=ALU.mult, op1=ALU.add),
                    reads=[y_trk, self.h_trk[t][c]], writes=[self.h_trk[t][c]])
            if t == ntl - 1:
                self.w_release()

        emit_ab(0)
        for i in range(len(steps)):
            if i + 1 < len(steps):
                g1, t1 = steps[i + 1]
                if g1 == 0:
                    self.norm_tile(t1, gi, xn[:, :, self.tsl(t1)], xn_trk[t1], tmp)
                emit_ab(i + 1)
            self.bg_hook()
            emit_y(i)
            self.bg_hook()
        self.bg_flush()

    def gmlp(self, l, tag="", emit_gv=True):
        nc, cx = self.nc, self.cx
        self.phase()
        gi = 4 * l + 1
        ct = self.const_trk
        xt, _ = self.palloc([128, NCH, TILE], BF16, "xn_t0")
        xtr = [self.ptrk() for c in range(NCH)]
        uT, uT_trk = self.palloc([128, NCH, TILE], BF16, "uT")
        vn, _ = self.palloc([128, 4, D], BF16, "vn")
        vn_trk = [self.ptrk() for _ in range(4)]
        us, us_trk = self.palloc([128, NCH, TILE], BF16, "us")
        tmp = self.norm_tmp()
        mixp, mixp_trk = self.palloc([128, 8, 128], BF16, "mixp")
        mixs, mixs_trk = self.palloc([128, 8, 128], BF16, "mixs")
        bsp, bsp_trk = self.palloc([128, 8, 128], F32, "bsp")
        bss, bss_trk = self.palloc([128, 8, 128], F32, "bss")
        gvB, gvB_trk = self.palloc([128, D], F32, "gvB")
        wsr, wsr_trk = self.palloc([128, 8, 128], F32, "wsr")
        x44, x44_trk = self.palloc([128, 8, 4], F32, "x44")
        t1s, t1s_trk = self.palloc([128, 8, 4], F32, "t1s")
        vf, vf_trk = self.palloc([128, D], F32, "vnf0")
        junk, junk_trk = self.palloc([128, D], BF16, "junk")
        ssv, ssv_trk = self.palloc([128, 2], F32, "ssv")
        t1 = [self.palloc([128, TILE], F32, f"t1_{i}") for i in range(2)]

        self.load(wsr, self.gmlp_w_s[l].rearrange("g t s -> t g s"), writes=[wsr_trk])
        self.load(bsp, self.gmlp_b_s[l].partition_broadcast(128), writes=[bsp_trk])
        self.load(gvB, self.gmlp_v_norm[l].partition_broadcast(128), writes=[gvB_trk])
        for g in range(8):
            bi = g % 2
            cx.op("pe", lambda: nc.tensor.transpose(self.bank[bi][:, 0:128], wsr[:, g, :], self.ident),
                  reads=[wsr_trk, ct], writes=[self.bank_trk[bi]])
            cx.op("dve", lambda: nc.vector.tensor_tensor(out=mixp[:, g, :], in0=self.bank[bi][:, 0:128],
                                                         in1=self.tri, op=ALU.mult),
                  reads=[self.bank_trk[bi], ct], writes=[mixp_trk])
            cx.op("dve", lambda: nc.vector.tensor_copy(out=x44[:, g, :], in_=self.bank[bi][:, 0:4]),
                  reads=[self.bank_trk[bi]], writes=[x44_trk])
        cx.mm_group(self.bank[2][:, 0:32], [(self.rep, x44.rearrange("a g t -> a (g t)"))],
                    reads=[ct, x44_trk], writes=[self.bank_trk[2]])
        cx.op("dve", lambda: nc.vector.tensor_copy(out=t1s.rearrange("p g t -> p (g t)"), in_=self.bank[2][:, 0:32]),
              reads=[self.bank_trk[2]], writes=[t1s_trk])
        for g in range(8):
            cx.op("dve", lambda: nc.vector.tensor_tensor(
                out=mixs[:, g, :].rearrange("p (a b) -> p a b", b=4),
                in0=t1s[:, g:g + 1, :].broadcast_to([128, 32, 4]),
                in1=self.bdm.rearrange("p (a b) -> p a b", b=4), op=ALU.mult),
                reads=[t1s_trk, ct], writes=[mixs_trk])
        cx.op("dve", lambda: nc.vector.tensor_copy(
            out=bss.rearrange("p g (a b) -> p g a b", b=4),
            in_=bsp[:, :, 0:4].unsqueeze(2).broadcast_to([128, 8, 32, 4])),
            reads=[bsp_trk], writes=[bss_trk])

        slotA = self.w_acquire(f"{tag}gmlpA_{l}")
        slotB = self.w_acquire(f"{tag}gmlpB_{l}")
        sA, sB = self.wslot_trk[slotA], self.wslot_trk[slotB]
        Wu = self.w_view(slotA, 0, 8192, [NCH, D])
        WvA = self.w_view(slotA, 8192, 4096, [4, D])
        WvB = self.w_view(slotB, 0, 4096, [4, D])
        Wo = self.w_view(slotB, 4096, 8192, [NCH, D])
        kk = 0
        for t in range(len(self.tiles)):
            off, W = self.tiles[t]
            sample = (off >= NPT)
            ts = self.tsl(t)
            self.norm_tile(t, gi, xt, xtr, tmp)
            for fc in range(NCH):
                bi = fc % 2
                cx.mm_group(self.bank[bi][:, 0:W], [(Wu[:, c, fc * 128:(fc + 1) * 128], xt[:, c, 0:W]) for c in range(NCH)],
                            reads=[sA] + xtr, writes=[self.bank_trk[bi]])
                cx.op("act", lambda: nc.scalar.activation(out=uT[:, fc, 0:W], in_=self.bank[bi][:, 0:W], func=AF.Copy),
                      reads=[self.bank_trk[bi]], writes=[uT_trk])
            blks = self.blocks(t)
            for blk, (tok0, bp) in enumerate(blks):
                pv = self.pp[1 + (blk % 2)]
                pv_trk = [self.bank_trk[2 + 2 * (blk % 2)], self.bank_trk[3 + 2 * (blk % 2)]]
                bs = slice(blk * 128, blk * 128 + bp)
                for hf in range(2):
                    fs = slice(hf * 512, (hf + 1) * 512)
                    cx.mm_group(pv[0:bp, fs], [(xt[:, c, bs], (WvA if c < 4 else WvB)[:, c % 4, fs]) for c in range(NCH)],
                                reads=[sA, sB] + xtr, writes=[pv_trk[hf]])
                cx.op("act", lambda: nc.scalar.activation(out=junk[0:bp], in_=pv[0:bp], func=AF.Square,
                                                          accum_out=ssv[0:bp, 0:1]),
                      reads=pv_trk, writes=[junk_trk, ssv_trk])
                cx.op("act", lambda: nc.scalar.activation(out=ssv[0:bp, 1:2], in_=ssv[0:bp, 0:1], func=AF.Ln,
                                                          bias=self.epsb[0:bp], scale=1.0 / D),
                      reads=[ssv_trk, ct], writes=[ssv_trk])
                cx.op("act", lambda: nc.scalar.activation(out=ssv[0:bp, 1:2], in_=ssv[0:bp, 1:2], func=AF.Exp, scale=-0.5),
                      reads=[ssv_trk], writes=[ssv_trk])
                if sample:
                    cx.op("dve", lambda: nc.vector.scalar_tensor_tensor(
                        out=vf[0:bp], in0=pv[0:bp], scalar=ssv[0:bp, 1:2], in1=gvB[0:bp], op0=ALU.mult, op1=ALU.mult),
                        reads=pv_trk + [ssv_trk, gvB_trk], writes=[vf_trk])
                    if emit_gv:
                        self.store(self.gv_out[l, tok0 - NPT:tok0 - NPT + bp, :], vf[0:bp], reads=[vf_trk])
                    cx.op("dve", lambda: nc.vector.tensor_copy(out=vn[0:bp, blk, :], in_=vf[0:bp]),
                          reads=[vf_trk], writes=[vn_trk[blk]])
                else:
                    cx.op("dve", lambda: nc.vector.scalar_tensor_tensor(
                        out=vn[0:bp, blk, :], in0=pv[0:bp], scalar=ssv[0:bp, 1:2], in1=gvB[0:bp], op0=ALU.mult, op1=ALU.mult),
                        reads=pv_trk + [ssv_trk, gvB_trk], writes=[vn_trk[blk]])
            mix, mix_trk = (mixs, mixs_trk) if sample else (mixp, mixp_trk)
            bsB, bsB_trk = (bss, bss_trk) if sample else (bsp, bsp_trk)
            nb = len(blks)
            for g in range(8):
                bi = g % 2
                for blk, (tok0, bp) in enumerate(blks):
                    cx.mm_group(self.bank[bi][:, blk * 128:blk * 128 + bp],
                                [(vn[0:bp, blk, g * 128:(g + 1) * 128], mix[0:bp, g, 0:bp])],
                                reads=[vn_trk[blk], mix_trk], writes=[self.bank_trk[bi]])
                tt, tt_trk = t1[kk % 2]
                kk += 1
                if nb == 4:
                    cx.op("dve", lambda: nc.vector.tensor_tensor(
                        out=tt.rearrange("p (a b) -> p a b", a=4),
                        in0=self.bank[bi].rearrange("p (a b) -> p a b", a=4),
                        in1=bsB[:, g:g + 1, :].broadcast_to([128, 4, 128]), op=ALU.add),
                        reads=[self.bank_trk[bi], bsB_trk], writes=[tt_trk])
                else:
                    cx.op("dve", lambda: nc.vector.tensor_tensor(
                        out=tt[:, 0:W], in0=self.bank[bi][:, 0:W], in1=bsB[:, g, 0:W], op=ALU.add),
                        reads=[self.bank_trk[bi], bsB_trk], writes=[tt_trk])
                cx.op("dve", lambda: nc.vector.tensor_tensor(out=us[:, g, 0:W], in0=tt[:, 0:W], in1=uT[:, g, 0:W],
                                                             op=ALU.mult),
                      reads=[tt_trk, uT_trk], writes=[us_trk])
            for fc in range(NCH):
                bi = 6 + fc % 2
                cx.mm_group(self.bank[bi][:, 0:W], [(Wo[:, g, fc * 128:(fc + 1) * 128], us[:, g, 0:W]) for g in range(NCH)],
                            reads=[sB, us_trk], writes=[self.bank_trk[bi]])
                cx.op("dve", lambda: nc.vector.tensor_tensor(out=self.h[:, fc, ts], in0=self.bank[bi][:, 0:W],
                                                             in1=self.h[:, fc, ts], op=ALU.add),
                      reads=[self.bank_trk[bi], self.h_trk[t][fc]], writes=[self.h_trk[t][fc]])
        self.w_release()
        self.w_release()

    def ple(self, l, pe_src, tag=""):
        nc, cx = self.nc, self.cx
        self.phase()
        gi = 4 * l + 3
        ct = self.const_trk
        xn = [self.palloc([128, NCH, TILE], BF16, f"xn_t{i}") for i in range(2)]
        xn_trks = [[self.ptrk() for c in range(NCH)] for i in range(2)]
        tmp = self.norm_tmp()
        peT = [self.palloc([128, 2, TILE], BF16, f"peT{i}") for i in range(2)]
        perow = [self.palloc([128, PLE], F32, f"perow{i}") for i in range(2)]
        gs = [self.palloc([128, TILE], F32, f"gs{i}") for i in range(2)]
        tm = [self.palloc([128, TILE], F32, f"tm{i}") for i in range(2)]
        slot = self.w_acquire(f"{tag}ple_{l}")
        strk = self.wslot_trk[slot]
        Wg = self.w_view(slot, 0, 8192, [NCH, D])
        Wp = self.w_view(slot, 8192, 2048, [2, D])
        kk = 0
        for t in range(len(self.tiles)):
            ts = self.tsl(t)
            W = self.tiles[t][1]
            pT, pT_trk = peT[t % 2]
            for blk, (tok0, bp) in enumerate(self.blocks(t)):
                pr, pr_trk = perow[blk % 2]
                self.load(pr[0:bp], pe_src[l, tok0:tok0 + bp, :], writes=[pr_trk])
                for c2 in range(2):
                    cx.op("pe", lambda: nc.tensor.transpose(self.bank[4 + c2][:, blk * 128:blk * 128 + bp],
                                                            pr[0:bp, c2 * 128:(c2 + 1) * 128], self.ident[0:bp, 0:bp]),
                          reads=[pr_trk, ct], writes=[self.bank_trk[4 + c2]])
            for c2 in range(2):
                cx.op("act", lambda: nc.scalar.activation(out=pT[:, c2, 0:W], in_=self.bank[4 + c2][:, 0:W], func=AF.Copy),
                      reads=[self.bank_trk[4 + c2]], writes=[pT_trk])
            xt, _ = xn[t % 2]
            xtr = xn_trks[t % 2]
            self.norm_tile(t, gi, xt, xtr, tmp)
            for fc in range(NCH):
                bg, bp_ = fc % 2, 2 + fc % 2
                fs = slice(fc * 128, (fc + 1) * 128)
                cx.mm_group(self.bank[bg][:, 0:W], [(Wg[:, c, fs], xt[:, c, 0:W]) for c in range(NCH)],
                            reads=[strk] + xtr, writes=[self.bank_trk[bg]])
                cx.mm_group(self.bank[bp_][:, 0:W], [(Wp[:, c2, fs], pT[:, c2, 0:W]) for c2 in range(2)],
                            reads=[strk, pT_trk], writes=[self.bank_trk[bp_]])
                g_sb, g_trk = gs[kk % 2]
                t_sb, t_trk = tm[kk % 2]
                kk += 1
                cx.op("act", lambda: nc.scalar.activation(out=g_sb[:, 0:W], in_=self.bank[bg][:, 0:W], func=AF.Sigmoid),
                      reads=[self.bank_trk[bg]], writes=[g_trk])
                cx.op("dve", lambda: nc.vector.tensor_tensor(out=t_sb[:, 0:W], in0=g_sb[:, 0:W], in1=self.bank[bp_][:, 0:W],
                                                             op=ALU.mult),
                      reads=[g_trk, self.bank_trk[bp_]], writes=[t_trk])
                cx.op("dve", lambda: nc.vector.tensor_tensor(out=self.h[:, fc, ts], in0=t_sb[:, 0:W], in1=self.h[:, fc, ts],
                                                             op=ALU.add),
                      reads=[t_trk, self.h_trk[t][fc]], writes=[self.h_trk[t][fc]])
        self.w_release()

    def shared_kv(self, prev, tag=""):
        nc, cx = self.nc, self.cx
        self.phase()
        gi = 16
        ct = self.const_trk
        xn = [self.palloc([128, NCH, TILE], BF16, f"xs_t{i}") for i in range(2)]
        xn_trks = [[self.ptrk() for c in range(NCH)] for i in range(2)]
        tmp = self.norm_tmp()
        kf = [self.palloc([128, D], F32, f"kf{i}") for i in range(2)]
        vf = [self.palloc([128, D], F32, f"vf{i}") for i in range(2)]
        kb = [self.palloc([128, D], BF16, f"kb{i}") for i in range(2)]
        vb = [self.palloc([128, D], BF16, f"vb{i}") for i in range(2)]
        kts = [self.palloc([128, NH, 128], BF16, f"kts{i}") for i in range(2)]
        sqj, sqj_trk = self.palloc([128, D], F32, "sqj")
        ss, ss_trk = self.palloc([128, 2 * NH], F32, "ss")
        fx, fx_trk = self.palloc([128, NH], F32, "fx")
        slotA = self.w_acquire(f"{tag}kvA")
        slotB = self.w_acquire(f"{tag}kvB")
        sA, sB = self.wslot_trk[slotA], self.wslot_trk[slotB]
        Wk = self.w_view(slotA, 0, 8192, [NCH, D])
        Wvf = self.w_view(slotB, 0, NCH * (D + NH), [NCH, D + NH])
        KT_dst = self.KTprev if prev else self.KTown
        V_dst = self.Vprev if prev else self.Vown
        lf_dst = self.lf_prev if prev else self.lf_all
        lf_trks = self.lfp_trk if prev else self.lf_trk
        ptb = self.bank[7].bitcast(BF16)
        b = 0
        for t in range(len(self.tiles)):
            xt, _ = xn[t % 2]
            xtr = xn_trks[t % 2]
            self.norm_tile(t, gi, xt, xtr, tmp)
            for blk, (tok0, bp) in enumerate(self.blocks(t)):
                bs = slice(blk * 128, blk * 128 + bp)
                pk, pk_trk = self.pp[0], self.bank_trk[0:2]
                pv, pv_trk = self.pp[1], self.bank_trk[2:4]
                pf, pf_trk = self.bank[4 + (b % 2)], self.bank_trk[4 + (b % 2)]
                for hf in range(2):
                    fs = slice(hf * 512, (hf + 1) * 512)
                    cx.mm_group(pk[0:bp, fs], [(xt[:, c, bs], Wk[:, c, fs]) for c in range(NCH)],
                                reads=[sA] + xtr, writes=[pk_trk[hf]])
                for hf in range(2):
                    fs = slice(hf * 512, (hf + 1) * 512)
                    cx.mm_group(pv[0:bp, fs], [(xt[:, c, bs], Wvf[:, c, fs]) for c in range(NCH)],
                                reads=[sB] + xtr, writes=[pv_trk[hf]])
                cx.mm_group(pf[0:bp, 0:NH], [(xt[:, c, bs], Wvf[:, c, D:D + NH]) for c in range(NCH)],
                            reads=[sB] + xtr, writes=[pf_trk])
                k_sb, k_trk = kf[b % 2]
                v_sb, v_trk = vf[b % 2]
                kb_sb, kb_trk = kb[b % 2]
                vb_sb, vb_trk = vb[b % 2]
                kt_sb, kt_trk = kts[b % 2]
                k_sb, v_sb, kb_sb, vb_sb = k_sb[0:bp], v_sb[0:bp], kb_sb[0:bp], vb_sb[0:bp]
                cx.op("act", lambda: nc.scalar.activation(out=k_sb, in_=pk[0:bp], func=AF.Copy),
                      reads=pk_trk, writes=[k_trk])
                cx.op("dve", lambda: nc.vector.tensor_tensor(out=sqj[0:bp], in0=k_sb, in1=k_sb, op=ALU.mult),
                      reads=[k_trk], writes=[sqj_trk])
                cx.op("dve", lambda: nc.vector.tensor_reduce(out=ss[0:bp, 0:NH],
                                                             in_=sqj[0:bp].rearrange("p (h d) -> p h d", h=NH),
                                                             axis=AX.X, op=ALU.add),
                      reads=[sqj_trk], writes=[ss_trk])
                cx.op("act", lambda: nc.scalar.activation(out=ss[0:bp, NH:2 * NH], in_=ss[0:bp, 0:NH], func=AF.Ln,
                                                          bias=self.epsb[0:bp], scale=1.0 / HD),
                      reads=[ss_trk, ct], writes=[ss_trk])
                cx.op("act", lambda: nc.scalar.activation(out=ss[0:bp, NH:2 * NH], in_=ss[0:bp, NH:2 * NH], func=AF.Exp,
                                                          scale=-0.5),
                      reads=[ss_trk], writes=[ss_trk])
                k3 = k_sb.rearrange("p (h d) -> p h d", h=NH)
                cx.op("dve", lambda: nc.vector.tensor_tensor(
                    out=k3, in0=k3, in1=ss[0:bp, NH:2 * NH].unsqueeze(2).broadcast_to([bp, NH, HD]), op=ALU.mult),
                    reads=[k_trk, ss_trk], writes=[k_trk])
                cx.op("dve", lambda: nc.vector.tensor_tensor(
                    out=k3, in0=k3, in1=self.gkB[0:bp].unsqueeze(1).broadcast_to([bp, NH, HD]), op=ALU.mult),
                    reads=[k_trk, ct], writes=[k_trk])
                if not prev:
                    self.store(self.k_out[tok0:tok0 + bp, :], k_sb, reads=[k_trk])
                cx.op("act", lambda: nc.scalar.activation(out=kb_sb, in_=k_sb, func=AF.Copy),
                      reads=[k_trk], writes=[kb_trk])
                for hh in range(NH):
                    cx.op("pe", lambda: nc.tensor.transpose(ptb[:, hh * 128:hh * 128 + bp],
                                                            kb_sb[:, hh * 128:(hh + 1) * 128], self.identb[0:bp, 0:bp]),
                          reads=[kb_trk, ct], writes=[self.bank_trk[7]])
                cx.op("dve", lambda: nc.vector.tensor_copy(
                    out=kt_sb[:, :, 0:bp], in_=ptb.rearrange("p (h t) -> p h t", h=NH)[:, :, 0:bp]),
                    reads=[self.bank_trk[7]], writes=[kt_trk])
                self.store(KT_dst.rearrange("h d t -> d h t")[:, :, tok0:tok0 + bp], kt_sb[:, :, 0:bp], reads=[kt_trk])
                cx.op("act", lambda: nc.scalar.activation(out=v_sb, in_=pv[0:bp], func=AF.Copy),
                      reads=pv_trk, writes=[v_trk])
                if not prev:
                    self.store(self.v_out[tok0:tok0 + bp, :], v_sb, reads=[v_trk])
                cx.op("dve", lambda: nc.vector.tensor_copy(out=vb_sb, in_=v_sb), reads=[v_trk], writes=[vb_trk])
                self.store(V_dst[tok0:tok0 + bp, :], vb_sb, reads=[vb_trk])
                lf = lf_dst[0:bp, b, :]
                cx.op("dve", lambda: nc.vector.tensor_tensor(out=fx[0:bp], in0=pf[0:bp, 0:NH], in1=self.bfB[0:bp], op=ALU.add),
                      reads=[pf_trk, ct], writes=[fx_trk])
                cx.op("act", lambda: nc.scalar.activation(out=fx[0:bp], in_=fx[0:bp], func=AF.Exp, scale=-1.0),
                      reads=[fx_trk], writes=[fx_trk])
                cx.op("act", lambda: nc.scalar.activation(out=fx[0:bp], in_=fx[0:bp], func=AF.Ln, bias=self.oneb[0:bp],
                                                          scale=1.0),
                      reads=[fx_trk, ct], writes=[fx_trk])
                cx.op("dve", lambda: nc.vector.tensor_scalar(out=lf, in0=fx[0:bp], scalar1=-1.0, scalar2=None, op0=ALU.mult),
                      reads=[fx_trk], writes=[lf_trks[b]])
                if not prev:
                    self.store(self.lf_out[tok0:tok0 + bp, :], lf, reads=[lf_trks[b]])
                b += 1
        self.w_release()
        self.w_release()

    def prep_forget(self):
        nc, cx = self.nc, self.cx
        ct = self.const_trk
        self.phase()
        tot, tot_trk = self.palloc([128, NH], F32, "tot")
        cn, cn_trk = self.palloc([128, NH], F32, "cn")
        b0, b1, b2 = self.bank[0], self.bank[1], self.bank[2]
        t0, t1, t2 = self.bank_trk[0], self.bank_trk[1], self.bank_trk[2]
        for i in range(16):
            pairs = [(self.tri, self.lf_all[:, i, :])] + [(self.ones_f, self.lf_all[:, j, :]) for j in range(i)]
            cx.mm_group(b0[:, i * NH:(i + 1) * NH], pairs, reads=[ct] + self.lf_trk[0:i + 1], writes=[t0])
        cx.op("dve", lambda: nc.vector.tensor_copy(out=self.csum_own.rearrange("p b h -> p (b h)"), in_=b0[:, 0:16 * NH]),
              reads=[t0], writes=[self.cs_trk])
        for i in range(16):
            pairs = [(self.tri, self.lf_prev[:, i, :])] + [(self.ones_f, self.lf_prev[:, j, :]) for j in range(i)]
            cx.mm_group(b1[:, i * NH:(i + 1) * NH], pairs, reads=[ct] + self.lfp_trk[0:i + 1], writes=[t1])
        cx.mm_group(b2[:, 0:NH], [(self.ones_f, self.lf_prev[:, j, :]) for j in range(16)],
                    reads=[ct] + self.lfp_trk, writes=[t2])
        cx.op("dve", lambda: nc.vector.tensor_scalar(out=tot, in0=b2[:, 0:NH], scalar1=self.flagB[:, 0:1], scalar2=None,
                                                     op0=ALU.add),
              reads=[t2, ct], writes=[tot_trk])
        cx.op("dve", lambda: nc.vector.scalar_tensor_tensor(
            out=self.tail_prev, in0=b1[:, 0:16 * NH].rearrange("p (b h) -> p b h", h=NH), scalar=-1.0,
            in1=tot.unsqueeze(1).broadcast_to([128, 16, NH]), op0=ALU.mult, op1=ALU.add),
            reads=[t1, tot_trk], writes=[self.cs_trk])
        for t in range(4):
            cx.mm_group(b2[:, 64 + t * NH:64 + (t + 1) * NH], [(self.sel, self.csum_own[:, 4 * t + 1, :])],
                        reads=[ct, self.cs_trk], writes=[t2])
        cx.op("dve", lambda: nc.vector.tensor_copy(out=self.Rt.rearrange("p t h -> p (t h)"), in_=b2[:, 64:64 + 4 * NH]),
              reads=[t2], writes=[self.cs_trk])
        cx.mm_group(b2[0:NST, 128:128 + NH], [(self.bdm[0:NST, 0:NST], self.lf_all[0:NST, 16, :])],
                    reads=[ct, self.lf_trk[16]], writes=[t2])
        cx.op("dve", lambda: nc.vector.tensor_scalar(out=self.ncn[0:NST], in0=b2[0:NST, 128:128 + NH], scalar1=-1.0,
                                                     scalar2=None, op0=ALU.mult),
              reads=[t2], writes=[self.cs_trk])

    def prep_past(self):
        nc, cx = self.nc, self.cx
        ct = self.const_trk
        self.phase()
        assert self.bg_nC >= self.bg_total
        tots, tots_trk = self.palloc([128, NPG, NH], F32, "tots")
        suf, suf_trk = self.palloc([128, NPG, NH], F32, "suf")
        for i in range(NSQ):
            lf2 = self.tail_all[:, i].rearrange("p g h -> p (g h)")
            cx.mm_group(self.bank[0][:, 0:128], [(self.su, lf2)], reads=[ct, self.tail_trk], writes=[self.bank_trk[0]])
            cx.mm_group(self.bank[1][:, 0:128], [(self.ones_f, lf2)], reads=[ct, self.tail_trk], writes=[self.bank_trk[1]])
            cx.op("act", lambda: nc.scalar.activation(out=tots.rearrange("p g h -> p (g h)"), in_=self.bank[1][:, 0:128],
                                                      func=AF.Copy),
                  reads=[self.bank_trk[1]], writes=[tots_trk])
            cx.op("dve", lambda: nc.vector.memset(suf[:, NPG - 1, :], 0.0), writes=[suf_trk])
            for pg in range(NPG - 2, -1, -1):
                cx.op("dve", lambda: nc.vector.tensor_tensor(out=suf[:, pg, :], in0=suf[:, pg + 1, :], in1=tots[:, pg + 1, :],
                                                             op=ALU.add),
                      reads=[suf_trk, tots_trk], writes=[suf_trk])
            cx.op("dve", lambda: nc.vector.tensor_tensor(
                out=lf2, in0=self.bank[0][:, 0:128], in1=suf.rearrange("p g h -> p (g h)"), op=ALU.add),
                reads=[self.bank_trk[0], self.bank_trk[1], suf_trk], writes=[self.tail_trk])

    def attention(self, l):
        nc, cx = self.nc, self.cx
        self.phase()
        j = l - N_A
        gi = 4 * l + 1
        ct = self.const_trk
        QT, qt_trk = self.palloc([128, NH, TILE], BF16, "QT")
        OT, ot_trk = self.palloc([128, NH, TILE], BF16, "OT")
        recip, recip_trk = self.palloc([128, TILE], F32, "recip")
        self._att_mark = self.RT.off
        xt, _ = self.palloc([128, NCH, TILE], BF16, "xn_t0")
        xtr = [self.ptrk() for c in range(NCH)]
        tmp = self.norm_tmp()
        sq, sq_trk, rs, rs_trk = tmp
        KTb = [self.palloc([128, 4096], BF16, f"KTb{i}") for i in range(2)]
        Vb = [self.palloc([128, 32, HD], BF16, f"Vb{i}") for i in range(2)]
        pts = [self.palloc([128, TILE], BF16, f"pt{i}") for i in range(3)]
        bias_t, bias_trk = self.palloc([128, 32, NH], F32, "bias_t")
        slotA = self.w_acquire(f"attA_{l}")
        slotB = self.w_acquire(f"attB_{l}")
        sA, sB = self.wslot_trk[slotA], self.wslot_trk[slotB]
        Wq = self.w_view(slotA, 0, 8192, [NCH, D])
        Wo = self.w_view(slotB, 0, 8192, [NCH, D])
        PS = [(self.bank[0], self.bank_trk[0]), (self.bank[1], self.bank_trk[1])]
        POs = [(self.bank[2], self.bank_trk[2]), (self.bank[4], self.bank_trk[4])]
        PDs = [(self.bank[3], self.bank_trk[3]), (self.bank[5], self.bank_trk[5])]

        def q_proj(t):
            W = self.tiles[t][1]
            self.norm_tile(t, gi, xt, xtr, tmp)
            for hh in range(NH):
                bi = 7
                qp, qp_trk = self.bank[bi], self.bank_trk[bi]
                cx.mm_group(qp[:, 0:W], [(Wq[:, c, hh * 128:(hh + 1) * 128], xt[:, c, 0:W]) for c in range(NCH)],
                            reads=[sA] + xtr, writes=[qp_trk])
                sqh = sq[hh % 2][:, 0, 0:W]
                cx.op("act", lambda: nc.scalar.activation(out=sqh, in_=qp[:, 0:W], func=AF.Square),
                      reads=[qp_trk], writes=[sq_trk[hh % 2]])
                cx.mm_group(self.bank[6][:, 0:W], [(self.ones_hd, sqh)], reads=[ct, sq_trk[hh % 2]],
                            writes=[self.bank_trk[6]])
                cx.op("act", lambda: nc.scalar.activation(out=rs[:, 0:W], in_=self.bank[6][:, 0:W], func=AF.Ln,
                                                          bias=self.epsb, scale=1.0),
                      reads=[self.bank_trk[6], ct], writes=[rs_trk])
                cx.op("act", lambda: nc.scalar.activation(out=rs[:, 0:W], in_=rs[:, 0:W], func=AF.Exp, scale=-0.5),
                      reads=[rs_trk], writes=[rs_trk])
                cx.op("dve", lambda: nc.vector.scalar_tensor_tensor(
                    out=QT[:, hh, 0:W], in0=qp[:, 0:W], scalar=self.gqcol[:, j:j + 1], in1=rs[:, 0:W],
                    op0=ALU.mult, op1=ALU.mult),
                    reads=[qp_trk, ct, rs_trk], writes=[qt_trk])

        def o_proj(t):
            W = self.tiles[t][1]
            ts = self.tsl(t)
            for fc in range(NCH):
                bi = 7
                cx.mm_group(self.bank[bi][:, 0:W], [(Wo[:, hh, fc * 128:(fc + 1) * 128], OT[:, hh, 0:W]) for hh in range(NH)],
                            reads=[sB, ot_trk], writes=[self.bank_trk[bi]])
                cx.op("dve", lambda: nc.vector.tensor_tensor(out=self.h[:, fc, ts], in0=self.bank[bi][:, 0:W],
                                                             in1=self.h[:, fc, ts], op=ALU.add),
                      reads=[self.bank_trk[bi], self.h_trk[t][fc]], writes=[self.h_trk[t][fc]])

        units = [(t, hh) for t in range(4) for hh in range(NH)]

        def load_kv(u):
            t, hh = units[u]
            kt, kt_trk = KTb[u % 2]
            vv, vv_trk = Vb[u % 2]
            nown = (t + 1) * 512
            self.load(kt[:, 0:NPT], self.KTprev[hh], writes=[kt_trk])
            self.load(kt[:, NPT:NPT + nown], self.KTown[hh, :, 0:nown], writes=[kt_trk])
            self.load(vv[:, 0:16, :],
                      self.Vprev.rearrange("(b p) (h d) -> p b h d", p=128, h=NH)[:, :, hh, :], writes=[vv_trk])
            self.load(vv[:, 16:16 + 4 * (t + 1), :],
                      self.Vown[0:nown].rearrange("(b p) (h d) -> p b h d", p=128, h=NH)[:, :, hh, :], writes=[vv_trk])

        def attend(u):
            t, hh = units[u]
            kt, kt_trk = KTb[u % 2]
            vv, vv_trk = Vb[u % 2]
            PO, po_trk = POs[u % 2]
            PD, pd_trk = PDs[u % 2]
            blocks = [(b, 0) for b in range(16 + 4 * t)] + [(16 + 4 * t + m, 128 * m) for m in range(4)]
            n = len(blocks)

            def emit_s(i):
                b, qo = blocks[i]
                N = TILE - qo
                ps, ps_trk = PS[i % 2]
                pt, pt_trk = pts[i % 3]
                cx.mm(ps[:, 0:N], kt[:, b * 128:(b + 1) * 128], QT[:, hh, qo:TILE], True, True,
                      reads=[kt_trk, qt_trk], writes=[ps_trk])
                cx.op("act", lambda: nc.scalar.activation(out=pt[:, 0:N], in_=ps[:, 0:N], func=AF.Exp,
                                                          bias=bias_t[:, b, hh:hh + 1], scale=SCALE),
                      reads=[ps_trk, bias_trk], writes=[pt_trk])
                if b >= 16 + 4 * t:
                    cx.op("dve", lambda: nc.vector.tensor_tensor(out=pt[:, 0:128], in0=pt[:, 0:128], in1=self.tri,
                                                                 op=ALU.mult),
                          reads=[pt_trk, ct], writes=[pt_trk])

            def emit_pv(i):
                b, qo = blocks[i]
                N = TILE - qo
                pt, pt_trk = pts[i % 3]
                cx.mm(PO[:, qo:TILE], vv[:, b, :], pt[:, 0:N], i == 0, i == n - 1,
                      reads=[vv_trk, pt_trk], writes=[po_trk])
                cx.mm(PD[:, qo:TILE], self.ones_bf, pt[:, 0:N], i == 0, i == n - 1,
                      reads=[ct, pt_trk], writes=[pd_trk])

            emit_s(0)
            for i in range(n):
                if i + 1 < n:
                    emit_s(i + 1)
                emit_pv(i)
            cx.op("dve", lambda: nc.vector.reciprocal(out=recip, in_=PD), reads=[pd_trk], writes=[recip_trk])
            cx.op("dve", lambda: nc.vector.tensor_tensor(out=OT[:, hh, :], in0=PO, in1=recip, op=ALU.mult),
                  reads=[po_trk, recip_trk], writes=[ot_trk])

        load_kv(0)
        for t in range(4):
            q_proj(t)
            nb = 4 * (t + 1)
            cx.op("dve", lambda: nc.vector.tensor_tensor(
                out=bias_t[:, 0:16, :], in0=self.tail_prev, in1=self.Rt[:, t:t + 1, :].broadcast_to([128, 16, NH]),
                op=ALU.add), reads=[self.cs_trk], writes=[bias_trk])
            cx.op("dve", lambda: nc.vector.scalar_tensor_tensor(
                out=bias_t[:, 16:16 + nb, :], in0=self.csum_own[:, 0:nb, :], scalar=-1.0,
                in1=self.Rt[:, t:t + 1, :].broadcast_to([128, nb, NH]), op0=ALU.mult, op1=ALU.add),
                reads=[self.cs_trk], writes=[bias_trk])
            for hh in range(NH):
                u = t * NH + hh
                if u + 1 < len(units):
                    load_kv(u + 1)
                attend(u)
            o_proj(t)

        t = 4
        q_proj(t)
        if self.do_sample_attn:
            self.sample_attention(QT, qt_trk, OT, ot_trk, recip, recip_trk)
        else:
            cx.op("dve", lambda: nc.vector.memset(OT[:, :, 0:NST], 0.0), writes=[ot_trk])
        o_proj(t)
        self.w_release()
        self.w_release()

    def sample_attention(self, QT, qt_trk, OT, ot_trk, recip, recip_trk):
        nc, cx = self.nc, self.cx
        ct = self.const_trk
        PSs = [(self.bank[0], self.bank_trk[0]), (self.bank[1], self.bank_trk[1])]
        PO, po_trk = self.bank[2], self.bank_trk[2]
        PD, pd_trk = self.bank[3], self.bank_trk[3]
        PN, pn_trk = self.bank[4], self.bank_trk[4]
        self.RT.off = self._att_mark
        self._phase_barrier = cx.barrier_events()
        KTn, ktn_trk = self.palloc([128, NH, NST], BF16, "KTn")
        Vn, vn_trk = self.palloc([128, D], BF16, "Vn")
        tn, tn_trk = self.palloc([128, NH, NST], F32, "tn")
        pn, pnb_trk = self.palloc([128, NH, NST], BF16, "pn")
        ktg = [self.palloc([128, PGG, D], BF16, f"ktg{i}") for i in range(2)]
        Vh = [self.palloc([128, PGG, D], BF16, f"Vh{i}") for i in range(2)]
        tsc, tsc_trk = self.palloc([128, PGG, NH, 4], F32, "tsc")
        psb = [self.palloc([128, PGG, NH, 4], BF16, f"psb{i}") for i in range(2)]
        PO3 = PO.rearrange("p (h q) -> p h q", h=NH)
        PD3 = PD.rearrange("p (h q) -> p h q", h=NH)
        self.load(KTn, self.KTown.rearrange("h d t -> d h t")[:, :, NPT:NT], writes=[ktn_trk])
        self.load(Vn[0:NST], self.Vown[NPT:NT, :], writes=[vn_trk])
        for hh in range(NH):
            cx.mm(PN[0:NST, hh * NST:(hh + 1) * NST], KTn[:, hh, :], QT[:, hh, 0:NST], True, True,
                  reads=[ktn_trk, qt_trk], writes=[pn_trk])
        cx.op("dve", lambda: nc.vector.scalar_tensor_tensor(
            out=tn[0:NST], in0=PN[0:NST].rearrange("p (h q) -> p h q", h=NH), scalar=SCALE,
            in1=self.ncn[0:NST].unsqueeze(2).broadcast_to([NST, NH, NST]), op0=ALU.mult, op1=ALU.add),
            reads=[pn_trk, self.cs_trk], writes=[tn_trk])
        cx.op("act", lambda: nc.scalar.activation(out=pn[0:NST], in_=tn[0:NST], func=AF.Exp),
              reads=[tn_trk], writes=[pnb_trk])
        cx.op("dve", lambda: nc.vector.tensor_tensor(
            out=pn[0:NST], in0=pn[0:NST], in1=self.bdm[0:NST, 0:NST].unsqueeze(1).broadcast_to([NST, NH, NST]),
            op=ALU.mult), reads=[pnb_trk, ct], writes=[pnb_trk])
        zt, zt_trk = self.palloc([128, TILE], BF16, "zt")
        cx.op("dve", lambda: nc.vector.memset(zt, 0.0), writes=[zt_trk])
        cx.mm(PO, self.ones_bf, zt, True, False, reads=[ct, zt_trk], writes=[po_trk])
        cx.mm(PD, self.ones_bf, zt, True, False, reads=[ct, zt_trk], writes=[pd_trk])
        for hh in range(NH):
            cx.mm(PO3[:, hh, :], Vn[0:NST, hh * 128:(hh + 1) * 128], pn[0:NST, hh, :], False, False,
                  reads=[vn_trk, pnb_trk], writes=[po_trk])
            cx.mm(PD3[:, hh, :], self.ones_bf[0:NST, :], pn[0:NST, hh, :], False, False,
                  reads=[ct, pnb_trk], writes=[pd_trk])
        gcount = 0
        for i in range(NSQ):
            for half in range(NPG // PGG):
                ps, ps_trk = PSs[gcount % 2]
                vh, vh_trk = Vh[gcount % 2]
                pb, pb_trk = psb[gcount % 2]
                ps4 = ps[:, 0:PGG * NH * 4].rearrange("p (g h q) -> p g h q", g=PGG, h=NH)
                kg, kg_trk = ktg[gcount % 2]
                j0 = i * NPG + half * PGG
                self.load(kg, self.KTpast[j0:j0 + PGG].rearrange("g d x -> d g x"), writes=[kg_trk])
                self.load(vh, self.Vpast[j0:j0 + PGG].rearrange("g t x -> t g x"), writes=[vh_trk])
                for pg in range(PGG):
                    for hh in range(NH):
                        cx.mm(ps4[:, pg, hh, :], kg[:, pg, hh * 128:(hh + 1) * 128], QT[:, hh, 4 * i:4 * i + 4], True, True,
                              reads=[kg_trk, qt_trk], writes=[ps_trk])
                cx.op("dve", lambda: nc.vector.scalar_tensor_tensor(
                    out=tsc, in0=ps4, scalar=SCALE,
                    in1=self.tail_all[:, i, half * PGG:(half + 1) * PGG, :].unsqueeze(3).broadcast_to([128, PGG, NH, 4]),
                    op0=ALU.mult, op1=ALU.add), reads=[ps_trk, self.tail_trk], writes=[tsc_trk])
                cx.op("act", lambda: nc.scalar.activation(out=pb, in_=tsc, func=AF.Exp), reads=[tsc_trk], writes=[pb_trk])
                last = (half == NPG // PGG - 1)
                for pg in range(PGG):
                    for hh in range(NH):
                        cx.mm(PO3[:, hh, 4 * i:4 * i + 4], vh[:, pg, hh * 128:(hh + 1) * 128], pb[:, pg, hh, :],
                              False, last and pg == PGG - 1, reads=[vh_trk, pb_trk], writes=[po_trk])
                    cx.mm(PD3[:, :, 4 * i:4 * i + 4], self.ones_bf, pb[:, pg, :, :], False, last and pg == PGG - 1,
                          reads=[ct, pb_trk], writes=[pd_trk])
                gcount += 1
        cx.op("dve", lambda: nc.vector.reciprocal(out=recip, in_=PD), reads=[pd_trk], writes=[recip_trk])
        cx.op("dve", lambda: nc.vector.tensor_tensor(out=OT[:, :, 0:NST], in0=PO3, in1=recip.rearrange("p (h q) -> p h q", h=NH),
                                                     op=ALU.mult),
              reads=[po_trk, recip_trk], writes=[ot_trk])

    def build(self):
        attn = self.do_attn and self.n_layers > N_A
        self.bg_total = NSQ * NPG if (attn and self.do_sample_attn) else 0
        self.bg_nA = self.bg_nC = 0
        self.bg_bufs = None
        self.plan_weights()
        self.w_issue()
        self.w_issue()
        self.setup_consts()
        if attn:
            self.tiles = PREV_TILES
            self.load_x(self.xprev)
            for l in range(N_A):
                self.ffn(l, 1, "p")
                self.gmlp(l, "p", emit_gv=False)
                self.ffn(l, 2, "p")
                self.ple(l, self.pprev, "p")
            self.shared_kv(True, "p")
        self.tiles = MAIN_TILES
        self.load_x(self.xin)
        for l in range(self.n_layers):
            self.ffn(l, 1)
            if l < N_A:
                self.gmlp(l)
            elif attn:
                self.attention(l)
            self.ffn(l, 2)
            self.ple(l, self.pin)
            if l == N_A - 1:
                self.shared_kv(False)
                if attn:
                    self.prep_forget()
                    if self.do_sample_attn:
                        self.prep_past()
        self.store_y(self.y_out)
        self.cx.finish("sp")
        print(f"[kernel] instructions emitted: {self.cx.n_ins}")


def _consts():
    s = np.arange(128)
    ident = np.eye(128, dtype=np.float32)
    tri = (s[:, None] <= s[None, :]).astype(np.float32)
    bdm = ((s[:, None] // 4 == s[None, :] // 4) & (s[:, None] % 4 <= s[None, :] % 4)).astype(np.float32)
    rep = ((s[None, :] % 4 == s[:, None]) & (s[:, None] < 4)).astype(np.float32)
    su = (s[:, None] > s[None, :]).astype(np.float32)
    sel = np.zeros((128, 128), np.float32)
    sel[127, :] = 1.0
    iota = s.astype(np.int32).reshape(128, 1)
    return ident, tri, bdm, rep, su, sel, iota


def make_in_maps(inputs, n_cores=8, with_cache=True):
    f = lambda a: np.ascontiguousarray(np.asarray(a, dtype=np.float32))
    x_prompt = f(inputs["x_prompt"])
    x_sample = f(inputs["x_sample"]).reshape(128 * 4, D)
    p_prompt = f(inputs["p_prompt"])
    p_sample = f(inputs["p_sample"]).reshape(DEPTH, 128 * 4, PLE)
    norms = np.zeros((17, D), np.float32)
    for l in range(DEPTH):
        norms[4 * l + 0] = inputs["ffn1_norm"][l]
        norms[4 * l + 1] = inputs["mix_norm"][l]
        norms[4 * l + 2] = inputs["ffn2_norm"][l]
        norms[4 * l + 3] = inputs["ple_norm"][l]
    norms[16] = inputs["kv_norm"]
    ident, tri, bdm, rep, su, sel, iota = _consts()
    shared = {
        "norms": norms,
        "ple_w_gate": f(inputs["ple_w_gate"]), "ple_w_proj": f(inputs["ple_w_proj"]),
        "gmlp_w_in": f(inputs["gmlp_w_in"]), "gmlp_v_norm": f(inputs["gmlp_v_norm"]),
        "gmlp_w_s": f(inputs["gmlp_w_s"]), "gmlp_b_s": f(inputs["gmlp_b_s"]),
        "gmlp_w_out": f(inputs["gmlp_w_out"]),
        "w_kvf": f(inputs["w_kvf"]), "b_f": f(inputs["b_f"]).reshape(1, NH),
        "k_norm": f(inputs["k_norm"]).reshape(1, HD),
        "att_w_q": f(inputs["att_w_q"]), "q_norm": f(inputs["q_norm"]), "att_w_o": f(inputs["att_w_o"]),
        "c_ident": ident, "c_tri": tri, "c_bdm": bdm, "c_rep": rep, "c_su": su, "c_sel": sel, "c_iota": iota,
    }
    if with_cache:
        shared["cache_k"] = f(inputs["cache_k"]).reshape(NPOOL * 128, D)
        shared["cache_v"] = f(inputs["cache_v"]).reshape(NPOOL * 128, D)
        shared["cache_lf"] = f(inputs["cache_logf"]).reshape(NPOOL * 128, NH)
    else:
        shared["cache_k"] = np.zeros((128, D), np.float32)
        shared["cache_v"] = np.zeros((128, D), np.float32)
        shared["cache_lf"] = np.zeros((128, NH), np.float32)
    for which in (1, 2):
        for nm in ("w1", "w3", "w2"):
            shared[f"ffn{which}_{nm}"] = f(inputs[f"ffn{which}_{nm}"])
    page_table = np.asarray(inputs["page_table"], dtype=np.int32)
    maps = []
    for c in range(n_cores):
        s, hf = c // 2, c % 2
        m = dict(shared)
        sq = slice(NSQ * 4 * c, NSQ * 4 * (c + 1))
        m["xin"] = np.concatenate([x_prompt[s, hf * NPT:(hf + 1) * NPT], x_sample[sq]], axis=0)
        m["pin"] = np.concatenate([p_prompt[:, s, hf * NPT:(hf + 1) * NPT], p_sample[:, sq]], axis=1)
        m["xprev"] = np.ascontiguousarray(x_prompt[s, 0:NPT])
        m["pprev"] = np.ascontiguousarray(p_prompt[0:N_A, s, 0:NPT])
        m["flag"] = np.full((1, 1), NEG if hf == 0 else 0.0, np.float32)
        m["ptab"] = np.ascontiguousarray(page_table[NSQ * c:NSQ * (c + 1)].reshape(1, NSQ * NPG))
        maps.append(m)
    return maps


def assemble(results, n_cores=8):
    B, S = 4, 4096
    y_prompt = np.zeros((B, S, D), np.float32)
    k_prompt = np.zeros((B, S, NH, HD), np.float32)
    v_prompt = np.zeros((B, S, NH, HD), np.float32)
    lf_prompt = np.zeros((B, S, NH), np.float32)
    y_sample = np.zeros((128, 4, D), np.float32)
    k_sample = np.zeros((128, 4, NH, HD), np.float32)
    v_sample = np.zeros((128, 4, NH, HD), np.float32)
    lf_sample = np.zeros((128, 4, NH), np.float32)
    gv = np.zeros((N_A, 128, 4, D), np.float32)
    for c in range(n_cores):
        s, hf = c // 2, c % 2
        r = results[c]
        sl = slice(hf * NPT, (hf + 1) * NPT)
        bq = slice(NSQ * c, NSQ * (c + 1))
        y_prompt[s, sl] = r["y_out"][:NPT]
        k_prompt[s, sl] = r["k_out"][:NPT].reshape(NPT, NH, HD)
        v_prompt[s, sl] = r["v_out"][:NPT].reshape(NPT, NH, HD)
        lf_prompt[s, sl] = r["lf_out"][:NPT]
        y_sample[bq] = r["y_out"][NPT:].reshape(NSQ, 4, D)
        k_sample[bq] = r["k_out"][NPT:].reshape(NSQ, 4, NH, HD)
        v_sample[bq] = r["v_out"][NPT:].reshape(NSQ, 4, NH, HD)
        lf_sample[bq] = r["lf_out"][NPT:].reshape(NSQ, 4, NH)
        gv[:, bq] = r["gv_out"].reshape(N_A, NSQ, 4, D)
    return (y_prompt, y_sample, k_prompt, v_prompt, lf_prompt, k_sample, v_sample, lf_sample, gv)


def kernel(**inputs):
    n_cores = 8
    nc = bass.Bass("TRN2", target_bir_lowering=False)
    Builder(nc).build()
    in_maps = make_in_maps(inputs, n_cores)
    res = run_bass_kernel_spmd(nc, in_maps, core_ids=list(range(n_cores)))
    return assemble(res.results, n_cores)
```

```python
import numpy as np
import concourse.bass as bass
import concourse.mybir as mybir
from concourse.bass_utils import run_bass_kernel_spmd

F32 = mybir.dt.float32
BF16 = mybir.dt.bfloat16
I32 = mybir.dt.int32
AF = mybir.ActivationFunctionType
ALU = mybir.AluOpType
AX = mybir.AxisListType

ENGS = ("pe", "act", "dve", "pool", "sp")

D = 1024
NCH = 8
DFF = 2816
NFC = 22
PLE = 256
DEPTH = 4
N_A = 2
NH = 8
HD = 128
TILE = 512
NPT = 2048
NSQ = 16
NST = NSQ * 4
NT = NPT + NST
MAIN_TILES = [(0, 512), (512, 512), (1024, 512), (1536, 512), (2048, NST)]
PREV_TILES = [(0, 512), (512, 512), (1024, 512), (1536, 512)]
NBLK = 17
NPG = 16
NPOOL = 2560
SCALE = HD ** -0.5
NEG = -30000.0
PGG = 4
BGD = 3
EPS = 1e-6
SLOT_EL = 12288
FFN_GROUPS = [(0, 4), (4, 4), (8, 4), (12, 4), (16, 4), (20, 2)]


class Trk:
    __slots__ = ("name", "last_w", "reads")

    def __init__(self, name=""):
        self.name = name
        self.last_w = None
        self.reads = []


class Chan:
    def __init__(self, ctx, name):
        self.key = "dma_" + name
        self.sem = ctx.nc.alloc_semaphore(name=self.key)
        self.count = 0
        ctx.semh[self.key] = self.sem


class Ctx:
    def __init__(self, nc, same_engine_sync=True):
        self.nc = nc
        self.eng = {"pe": nc.tensor, "act": nc.scalar, "dve": nc.vector,
                    "pool": nc.gpsimd, "sp": nc.sync}
        self.semh = {}
        self.cnt = {}
        for e in ENGS:
            self.semh[e] = nc.alloc_semaphore(name="sem_" + e)
            self.cnt[e] = 0
        self.seen = {e: {} for e in ENGS}
        self.same_engine_sync = same_engine_sync
        self.chans = []
        self.n_ins = 0
        self.all_trk = []

    def trk(self, name=""):
        t = Trk(name)
        self.all_trk.append(t)
        return t

    def chan(self, name):
        c = Chan(self, name)
        self.chans.append(c)
        return c

    def _wait(self, e, events):
        need = {}
        for ev in events:
            if ev is None:
                continue
            k, v = ev
            if k == e and (e == "pe" or not self.same_engine_sync):
                continue
            if need.get(k, 0) < v:
                need[k] = v
        seen = self.seen[e]
        for k, v in need.items():
            if seen.get(k, 0) < v:
                self.eng[e].wait_ge(self.semh[k], v)
                seen[k] = v

    @staticmethod
    def _deps(reads, writes):
        evs = []
        for t in reads:
            evs.append(t.last_w)
        for t in writes:
            evs.append(t.last_w)
            evs.extend(t.reads)
        return evs

    @staticmethod
    def _commit(ev, reads, writes):
        for t in reads:
            t.reads.append(ev)
            if len(t.reads) > 32:
                mx = {}
                for k, v in t.reads:
                    if mx.get(k, 0) < v:
                        mx[k] = v
                t.reads = list(mx.items())
        for t in writes:
            t.last_w = ev
            t.reads = []

    def op(self, e, fn, reads=(), writes=()):
        self._wait(e, self._deps(reads, writes))
        ins = fn()
        self.cnt[e] += 1
        ins.then_inc(self.semh[e], 1)
        ev = (e, self.cnt[e])
        self._commit(ev, reads, writes)
        self.n_ins += 1
        return ev

    def mm_group(self, out_ap, pairs, reads=(), writes=()):
        e = "pe"
        self._wait(e, self._deps(reads, writes))
        n = len(pairs)
        ins = None
        for i, (l, r) in enumerate(pairs):
            ins = self.nc.tensor.matmul(out_ap, l, r, start=(i == 0), stop=(i == n - 1))
        self.n_ins += n
        self.cnt[e] += 1
        ins.then_inc(self.semh[e], 1)
        ev = (e, self.cnt[e])
        self._commit(ev, reads, writes)
        return ev

    def mm(self, out_ap, lhsT, rhs, start, stop, reads=(), writes=()):
        e = "pe"
        self._wait(e, self._deps(reads, writes))
        ins = self.nc.tensor.matmul(out_ap, lhsT, rhs, start=start, stop=stop)
        self.n_ins += 1
        self.cnt[e] += 1
        ins.then_inc(self.semh[e], 1)
        ev = (e, self.cnt[e])
        self._commit(ev, reads, writes)
        return ev

    def idma(self, chan, out, in_, idx_ap, reads=(), writes=()):
        q = "pool"
        self._wait(q, self._deps(reads, writes))
        ins = self.nc.gpsimd.indirect_dma_start(
            out=out, out_offset=None, in_=in_,
            in_offset=bass.IndirectOffsetOnAxis(ap=idx_ap, axis=0))
        chan.count += 1
        ins.then_inc(chan.sem, 16)
        ev = (chan.key, 16 * chan.count)
        self._commit(ev, reads, writes)
        self.n_ins += 1
        return ev

    def dma(self, q, chan, out, in_, reads=(), writes=()):
        self._wait(q, self._deps(reads, writes))
        ins = self.eng[q].dma_start(out=out, in_=in_)
        chan.count += 1
        ins.then_inc(chan.sem, 16)
        ev = (chan.key, 16 * chan.count)
        self._commit(ev, reads, writes)
        self.n_ins += 1
        return ev

    def barrier_events(self):
        evs = [(e, self.cnt[e]) for e in ENGS if self.cnt[e]]
        evs += [(c.key, 16 * c.count) for c in self.chans if c.count and not c.key.startswith("dma_w")]
        return evs

    def finish(self, e="sp"):
        self._wait(e, self.barrier_events())


class SB:
    def __init__(self, big_ap, base, limit):
        self.big = big_ap
        self.base = base
        self.off = base
        self.limit = limit

    def reset(self):
        self.off = self.base

    def alloc(self, shape, dtype, align=64):
        esz = 4 if dtype in (F32, I32) else 2
        n = int(np.prod(shape[1:]))
        nbytes = esz * n
        self.off = (self.off + align - 1) // align * align
        a = self.big[:, self.off // 2:(self.off + nbytes) // 2]
        if dtype != BF16:
            a = a.bitcast(dtype)
        if shape[0] != 128:
            a = a[0:shape[0]]
        if len(shape) == 3:
            a = a.rearrange("p (a b) -> p a b", a=shape[1])
        elif len(shape) == 4:
            a = a.rearrange("p (a b c) -> p a b c", a=shape[1], b=shape[2])
        self.off += nbytes
        assert self.off <= self.limit, ("SBUF region overflow", self.off, self.limit)
        return a


class Builder:
    def __init__(self, nc, n_layers=DEPTH, do_attn=True, do_sample_attn=True):
        self.nc = nc
        self.n_layers = n_layers
        self.do_attn = do_attn
        self.do_sample_attn = do_sample_attn
        self.cx = Ctx(nc)
        self.tiles = MAIN_TILES
        self.npool = NPOOL if do_sample_attn else 1
        self.declare_dram()
        self.alloc_onchip()

    def declare_dram(self):
        nc = self.nc

        def inp(name, shape, dt=F32):
            return nc.dram_tensor(name, list(shape), dt, kind="ExternalInput").ap()

        def outp(name, shape, dt=F32):
            return nc.dram_tensor(name, list(shape), dt, kind="ExternalOutput").ap()

        def scr(name, shape, dt=BF16):
            return nc.dram_tensor(name, list(shape), dt, kind="Internal").ap()

        self.xin = inp("xin", [NT, D])
        self.pin = inp("pin", [DEPTH, NT, PLE])
        self.xprev = inp("xprev", [NPT, D])
        self.pprev = inp("pprev", [N_A, NPT, PLE])
        self.flag = inp("flag", [1, 1])
        self.ptab = inp("ptab", [1, NSQ * NPG], I32)
        self.cache_k = inp("cache_k", [self.npool * 128, D])
        self.cache_v = inp("cache_v", [self.npool * 128, D])
        self.cache_lf = inp("cache_lf", [self.npool * 128, NH])
        self.norms = inp("norms", [17, D])
        self.w_ffn = {}
        for which in (1, 2):
            for nm in ("w1", "w3", "w2"):
                shp = [DEPTH, D, DFF] if nm != "w2" else [DEPTH, DFF, D]
                self.w_ffn[(which, nm)] = inp(f"ffn{which}_{nm}", shp)
        self.ple_w_gate = inp("ple_w_gate", [DEPTH, D, D])
        self.ple_w_proj = inp("ple_w_proj", [DEPTH, PLE, D])
        self.gmlp_w_in = inp("gmlp_w_in", [N_A, D, 2 * D])
        self.gmlp_v_norm = inp("gmlp_v_norm", [N_A, D])
        self.gmlp_w_s = inp("gmlp_w_s", [N_A, 8, 128, 128])
        self.gmlp_b_s = inp("gmlp_b_s", [N_A, 8, 128])
        self.gmlp_w_out = inp("gmlp_w_out", [N_A, D, D])
        self.w_kvf = inp("w_kvf", [D, 2 * D + NH])
        self.b_f = inp("b_f", [1, NH])
        self.k_norm = inp("k_norm", [1, HD])
        self.att_w_q = inp("att_w_q", [2, D, D])
        self.q_norm = inp("q_norm", [2, HD])
        self.att_w_o = inp("att_w_o", [2, D, D])
        self.c_ident = inp("c_ident", [128, 128])
        self.c_tri = inp("c_tri", [128, 128])
        self.c_bdm = inp("c_bdm", [128, 128])
        self.c_rep = inp("c_rep", [128, 128])
        self.c_su = inp("c_su", [128, 128])
        self.c_sel = inp("c_sel", [128, 128])
        self.c_iota = inp("c_iota", [128, 1], I32)
        self.y_out = outp("y_out", [NT, D])
        self.k_out = outp("k_out", [NT, D])
        self.v_out = outp("v_out", [NT, D])
        self.lf_out = outp("lf_out", [NT, NH])
        self.gv_out = outp("gv_out", [N_A, NST, D])
        self.KTprev = scr("KTprev", [NH, HD, NPT])
        self.Vprev = scr("Vprev", [NPT, D])
        self.KTown = scr("KTown", [NH, HD, NT])
        self.Vown = scr("Vown", [NT, D])
        self.KTpast = scr("KTpast", [NSQ * NPG, HD, D])
        self.Vpast = scr("Vpast", [NSQ * NPG, 128, D])

    def alloc_onchip(self):
        nc, cx = self.nc, self.cx
        total = nc.sbuf_bytes_remaining
        total = (total - 256) // 64 * 64
        big = nc.alloc_sbuf_tensor("big", [128, total // 2], BF16).ap()
        self.big = big
        P = SB(big, 0, total)
        self.h = P.alloc([128, NCH, NT], F32)
        self.h_trk = [[cx.trk(f"h{t}_{c}") for c in range(NCH)] for t in range(len(MAIN_TILES))]
        self.wslot = [P.alloc([128, SLOT_EL], BF16) for _ in range(2)]
        self.wslot_trk = [cx.trk("wslot0"), cx.trk("wslot1")]
        self.ident = P.alloc([128, 128], F32)
        self.identb = P.alloc([128, 128], BF16)
        self.ones_mean = P.alloc([128, 128], BF16)
        self.ones_hd = P.alloc([128, 128], BF16)
        self.ones_bf = P.alloc([128, 128], BF16)
        self.ones_f = P.alloc([128, 128], F32)
        self.tri = P.alloc([128, 128], F32)
        self.bdm = P.alloc([128, 128], F32)
        self.rep = P.alloc([128, 128], F32)
        self.su = P.alloc([128, 128], F32)
        self.sel = P.alloc([128, 128], F32)
        self.gcol = P.alloc([128, 17 * NCH], F32)
        self.gqcol = P.alloc([128, 2], F32)
        self.epsb = P.alloc([128, 1], F32)
        self.oneb = P.alloc([128, 1], F32)
        self.flagB = P.alloc([128, 1], F32)
        self.iota = P.alloc([128, 1], I32)
        self.lf_all = P.alloc([128, NBLK, NH], F32)
        self.lf_prev = P.alloc([128, 16, NH], F32)
        self.csum_own = P.alloc([128, 16, NH], F32)
        self.tail_prev = P.alloc([128, 16, NH], F32)
        self.Rt = P.alloc([128, 4, NH], F32)
        self.ncn = P.alloc([128, NH], F32)
        self.idx_all = P.alloc([128, NSQ * NPG], I32)
        self.tail_all = P.alloc([128, NSQ, NPG, NH], F32)
        self.bfB = P.alloc([128, NH], F32)
        self.gkB = P.alloc([128, HD], F32)
        self.const_trk = cx.trk("const")
        self.lf_trk = [cx.trk("lf") for b in range(NBLK)]
        self.lfp_trk = [cx.trk("lfp") for b in range(16)]
        self.cs_trk = cx.trk("csum")
        self.idx_trk = cx.trk("idx")
        self.tail_trk = cx.trk("tailall")
        self.RT = SB(big, P.off, total)
        self.rt_bytes = total - P.off
        print(f"[kernel] SBUF persistent={P.off} phase_region={self.rt_bytes}")
        self.pp = [nc.alloc_psum_tensor(f"pp{i}", [128, 1024], F32).ap() for i in range(4)]
        self.bank = []
        for i in range(4):
            self.bank += [self.pp[i][:, 0:512], self.pp[i][:, 512:1024]]
        self.bank_trk = [cx.trk(f"bank{i}") for i in range(8)]
        self.ch_w = [cx.chan("w0"), cx.chan("w1")]
        self.ch_const = cx.chan("const")
        self.ch_by_name = {}

    def phase(self):
        self.RT.reset()
        self._phase_barrier = self.cx.barrier_events()

    def ptrk(self, name=""):
        t = self.cx.trk(name)
        t.reads = list(self._phase_barrier)
        return t

    def palloc(self, shape, dtype, name=""):
        return self.RT.alloc(shape, dtype), self.ptrk(name)

    def plan_weights(self):
        pieces = []

        def chunked(ap2d):
            return ap2d.rearrange("(c p) f -> p c f", p=128)

        def ffn_pieces(l, which, tag):
            for (f0, gc) in FFN_GROUPS:
                w1 = self.w_ffn[(which, "w1")][l]
                w3 = self.w_ffn[(which, "w3")][l]
                w2 = self.w_ffn[(which, "w2")][l]
                n1 = NCH * gc * 128
                pieces.append((f"{tag}ffn{which}_{l}_{f0}", [
                    ((0, n1, [NCH, gc * 128]), chunked(w1[:, f0 * 128:(f0 + gc) * 128])),
                    ((4096, n1, [NCH, gc * 128]), chunked(w3[:, f0 * 128:(f0 + gc) * 128])),
                    ((8192, gc * D, [gc, D]), chunked(w2[f0 * 128:(f0 + gc) * 128, :])),
                ]))

        def gmlp_pieces(l, tag):
            wi = self.gmlp_w_in[l]
            pieces.append((f"{tag}gmlpA_{l}", [
                ((0, 8192, [NCH, D]), chunked(wi[:, 0:D])),
                ((8192, 4096, [4, D]), chunked(wi[0:512, D:2 * D])),
            ]))
            pieces.append((f"{tag}gmlpB_{l}", [
                ((0, 4096, [4, D]), chunked(wi[512:1024, D:2 * D])),
                ((4096, 8192, [NCH, D]), chunked(self.gmlp_w_out[l])),
            ]))

        def ple_piece(l, tag):
            pieces.append((f"{tag}ple_{l}", [
                ((0, 8192, [NCH, D]), chunked(self.ple_w_gate[l])),
                ((8192, 2048, [2, D]), chunked(self.ple_w_proj[l])),
            ]))

        def kv_pieces(tag):
            pieces.append((f"{tag}kvA", [((0, 8192, [NCH, D]), chunked(self.w_kvf[:, 0:D]))]))
            pieces.append((f"{tag}kvB", [((0, NCH * (D + NH), [NCH, D + NH]),
                                          chunked(self.w_kvf[:, D:2 * D + NH]))]))

        if self.do_attn and self.n_layers > N_A:
            for l in range(N_A):
                ffn_pieces(l, 1, "p")
                gmlp_pieces(l, "p")
                ffn_pieces(l, 2, "p")
                ple_piece(l, "p")
            kv_pieces("p")
        for l in range(self.n_layers):
            ffn_pieces(l, 1, "")
            if l < N_A:
                gmlp_pieces(l, "")
            elif self.do_attn:
                j = l - N_A
                pieces.append((f"attA_{l}", [((0, 8192, [NCH, D]), chunked(self.att_w_q[j]))]))
                pieces.append((f"attB_{l}", [((0, 8192, [NCH, D]), chunked(self.att_w_o[j]))]))
            ffn_pieces(l, 2, "")
            ple_piece(l, "")
            if l == N_A - 1:
                kv_pieces("")
        self.pieces = pieces
        self.w_cur = 0
        self.w_issued = 0

    def w_view(self, slot, off, n, shp):
        return self.wslot[slot][:, off:off + n].rearrange("p (a b) -> p a b", a=shp[0])

    def w_issue(self):
        i = self.w_issued
        if i >= len(self.pieces):
            return
        slot = i % 2
        name, loads = self.pieces[i]
        for (off, n, shp), src in loads:
            self.cx.dma("pool", self.ch_w[slot], self.w_view(slot, off, n, shp), src,
                        writes=[self.wslot_trk[slot]])
        self.w_issued += 1

    def w_acquire(self, name):
        i = self.w_cur
        assert self.pieces[i][0] == name, (self.pieces[i][0], name)
        assert self.w_issued > i
        self.w_cur += 1
        return i % 2

    def w_release(self):
        self.w_issue()

    def tsl(self, t):
        off, w = self.tiles[t]
        return slice(off, off + w)

    def blocks(self, t):
        off, w = self.tiles[t]
        return [(off + 128 * j, min(128, w - 128 * j)) for j in range((w + 127) // 128)]

    def chan_for(self, trk, kind):
        key = kind + "_" + trk.name
        if key not in self.ch_by_name:
            self.ch_by_name[key] = self.cx.chan(key)
        return self.ch_by_name[key]

    def store(self, out_ap, in_ap, reads, writes=()):
        return self.cx.dma("sp", self.chan_for(reads[0], "st"), out_ap, in_ap, reads=reads, writes=writes)

    def load(self, out_ap, in_ap, writes, reads=(), q="sp"):
        return self.cx.dma(q, self.chan_for(writes[0], "ld"), out_ap, in_ap, reads=reads, writes=writes)

    def setup_consts(self):
        nc, cx = self.nc, self.cx
        ct = self.const_trk
        for dst, src in ((self.ident, self.c_ident), (self.tri, self.c_tri), (self.bdm, self.c_bdm),
                         (self.rep, self.c_rep), (self.su, self.c_su), (self.sel, self.c_sel),
                         (self.iota, self.c_iota)):
            cx.dma("sp", self.ch_const, dst, src, writes=[ct])
        cx.dma("sp", self.ch_const, self.bfB, self.b_f[0].partition_broadcast(128), writes=[ct])
        cx.dma("sp", self.ch_const, self.gkB, self.k_norm[0].partition_broadcast(128), writes=[ct])
        cx.dma("sp", self.ch_const, self.flagB, self.flag[0].partition_broadcast(128), writes=[ct])
        with nc.allow_non_contiguous_dma(reason="256-element gain transpose"):
            cx.dma("sp", self.ch_const, self.gqcol, self.q_norm.rearrange("j d -> d j"), writes=[ct])
        cx.op("dve", lambda: nc.vector.memset(self.ones_mean, 1.0 / D), writes=[ct])
        cx.op("dve", lambda: nc.vector.memset(self.ones_hd, 1.0 / HD), writes=[ct])
        cx.op("dve", lambda: nc.vector.memset(self.ones_bf, 1.0), writes=[ct])
        cx.op("dve", lambda: nc.vector.memset(self.ones_f, 1.0), writes=[ct])
        cx.op("dve", lambda: nc.vector.memset(self.epsb, EPS), writes=[ct])
        cx.op("dve", lambda: nc.vector.memset(self.oneb, 1.0), writes=[ct])
        cx.op("dve", lambda: nc.vector.tensor_copy(out=self.identb, in_=self.ident), reads=[ct], writes=[ct])
        self.phase()
        ptab_b, ptab_trk = self.palloc([128, NSQ * NPG], I32, "ptab_b")
        self.load(ptab_b, self.ptab[0].partition_broadcast(128), writes=[ptab_trk])
        cx.op("dve", lambda: nc.vector.tensor_scalar(out=self.idx_all, in0=ptab_b, scalar1=128, scalar2=self.iota[:, 0:1],
                                                     op0=ALU.mult, op1=ALU.add),
              reads=[ptab_trk, ct], writes=[self.idx_trk])
        rows, rt = self.palloc([17, D], F32, "normrows")
        self.load(rows, self.norms, writes=[rt])
        bt = self.bank_trk[0]
        for c in range(NCH):
            cx.op("pe", lambda: nc.tensor.transpose(self.bank[0][:, c * 17:(c + 1) * 17],
                                                    rows[:, c * 128:(c + 1) * 128], self.ident[0:17, 0:17]),
                  reads=[rt, ct], writes=[bt])
        cx.op("act", lambda: nc.scalar.activation(
            out=self.gcol.rearrange("p (n c) -> p c n", c=NCH),
            in_=self.bank[0][:, 0:NCH * 17].rearrange("p (c n) -> p c n", n=17), func=AF.Copy),
            reads=[bt], writes=[ct])

    def load_x(self, src):
        nc, cx = self.nc, self.cx
        self.phase()
        xr = [self.palloc([128, D], F32, f"xrow{i}") for i in range(2)]
        b = 0
        for t in range(len(self.tiles)):
            for (tok0, bp) in self.blocks(t):
                row, rtrk = xr[b % 2]
                self.load(row[0:bp], src[tok0:tok0 + bp, :], writes=[rtrk])
                for j in range(2):
                    bi = (2 * b + j) % 4
                    for cc in range(4):
                        c = 4 * j + cc
                        cx.op("pe", lambda: nc.tensor.transpose(self.bank[bi][:, cc * 128:cc * 128 + bp],
                                                                row[0:bp, c * 128:(c + 1) * 128],
                                                                self.ident[0:bp, 0:bp]),
                              reads=[rtrk, self.const_trk], writes=[self.bank_trk[bi]])
                    dst = self.h[:, 4 * j:4 * j + 4, tok0:tok0 + bp]
                    srcp = self.bank[bi].rearrange("p (a b) -> p a b", a=4)[:, :, 0:bp]
                    if j == 0:
                        fn = lambda: nc.scalar.activation(out=dst, in_=srcp, func=AF.Copy)
                    else:
                        fn = lambda: nc.vector.tensor_copy(out=dst, in_=srcp)
                    cx.op("act" if j == 0 else "dve", fn, reads=[self.bank_trk[bi]],
                          writes=self.h_trk[t][4 * j:4 * j + 4])
                b += 1

    def store_y(self, dst):
        nc, cx = self.nc, self.cx
        self.phase()
        yr = [self.palloc([128, D], F32, f"yrow{i}") for i in range(2)]
        b = 0
        for t in range(len(self.tiles)):
            for (tok0, bp) in self.blocks(t):
                row, rtrk = yr[b % 2]
                for j in range(2):
                    bi = (2 * b + j) % 4
                    for cc in range(4):
                        c = 4 * j + cc
                        cx.op("pe", lambda: nc.tensor.transpose(self.bank[bi][0:bp, cc * 128:(cc + 1) * 128],
                                                                self.h[:, c, tok0:tok0 + bp], self.ident),
                              reads=[self.h_trk[t][c], self.const_trk], writes=[self.bank_trk[bi]])
                    d_ = row[0:bp, 512 * j:512 * (j + 1)]
                    s_ = self.bank[bi][0:bp, :]
                    if j == 0:
                        fn = lambda: nc.scalar.activation(out=d_, in_=s_, func=AF.Copy)
                    else:
                        fn = lambda: nc.vector.tensor_copy(out=d_, in_=s_)
                    cx.op("act" if j == 0 else "dve", fn, reads=[self.bank_trk[bi]], writes=[rtrk])
                self.store(dst[tok0:tok0 + bp, :], row[0:bp], reads=[rtrk])
                b += 1

    def norm_tile(self, t, gi, xn_dst, xn_trks, tmp):
        nc, cx = self.nc, self.cx
        sq, sq_trk, rs, rs_trk = tmp
        st_bank, st_trk = self.bank[6], self.bank_trk[6]
        ts = self.tsl(t)
        W = self.tiles[t][1]
        for hf in range(2):
            cx.op("act", lambda: nc.scalar.activation(out=sq[hf][:, :, 0:W], in_=self.h[:, 4 * hf:4 * hf + 4, ts],
                                                      func=AF.Square),
                  reads=self.h_trk[t][4 * hf:4 * hf + 4], writes=[sq_trk[hf]])
        cx.mm_group(st_bank[:, 0:W], [(self.ones_mean, sq[c // 4][:, c % 4, 0:W]) for c in range(NCH)],
                    reads=[self.const_trk, sq_trk[0], sq_trk[1]], writes=[st_trk])
        cx.op("act", lambda: nc.scalar.activation(out=rs[:, 0:W], in_=st_bank[:, 0:W], func=AF.Ln,
                                                  bias=self.epsb, scale=1.0),
              reads=[st_trk, self.const_trk], writes=[rs_trk])
        cx.op("act", lambda: nc.scalar.activation(out=rs[:, 0:W], in_=rs[:, 0:W], func=AF.Exp, scale=-0.5),
              reads=[rs_trk], writes=[rs_trk])
        for c in range(NCH):
            cx.op("dve", lambda: nc.vector.scalar_tensor_tensor(
                out=xn_dst[:, c, 0:W], in0=self.h[:, c, ts], scalar=self.gcol[:, gi * NCH + c:gi * NCH + c + 1],
                in1=rs[:, 0:W], op0=ALU.mult, op1=ALU.mult),
                reads=[self.h_trk[t][c], self.const_trk, rs_trk], writes=[xn_trks[c]])

    def norm_tmp(self):
        sq, sq_trk = [], []
        for i in range(2):
            a, tk = self.palloc([128, 4, TILE], BF16, f"sq{i}")
            sq.append(a)
            sq_trk.append(tk)
        rs, rs_trk = self.palloc([128, TILE], F32, "rstd")
        return (sq, sq_trk, rs, rs_trk)

    def bg_alloc(self):
        self.bg_bufs = None
        if self.bg_nC >= self.bg_total:
            return
        kbf = [self.palloc([128, D], BF16, f"bgk{i}") for i in range(BGD)]
        vbf = [self.palloc([128, D], BF16, f"bgv{i}") for i in range(BGD)]
        ktp = [self.palloc([128, D], BF16, f"bgt{i}") for i in range(2)]
        self.bg_bufs = (kbf, vbf, ktp)

    def bg_A(self, p):
        cx = self.cx
        kbf, vbf, ktp = self.bg_bufs
        i, pg = divmod(p, NPG)
        kb, kb_trk = kbf[p % BGD]
        vb, vb_trk = vbf[p % BGD]
        ix = self.idx_all[:, p:p + 1]
        cx.idma(self.chan_for(kb_trk, "ld"), kb, self.cache_k, ix, reads=[self.idx_trk], writes=[kb_trk])
        cx.idma(self.chan_for(vb_trk, "ld"), vb, self.cache_v, ix, reads=[self.idx_trk], writes=[vb_trk])
        cx.idma(self.chan_for(self.tail_trk, "ld"), self.tail_all[:, i, pg, :], self.cache_lf, ix,
                reads=[self.idx_trk], writes=[self.tail_trk])

    def bg_C(self, p):
        nc, cx = self.nc, self.cx
        kbf, vbf, ktp = self.bg_bufs
        kb, kb_trk = kbf[p % BGD]
        vb, vb_trk = vbf[p % BGD]
        kp, kp_trk = ktp[p % 2]
        ptb = self.bank[7].bitcast(BF16)
        ptb_trk = self.bank_trk[7]
        for hh in range(NH):
            cx.op("pe", lambda: nc.tensor.transpose(ptb[:, hh * 128:(hh + 1) * 128], kb[:, hh * 128:(hh + 1) * 128],
                                                    self.identb),
                  reads=[kb_trk, self.const_trk], writes=[ptb_trk])
        if p % 2 == 0:
            cx.op("act", lambda: nc.scalar.activation(out=kp, in_=ptb, func=AF.Copy), reads=[ptb_trk], writes=[kp_trk])
        else:
            cx.op("dve", lambda: nc.vector.tensor_copy(out=kp, in_=ptb), reads=[ptb_trk], writes=[kp_trk])
        self.store(self.KTpast[p], kp, reads=[kp_trk])
        self.store(self.Vpast[p], vb, reads=[vb_trk])

    def bg_hook(self):
        if not self.bg_bufs:
            return
        if self.bg_nA - self.bg_nC >= BGD - 1 or (self.bg_nA >= self.bg_total and self.bg_nC < self.bg_nA):
            self.bg_C(self.bg_nC)
            self.bg_nC += 1
        if self.bg_nA < self.bg_total and self.bg_nA - self.bg_nC < BGD:
            self.bg_A(self.bg_nA)
            self.bg_nA += 1

    def bg_flush(self):
        if not self.bg_bufs:
            return
        while self.bg_nC < self.bg_nA:
            self.bg_C(self.bg_nC)
            self.bg_nC += 1
        self.bg_bufs = None

    def ffn(self, l, which, tag=""):
        nc, cx = self.nc, self.cx
        self.phase()
        ntl = len(self.tiles)
        ntok = self.tiles[-1][0] + self.tiles[-1][1]
        gi = 4 * l + (0 if which == 1 else 2)
        xn, _ = self.palloc([128, NCH, ntok], BF16, "xn_all")
        xn_trk = [[self.ptrk(f"xn{t}_{c}") for c in range(NCH)] for t in range(ntl)]
        tmp = self.norm_tmp()
        hmid = [self.palloc([128, 4, TILE], BF16, f"hmid{i}") for i in range(2)]
        sa = [self.palloc([128, TILE], F32, f"sa{i}") for i in range(2)]
        self.bg_alloc()
        self.norm_tile(0, gi, xn[:, :, self.tsl(0)], xn_trk[0], tmp)
        PA = [(self.bank[0], self.bank_trk[0]), (self.bank[1], self.bank_trk[1])]
        PB = [(self.bank[2], self.bank_trk[2]), (self.bank[3], self.bank_trk[3])]
        PY = [(self.bank[4], self.bank_trk[4]), (self.bank[5], self.bank_trk[5])]
        steps = [(g, t) for g in range(len(FFN_GROUPS)) for t in range(ntl)]
        state = {"k": 0, "j": 0, "slot": {}}

        def emit_ab(i):
            g, t = steps[i]
            f0, gc = FFN_GROUPS[g]
            if t == 0:
                state["slot"][g] = self.w_acquire(f"{tag}ffn{which}_{l}_{f0}")
            slot = state["slot"][g]
            strk = self.wslot_trk[slot]
            w1g = self.w_view(slot, 0, NCH * gc * 128, [NCH, gc * 128])
            w3g = self.w_view(slot, 4096, NCH * gc * 128, [NCH, gc * 128])
            hm, hm_trk = hmid[i % 2]
            ts = self.tsl(t)
            W = self.tiles[t][1]
            for fc in range(gc):
                k = state["k"]
                state["k"] += 1
                (a_ps, a_trk), (b_ps, b_trk) = PA[k % 2], PB[k % 2]
                s_sb, s_trk = sa[k % 2]
                fs = slice(fc * 128, (fc + 1) * 128)
                cx.mm_group(a_ps[:, 0:W], [(w1g[:, c, fs], xn[:, c, ts]) for c in range(NCH)],
                            reads=[strk] + xn_trk[t], writes=[a_trk])
                cx.mm_group(b_ps[:, 0:W], [(w3g[:, c, fs], xn[:, c, ts]) for c in range(NCH)],
                            reads=[strk] + xn_trk[t], writes=[b_trk])
                cx.op("act", lambda: nc.scalar.activation(out=s_sb[:, 0:W], in_=a_ps[:, 0:W], func=AF.Silu),
                      reads=[a_trk], writes=[s_trk])
                cx.op("dve", lambda: nc.vector.tensor_tensor(out=hm[:, fc, 0:W], in0=s_sb[:, 0:W], in1=b_ps[:, 0:W],
                                                             op=ALU.mult),
                      reads=[s_trk, b_trk], writes=[hm_trk])

        def emit_y(i):
            g, t = steps[i]
            f0, gc = FFN_GROUPS[g]
            slot = state["slot"][g]
            strk = self.wslot_trk[slot]
            w2g = self.w_view(slot, 8192, gc * D, [gc, D])
            hm, hm_trk = hmid[i % 2]
            ts = self.tsl(t)
            W = self.tiles[t][1]
            for c in range(NCH):
                j = state["j"]
                state["j"] += 1
                y_ps, y_trk = PY[j % 2]
                cx.mm_group(y_ps[:, 0:W], [(w2g[:, fc, c * 128:(c + 1) * 128], hm[:, fc, 0:W]) for fc in range(gc)],
                            reads=[strk, hm_trk], writes=[y_trk])
                cx.op("dve", lambda: nc.vector.scalar_tensor_tensor(
                    out=self.h[:, c, ts], in0=y_ps[:, 0:W], scalar=0.5, in1=self.h[:, c, ts],
                    op0=ALU.mult, op1=ALU.add),
                    reads=[y_trk, self.h_trk[t][c]], writes=[self.h_trk[t][c]])
            if t == ntl - 1:
                self.w_release()

        emit_ab(0)
        for i in range(len(steps)):
            if i + 1 < len(steps):
                g1, t1 = steps[i + 1]
                if g1 == 0:
                    self.norm_tile(t1, gi, xn[:, :, self.tsl(t1)], xn_trk[t1], tmp)
                emit_ab(i + 1)
            self.bg_hook()
            emit_y(i)
            self.bg_hook()
        self.bg_flush()

    def gmlp(self, l, tag="", emit_gv=True):
        nc, cx = self.nc, self.cx
        self.phase()
        gi = 4 * l + 1
        ct = self.const_trk
        xt, _ = self.palloc([128, NCH, TILE], BF16, "xn_t0")
        xtr = [self.ptrk() for c in range(NCH)]
        uT, uT_trk = self.palloc([128, NCH, TILE], BF16, "uT")
        vn, _ = self.palloc([128, 4, D], BF16, "vn")
        vn_trk = [self.ptrk() for _ in range(4)]
        us, us_trk = self.palloc([128, NCH, TILE], BF16, "us")
        tmp = self.norm_tmp()
        mixp, mixp_trk = self.palloc([128, 8, 128], BF16, "mixp")
        mixs, mixs_trk = self.palloc([128, 8, 128], BF16, "mixs")
        bsp, bsp_trk = self.palloc([128, 8, 128], F32, "bsp")
        bss, bss_trk = self.palloc([128, 8, 128], F32, "bss")
        gvB, gvB_trk = self.palloc([128, D], F32, "gvB")
        wsr, wsr_trk = self.palloc([128, 8, 128], F32, "wsr")
        x44, x44_trk = self.palloc([128, 8, 4], F32, "x44")
        t1s, t1s_trk = self.palloc([128, 8, 4], F32, "t1s")
        vf, vf_trk = self.palloc([128, D], F32, "vnf0")
        junk, junk_trk = self.palloc([128, D], BF16, "junk")
        ssv, ssv_trk = self.palloc([128, 2], F32, "ssv")
        t1 = [self.palloc([128, TILE], F32, f"t1_{i}") for i in range(2)]

        self.load(wsr, self.gmlp_w_s[l].rearrange("g t s -> t g s"), writes=[wsr_trk])
        self.load(bsp, self.gmlp_b_s[l].partition_broadcast(128), writes=[bsp_trk])
        self.load(gvB, self.gmlp_v_norm[l].partition_broadcast(128), writes=[gvB_trk])
        for g in range(8):
            bi = g % 2
            cx.op("pe", lambda: nc.tensor.transpose(self.bank[bi][:, 0:128], wsr[:, g, :], self.ident),
                  reads=[wsr_trk, ct], writes=[self.bank_trk[bi]])
            cx.op("dve", lambda: nc.vector.tensor_tensor(out=mixp[:, g, :], in0=self.bank[bi][:, 0:128],
                                                         in1=self.tri, op=ALU.mult),
                  reads=[self.bank_trk[bi], ct], writes=[mixp_trk])
            cx.op("dve", lambda: nc.vector.tensor_copy(out=x44[:, g, :], in_=self.bank[bi][:, 0:4]),
                  reads=[self.bank_trk[bi]], writes=[x44_trk])
        cx.mm_group(self.bank[2][:, 0:32], [(self.rep, x44.rearrange("a g t -> a (g t)"))],
                    reads=[ct, x44_trk], writes=[self.bank_trk[2]])
        cx.op("dve", lambda: nc.vector.tensor_copy(out=t1s.rearrange("p g t -> p (g t)"), in_=self.bank[2][:, 0:32]),
              reads=[self.bank_trk[2]], writes=[t1s_trk])
        for g in range(8):
            cx.op("dve", lambda: nc.vector.tensor_tensor(
                out=mixs[:, g, :].rearrange("p (a b) -> p a b", b=4),
                in0=t1s[:, g:g + 1, :].broadcast_to([128, 32, 4]),
                in1=self.bdm.rearrange("p (a b) -> p a b", b=4), op=ALU.mult),
                reads=[t1s_trk, ct], writes=[mixs_trk])
        cx.op("dve", lambda: nc.vector.tensor_copy(
            out=bss.rearrange("p g (a b) -> p g a b", b=4),
            in_=bsp[:, :, 0:4].unsqueeze(2).broadcast_to([128, 8, 32, 4])),
            reads=[bsp_trk], writes=[bss_trk])

        slotA = self.w_acquire(f"{tag}gmlpA_{l}")
        slotB = self.w_acquire(f"{tag}gmlpB_{l}")
        sA, sB = self.wslot_trk[slotA], self.wslot_trk[slotB]
        Wu = self.w_view(slotA, 0, 8192, [NCH, D])
        WvA = self.w_view(slotA, 8192, 4096, [4, D])
        WvB = self.w_view(slotB, 0, 4096, [4, D])
        Wo = self.w_view(slotB, 4096, 8192, [NCH, D])
        kk = 0
        for t in range(len(self.tiles)):
            off, W = self.tiles[t]
            sample = (off >= NPT)
            ts = self.tsl(t)
            self.norm_tile(t, gi, xt, xtr, tmp)
            for fc in range(NCH):
                bi = fc % 2
                cx.mm_group(self.bank[bi][:, 0:W], [(Wu[:, c, fc * 128:(fc + 1) * 128], xt[:, c, 0:W]) for c in range(NCH)],
                            reads=[sA] + xtr, writes=[self.bank_trk[bi]])
                cx.op("act", lambda: nc.scalar.activation(out=uT[:, fc, 0:W], in_=self.bank[bi][:, 0:W], func=AF.Copy),
                      reads=[self.bank_trk[bi]], writes=[uT_trk])
            blks = self.blocks(t)
            for blk, (tok0, bp) in enumerate(blks):
                pv = self.pp[1 + (blk % 2)]
                pv_trk = [self.bank_trk[2 + 2 * (blk % 2)], self.bank_trk[3 + 2 * (blk % 2)]]
                bs = slice(blk * 128, blk * 128 + bp)
                for hf in range(2):
                    fs = slice(hf * 512, (hf + 1) * 512)
                    cx.mm_group(pv[0:bp, fs], [(xt[:, c, bs], (WvA if c < 4 else WvB)[:, c % 4, fs]) for c in range(NCH)],
                                reads=[sA, sB] + xtr, writes=[pv_trk[hf]])
                cx.op("act", lambda: nc.scalar.activation(out=junk[0:bp], in_=pv[0:bp], func=AF.Square,
                                                          accum_out=ssv[0:bp, 0:1]),
                      reads=pv_trk, writes=[junk_trk, ssv_trk])
                cx.op("act", lambda: nc.scalar.activation(out=ssv[0:bp, 1:2], in_=ssv[0:bp, 0:1], func=AF.Ln,
                                                          bias=self.epsb[0:bp], scale=1.0 / D),
                      reads=[ssv_trk, ct], writes=[ssv_trk])
                cx.op("act", lambda: nc.scalar.activation(out=ssv[0:bp, 1:2], in_=ssv[0:bp, 1:2], func=AF.Exp, scale=-0.5),
                      reads=[ssv_trk], writes=[ssv_trk])
                if sample:
                    cx.op("dve", lambda: nc.vector.scalar_tensor_tensor(
                        out=vf[0:bp], in0=pv[0:bp], scalar=ssv[0:bp, 1:2], in1=gvB[0:bp], op0=ALU.mult, op1=ALU.mult),
                        reads=pv_trk + [ssv_trk, gvB_trk], writes=[vf_trk])
                    if emit_gv:
                        self.store(self.gv_out[l, tok0 - NPT:tok0 - NPT + bp, :], vf[0:bp], reads=[vf_trk])
                    cx.op("dve", lambda: nc.vector.tensor_copy(out=vn[0:bp, blk, :], in_=vf[0:bp]),
                          reads=[vf_trk], writes=[vn_trk[blk]])
                else:
                    cx.op("dve", lambda: nc.vector.scalar_tensor_tensor(
                        out=vn[0:bp, blk, :], in0=pv[0:bp], scalar=ssv[0:bp, 1:2], in1=gvB[0:bp], op0=ALU.mult, op1=ALU.mult),
                        reads=pv_trk + [ssv_trk, gvB_trk], writes=[vn_trk[blk]])
            mix, mix_trk = (mixs, mixs_trk) if sample else (mixp, mixp_trk)
            bsB, bsB_trk = (bss, bss_trk) if sample else (bsp, bsp_trk)
            nb = len(blks)
            for g in range(8):
                bi = g % 2
                for blk, (tok0, bp) in enumerate(blks):
                    cx.mm_group(self.bank[bi][:, blk * 128:blk * 128 + bp],
                                [(vn[0:bp, blk, g * 128:(g + 1) * 128], mix[0:bp, g, 0:bp])],
                                reads=[vn_trk[blk], mix_trk], writes=[self.bank_trk[bi]])
                tt, tt_trk = t1[kk % 2]
                kk += 1
                if nb == 4:
                    cx.op("dve", lambda: nc.vector.tensor_tensor(
                        out=tt.rearrange("p (a b) -> p a b", a=4),
                        in0=self.bank[bi].rearrange("p (a b) -> p a b", a=4),
                        in1=bsB[:, g:g + 1, :].broadcast_to([128, 4, 128]), op=ALU.add),
                        reads=[self.bank_trk[bi], bsB_trk], writes=[tt_trk])
                else:
                    cx.op("dve", lambda: nc.vector.tensor_tensor(
                        out=tt[:, 0:W], in0=self.bank[bi][:, 0:W], in1=bsB[:, g, 0:W], op=ALU.add),
                        reads=[self.bank_trk[bi], bsB_trk], writes=[tt_trk])
                cx.op("dve", lambda: nc.vector.tensor_tensor(out=us[:, g, 0:W], in0=tt[:, 0:W], in1=uT[:, g, 0:W],
                                                             op=ALU.mult),
                      reads=[tt_trk, uT_trk], writes=[us_trk])
            for fc in range(NCH):
                bi = 6 + fc % 2
                cx.mm_group(self.bank[bi][:, 0:W], [(Wo[:, g, fc * 128:(fc + 1) * 128], us[:, g, 0:W]) for g in range(NCH)],
                            reads=[sB, us_trk], writes=[self.bank_trk[bi]])
                cx.op("dve", lambda: nc.vector.tensor_tensor(out=self.h[:, fc, ts], in0=self.bank[bi][:, 0:W],
                                                             in1=self.h[:, fc, ts], op=ALU.add),
                      reads=[self.bank_trk[bi], self.h_trk[t][fc]], writes=[self.h_trk[t][fc]])
        self.w_release()
        self.w_release()

    def ple(self, l, pe_src, tag=""):
        nc, cx = self.nc, self.cx
        self.phase()
        gi = 4 * l + 3
        ct = self.const_trk
        xn = [self.palloc([128, NCH, TILE], BF16, f"xn_t{i}") for i in range(2)]
        xn_trks = [[self.ptrk() for c in range(NCH)] for i in range(2)]
        tmp = self.norm_tmp()
        peT = [self.palloc([128, 2, TILE], BF16, f"peT{i}") for i in range(2)]
        perow = [self.palloc([128, PLE], F32, f"perow{i}") for i in range(2)]
        gs = [self.palloc([128, TILE], F32, f"gs{i}") for i in range(2)]
        tm = [self.palloc([128, TILE], F32, f"tm{i}") for i in range(2)]
        slot = self.w_acquire(f"{tag}ple_{l}")
        strk = self.wslot_trk[slot]
        Wg = self.w_view(slot, 0, 8192, [NCH, D])
        Wp = self.w_view(slot, 8192, 2048, [2, D])
        kk = 0
        for t in range(len(self.tiles)):
            ts = self.tsl(t)
            W = self.tiles[t][1]
            pT, pT_trk = peT[t % 2]
            for blk, (tok0, bp) in enumerate(self.blocks(t)):
                pr, pr_trk = perow[blk % 2]
                self.load(pr[0:bp], pe_src[l, tok0:tok0 + bp, :], writes=[pr_trk])
                for c2 in range(2):
                    cx.op("pe", lambda: nc.tensor.transpose(self.bank[4 + c2][:, blk * 128:blk * 128 + bp],
                                                            pr[0:bp, c2 * 128:(c2 + 1) * 128], self.ident[0:bp, 0:bp]),
                          reads=[pr_trk, ct], writes=[self.bank_trk[4 + c2]])
            for c2 in range(2):
                cx.op("act", lambda: nc.scalar.activation(out=pT[:, c2, 0:W], in_=self.bank[4 + c2][:, 0:W], func=AF.Copy),
                      reads=[self.bank_trk[4 + c2]], writes=[pT_trk])
            xt, _ = xn[t % 2]
            xtr = xn_trks[t % 2]
            self.norm_tile(t, gi, xt, xtr, tmp)
            for fc in range(NCH):
                bg, bp_ = fc % 2, 2 + fc % 2
                fs = slice(fc * 128, (fc + 1) * 128)
                cx.mm_group(self.bank[bg][:, 0:W], [(Wg[:, c, fs], xt[:, c, 0:W]) for c in range(NCH)],
                            reads=[strk] + xtr, writes=[self.bank_trk[bg]])
                cx.mm_group(self.bank[bp_][:, 0:W], [(Wp[:, c2, fs], pT[:, c2, 0:W]) for c2 in range(2)],
                            reads=[strk, pT_trk], writes=[self.bank_trk[bp_]])
                g_sb, g_trk = gs[kk % 2]
                t_sb, t_trk = tm[kk % 2]
                kk += 1
                cx.op("act", lambda: nc.scalar.activation(out=g_sb[:, 0:W], in_=self.bank[bg][:, 0:W], func=AF.Sigmoid),
                      reads=[self.bank_trk[bg]], writes=[g_trk])
                cx.op("dve", lambda: nc.vector.tensor_tensor(out=t_sb[:, 0:W], in0=g_sb[:, 0:W], in1=self.bank[bp_][:, 0:W],
                                                             op=ALU.mult),
                      reads=[g_trk, self.bank_trk[bp_]], writes=[t_trk])
                cx.op("dve", lambda: nc.vector.tensor_tensor(out=self.h[:, fc, ts], in0=t_sb[:, 0:W], in1=self.h[:, fc, ts],
                                                             op=ALU.add),
                      reads=[t_trk, self.h_trk[t][fc]], writes=[self.h_trk[t][fc]])
        self.w_release()

    def shared_kv(self, prev, tag=""):
        nc, cx = self.nc, self.cx
        self.phase()
        gi = 16
        ct = self.const_trk
        xn = [self.palloc([128, NCH, TILE], BF16, f"xs_t{i}") for i in range(2)]
        xn_trks = [[self.ptrk() for c in range(NCH)] for i in range(2)]
        tmp = self.norm_tmp()
        kf = [self.palloc([128, D], F32, f"kf{i}") for i in range(2)]
        vf = [self.palloc([128, D], F32, f"vf{i}") for i in range(2)]
        kb = [self.palloc([128, D], BF16, f"kb{i}") for i in range(2)]
        vb = [self.palloc([128, D], BF16, f"vb{i}") for i in range(2)]
        kts = [self.palloc([128, NH, 128], BF16, f"kts{i}") for i in range(2)]
        sqj, sqj_trk = self.palloc([128, D], F32, "sqj")
        ss, ss_trk = self.palloc([128, 2 * NH], F32, "ss")
        fx, fx_trk = self.palloc([128, NH], F32, "fx")
        slotA = self.w_acquire(f"{tag}kvA")
        slotB = self.w_acquire(f"{tag}kvB")
        sA, sB = self.wslot_trk[slotA], self.wslot_trk[slotB]
        Wk = self.w_view(slotA, 0, 8192, [NCH, D])
        Wvf = self.w_view(slotB, 0, NCH * (D + NH), [NCH, D + NH])
        KT_dst = self.KTprev if prev else self.KTown
        V_dst = self.Vprev if prev else self.Vown
        lf_dst = self.lf_prev if prev else self.lf_all
        lf_trks = self.lfp_trk if prev else self.lf_trk
        ptb = self.bank[7].bitcast(BF16)
        b = 0
        for t in range(len(self.tiles)):
            xt, _ = xn[t % 2]
            xtr = xn_trks[t % 2]
            self.norm_tile(t, gi, xt, xtr, tmp)
            for blk, (tok0, bp) in enumerate(self.blocks(t)):
                bs = slice(blk * 128, blk * 128 + bp)
                pk, pk_trk = self.pp[0], self.bank_trk[0:2]
                pv, pv_trk = self.pp[1], self.bank_trk[2:4]
                pf, pf_trk = self.bank[4 + (b % 2)], self.bank_trk[4 + (b % 2)]
                for hf in range(2):
                    fs = slice(hf * 512, (hf + 1) * 512)
                    cx.mm_group(pk[0:bp, fs], [(xt[:, c, bs], Wk[:, c, fs]) for c in range(NCH)],
                                reads=[sA] + xtr, writes=[pk_trk[hf]])
                for hf in range(2):
                    fs = slice(hf * 512, (hf + 1) * 512)
                    cx.mm_group(pv[0:bp, fs], [(xt[:, c, bs], Wvf[:, c, fs]) for c in range(NCH)],
                                reads=[sB] + xtr, writes=[pv_trk[hf]])
                cx.mm_group(pf[0:bp, 0:NH], [(xt[:, c, bs], Wvf[:, c, D:D + NH]) for c in range(NCH)],
                            reads=[sB] + xtr, writes=[pf_trk])
                k_sb, k_trk = kf[b % 2]
                v_sb, v_trk = vf[b % 2]
                kb_sb, kb_trk = kb[b % 2]
                vb_sb, vb_trk = vb[b % 2]
                kt_sb, kt_trk = kts[b % 2]
                k_sb, v_sb, kb_sb, vb_sb = k_sb[0:bp], v_sb[0:bp], kb_sb[0:bp], vb_sb[0:bp]
                cx.op("act", lambda: nc.scalar.activation(out=k_sb, in_=pk[0:bp], func=AF.Copy),
                      reads=pk_trk, writes=[k_trk])
                cx.op("dve", lambda: nc.vector.tensor_tensor(out=sqj[0:bp], in0=k_sb, in1=k_sb, op=ALU.mult),
                      reads=[k_trk], writes=[sqj_trk])
                cx.op("dve", lambda: nc.vector.tensor_reduce(out=ss[0:bp, 0:NH],
                                                             in_=sqj[0:bp].rearrange("p (h d) -> p h d", h=NH),
                                                             axis=AX.X, op=ALU.add),
                      reads=[sqj_trk], writes=[ss_trk])
                cx.op("act", lambda: nc.scalar.activation(out=ss[0:bp, NH:2 * NH], in_=ss[0:bp, 0:NH], func=AF.Ln,
                                                          bias=self.epsb[0:bp], scale=1.0 / HD),
                      reads=[ss_trk, ct], writes=[ss_trk])
                cx.op("act", lambda: nc.scalar.activation(out=ss[0:bp, NH:2 * NH], in_=ss[0:bp, NH:2 * NH], func=AF.Exp,
                                                          scale=-0.5),
                      reads=[ss_trk], writes=[ss_trk])
                k3 = k_sb.rearrange("p (h d) -> p h d", h=NH)
                cx.op("dve", lambda: nc.vector.tensor_tensor(
                    out=k3, in0=k3, in1=ss[0:bp, NH:2 * NH].unsqueeze(2).broadcast_to([bp, NH, HD]), op=ALU.mult),
                    reads=[k_trk, ss_trk], writes=[k_trk])
                cx.op("dve", lambda: nc.vector.tensor_tensor(
                    out=k3, in0=k3, in1=self.gkB[0:bp].unsqueeze(1).broadcast_to([bp, NH, HD]), op=ALU.mult),
                    reads=[k_trk, ct], writes=[k_trk])
                if not prev:
                    self.store(self.k_out[tok0:tok0 + bp, :], k_sb, reads=[k_trk])
                cx.op("act", lambda: nc.scalar.activation(out=kb_sb, in_=k_sb, func=AF.Copy),
                      reads=[k_trk], writes=[kb_trk])
                for hh in range(NH):
                    cx.op("pe", lambda: nc.tensor.transpose(ptb[:, hh * 128:hh * 128 + bp],
                                                            kb_sb[:, hh * 128:(hh + 1) * 128], self.identb[0:bp, 0:bp]),
                          reads=[kb_trk, ct], writes=[self.bank_trk[7]])
                cx.op("dve", lambda: nc.vector.tensor_copy(
                    out=kt_sb[:, :, 0:bp], in_=ptb.rearrange("p (h t) -> p h t", h=NH)[:, :, 0:bp]),
                    reads=[self.bank_trk[7]], writes=[kt_trk])
                self.store(KT_dst.rearrange("h d t -> d h t")[:, :, tok0:tok0 + bp], kt_sb[:, :, 0:bp], reads=[kt_trk])
                cx.op("act", lambda: nc.scalar.activation(out=v_sb, in_=pv[0:bp], func=AF.Copy),
                      reads=pv_trk, writes=[v_trk])
                if not prev:
                    self.store(self.v_out[tok0:tok0 + bp, :], v_sb, reads=[v_trk])
                cx.op("dve", lambda: nc.vector.tensor_copy(out=vb_sb, in_=v_sb), reads=[v_trk], writes=[vb_trk])
                self.store(V_dst[tok0:tok0 + bp, :], vb_sb, reads=[vb_trk])
                lf = lf_dst[0:bp, b, :]
                cx.op("dve", lambda: nc.vector.tensor_tensor(out=fx[0:bp], in0=pf[0:bp, 0:NH], in1=self.bfB[0:bp], op=ALU.add),
                      reads=[pf_trk, ct], writes=[fx_trk])
                cx.op("act", lambda: nc.scalar.activation(out=fx[0:bp], in_=fx[0:bp], func=AF.Exp, scale=-1.0),
                      reads=[fx_trk], writes=[fx_trk])
                cx.op("act", lambda: nc.scalar.activation(out=fx[0:bp], in_=fx[0:bp], func=AF.Ln, bias=self.oneb[0:bp],
                                                          scale=1.0),
                      reads=[fx_trk, ct], writes=[fx_trk])
                cx.op("dve", lambda: nc.vector.tensor_scalar(out=lf, in0=fx[0:bp], scalar1=-1.0, scalar2=None, op0=ALU.mult),
                      reads=[fx_trk], writes=[lf_trks[b]])
                if not prev:
                    self.store(self.lf_out[tok0:tok0 + bp, :], lf, reads=[lf_trks[b]])
                b += 1
        self.w_release()
        self.w_release()

    def prep_forget(self):
        nc, cx = self.nc, self.cx
        ct = self.const_trk
        self.phase()
        tot, tot_trk = self.palloc([128, NH], F32, "tot")
        cn, cn_trk = self.palloc([128, NH], F32, "cn")
        b0, b1, b2 = self.bank[0], self.bank[1], self.bank[2]
        t0, t1, t2 = self.bank_trk[0], self.bank_trk[1], self.bank_trk[2]
        for i in range(16):
            pairs = [(self.tri, self.lf_all[:, i, :])] + [(self.ones_f, self.lf_all[:, j, :]) for j in range(i)]
            cx.mm_group(b0[:, i * NH:(i + 1) * NH], pairs, reads=[ct] + self.lf_trk[0:i + 1], writes=[t0])
        cx.op("dve", lambda: nc.vector.tensor_copy(out=self.csum_own.rearrange("p b h -> p (b h)"), in_=b0[:, 0:16 * NH]),
              reads=[t0], writes=[self.cs_trk])
        for i in range(16):
            pairs = [(self.tri, self.lf_prev[:, i, :])] + [(self.ones_f, self.lf_prev[:, j, :]) for j in range(i)]
            cx.mm_group(b1[:, i * NH:(i + 1) * NH], pairs, reads=[ct] + self.lfp_trk[0:i + 1], writes=[t1])
        cx.mm_group(b2[:, 0:NH], [(self.ones_f, self.lf_prev[:, j, :]) for j in range(16)],
                    reads=[ct] + self.lfp_trk, writes=[t2])
        cx.op("dve", lambda: nc.vector.tensor_scalar(out=tot, in0=b2[:, 0:NH], scalar1=self.flagB[:, 0:1], scalar2=None,
                                                     op0=ALU.add),
              reads=[t2, ct], writes=[tot_trk])
        cx.op("dve", lambda: nc.vector.scalar_tensor_tensor(
            out=self.tail_prev, in0=b1[:, 0:16 * NH].rearrange("p (b h) -> p b h", h=NH), scalar=-1.0,
            in1=tot.unsqueeze(1).broadcast_to([128, 16, NH]), op0=ALU.mult, op1=ALU.add),
            reads=[t1, tot_trk], writes=[self.cs_trk])
        for t in range(4):
            cx.mm_group(b2[:, 64 + t * NH:64 + (t + 1) * NH], [(self.sel, self.csum_own[:, 4 * t + 1, :])],
                        reads=[ct, self.cs_trk], writes=[t2])
        cx.op("dve", lambda: nc.vector.tensor_copy(out=self.Rt.rearrange("p t h -> p (t h)"), in_=b2[:, 64:64 + 4 * NH]),
              reads=[t2], writes=[self.cs_trk])
        cx.mm_group(b2[0:NST, 128:128 + NH], [(self.bdm[0:NST, 0:NST], self.lf_all[0:NST, 16, :])],
                    reads=[ct, self.lf_trk[16]], writes=[t2])
        cx.op("dve", lambda: nc.vector.tensor_scalar(out=self.ncn[0:NST], in0=b2[0:NST, 128:128 + NH], scalar1=-1.0,
                                                     scalar2=None, op0=ALU.mult),
              reads=[t2], writes=[self.cs_trk])

    def prep_past(self):
        nc, cx = self.nc, self.cx
        ct = self.const_trk
        self.phase()
        assert self.bg_nC >= self.bg_total
        tots, tots_trk = self.palloc([128, NPG, NH], F32, "tots")
        suf, suf_trk = self.palloc([128, NPG, NH], F32, "suf")
        for i in range(NSQ):
            lf2 = self.tail_all[:, i].rearrange("p g h -> p (g h)")
            cx.mm_group(self.bank[0][:, 0:128], [(self.su, lf2)], reads=[ct, self.tail_trk], writes=[self.bank_trk[0]])
            cx.mm_group(self.bank[1][:, 0:128], [(self.ones_f, lf2)], reads=[ct, self.tail_trk], writes=[self.bank_trk[1]])
            cx.op("act", lambda: nc.scalar.activation(out=tots.rearrange("p g h -> p (g h)"), in_=self.bank[1][:, 0:128],
                                                      func=AF.Copy),
                  reads=[self.bank_trk[1]], writes=[tots_trk])
            cx.op("dve", lambda: nc.vector.memset(suf[:, NPG - 1, :], 0.0), writes=[suf_trk])
            for pg in range(NPG - 2, -1, -1):
                cx.op("dve", lambda: nc.vector.tensor_tensor(out=suf[:, pg, :], in0=suf[:, pg + 1, :], in1=tots[:, pg + 1, :],
                                                             op=ALU.add),
                      reads=[suf_trk, tots_trk], writes=[suf_trk])
            cx.op("dve", lambda: nc.vector.tensor_tensor(
                out=lf2, in0=self.bank[0][:, 0:128], in1=suf.rearrange("p g h -> p (g h)"), op=ALU.add),
                reads=[self.bank_trk[0], self.bank_trk[1], suf_trk], writes=[self.tail_trk])

    def attention(self, l):
        nc, cx = self.nc, self.cx
        self.phase()
        j = l - N_A
        gi = 4 * l + 1
        ct = self.const_trk
        QT, qt_trk = self.palloc([128, NH, TILE], BF16, "QT")
        OT, ot_trk = self.palloc([128, NH, TILE], BF16, "OT")
        recip, recip_trk = self.palloc([128, TILE], F32, "recip")
        self._att_mark = self.RT.off
        xt, _ = self.palloc([128, NCH, TILE], BF16, "xn_t0")
        xtr = [self.ptrk() for c in range(NCH)]
        tmp = self.norm_tmp()
        sq, sq_trk, rs, rs_trk = tmp
        KTb = [self.palloc([128, 4096], BF16, f"KTb{i}") for i in range(2)]
        Vb = [self.palloc([128, 32, HD], BF16, f"Vb{i}") for i in range(2)]
        pts = [self.palloc([128, TILE], BF16, f"pt{i}") for i in range(4)]
        bias_t, bias_trk = self.palloc([128, 32, NH], F32, "bias_t")
        slotA = self.w_acquire(f"attA_{l}")
        slotB = self.w_acquire(f"attB_{l}")
        sA, sB = self.wslot_trk[slotA], self.wslot_trk[slotB]
        Wq = self.w_view(slotA, 0, 8192, [NCH, D])
        Wo = self.w_view(slotB, 0, 8192, [NCH, D])
        PS = [(self.bank[0], self.bank_trk[0]), (self.bank[1], self.bank_trk[1]), (self.bank[7], self.bank_trk[7])]
        POs = [(self.bank[2], self.bank_trk[2]), (self.bank[4], self.bank_trk[4])]
        PDs = [(self.bank[3], self.bank_trk[3]), (self.bank[5], self.bank_trk[5])]

        def q_proj(t):
            W = self.tiles[t][1]
            self.norm_tile(t, gi, xt, xtr, tmp)
            for hh in range(NH):
                bi = hh % 2
                qp, qp_trk = self.bank[bi], self.bank_trk[bi]
                cx.mm_group(qp[:, 0:W], [(Wq[:, c, hh * 128:(hh + 1) * 128], xt[:, c, 0:W]) for c in range(NCH)],
                            reads=[sA] + xtr, writes=[qp_trk])
                sqh = sq[hh % 2][:, 0, 0:W]
                cx.op("act", lambda: nc.scalar.activation(out=sqh, in_=qp[:, 0:W], func=AF.Square),
                      reads=[qp_trk], writes=[sq_trk[hh % 2]])
                cx.mm_group(self.bank[6][:, 0:W], [(self.ones_hd, sqh)], reads=[ct, sq_trk[hh % 2]],
                            writes=[self.bank_trk[6]])
                cx.op("act", lambda: nc.scalar.activation(out=rs[:, 0:W], in_=self.bank[6][:, 0:W], func=AF.Ln,
                                                          bias=self.epsb, scale=1.0),
                      reads=[self.bank_trk[6], ct], writes=[rs_trk])
                cx.op("act", lambda: nc.scalar.activation(out=rs[:, 0:W], in_=rs[:, 0:W], func=AF.Exp, scale=-0.5),
                      reads=[rs_trk], writes=[rs_trk])
                cx.op("dve", lambda: nc.vector.scalar_tensor_tensor(
                    out=QT[:, hh, 0:W], in0=qp[:, 0:W], scalar=self.gqcol[:, j:j + 1], in1=rs[:, 0:W],
                    op0=ALU.mult, op1=ALU.mult),
                    reads=[qp_trk, ct, rs_trk], writes=[qt_trk])

        def o_proj(t):
            W = self.tiles[t][1]
            ts = self.tsl(t)
            for fc in range(NCH):
                bi = fc % 2
                cx.mm_group(self.bank[bi][:, 0:W], [(Wo[:, hh, fc * 128:(fc + 1) * 128], OT[:, hh, 0:W]) for hh in range(NH)],
                            reads=[sB, ot_trk], writes=[self.bank_trk[bi]])
                cx.op("dve", lambda: nc.vector.tensor_tensor(out=self.h[:, fc, ts], in0=self.bank[bi][:, 0:W],
                                                             in1=self.h[:, fc, ts], op=ALU.add),
                      reads=[self.bank_trk[bi], self.h_trk[t][fc]], writes=[self.h_trk[t][fc]])

        units = [(t, hh) for t in range(4) for hh in range(NH)]

        def load_kv(u):
            t, hh = units[u]
            kt, kt_trk = KTb[u % 2]
            vv, vv_trk = Vb[u % 2]
            nown = (t + 1) * 512
            self.load(kt[:, 0:NPT], self.KTprev[hh], writes=[kt_trk])
            self.load(kt[:, NPT:NPT + nown], self.KTown[hh, :, 0:nown], writes=[kt_trk])
            self.load(vv[:, 0:16, :],
                      self.Vprev.rearrange("(b p) (h d) -> p b h d", p=128, h=NH)[:, :, hh, :], writes=[vv_trk])
            self.load(vv[:, 16:16 + 4 * (t + 1), :],
                      self.Vown[0:nown].rearrange("(b p) (h d) -> p b h d", p=128, h=NH)[:, :, hh, :], writes=[vv_trk])

        def attend(u):
            t, hh = units[u]
            kt, kt_trk = KTb[u % 2]
            vv, vv_trk = Vb[u % 2]
            PO, po_trk = POs[u % 2]
            PD, pd_trk = PDs[u % 2]
            blocks = [(b, 0) for b in range(16 + 4 * t)] + [(16 + 4 * t + m, 128 * m) for m in range(4)]
            n = len(blocks)

            def emit_s(i):
                b, qo = blocks[i]
                N = TILE - qo
                ps, ps_trk = PS[i % 3]
                pt, pt_trk = pts[i % 4]
                cx.mm(ps[:, 0:N], kt[:, b * 128:(b + 1) * 128], QT[:, hh, qo:TILE], True, True,
                      reads=[kt_trk, qt_trk], writes=[ps_trk])
                cx.op("act", lambda: nc.scalar.activation(out=pt[:, 0:N], in_=ps[:, 0:N], func=AF.Exp,
                                                          bias=bias_t[:, b, hh:hh + 1], scale=SCALE),
                      reads=[ps_trk, bias_trk], writes=[pt_trk])
                if b >= 16 + 4 * t:
                    cx.op("dve", lambda: nc.vector.tensor_tensor(out=pt[:, 0:128], in0=pt[:, 0:128], in1=self.tri,
                                                                 op=ALU.mult),
                          reads=[pt_trk, ct], writes=[pt_trk])

            def emit_pv(i):
                b, qo = blocks[i]
                N = TILE - qo
                pt, pt_trk = pts[i % 4]
                cx.mm(PO[:, qo:TILE], vv[:, b, :], pt[:, 0:N], i == 0, i == n - 1,
                      reads=[vv_trk, pt_trk], writes=[po_trk])
                cx.mm(PD[:, qo:TILE], self.ones_bf, pt[:, 0:N], i == 0, i == n - 1,
                      reads=[ct, pt_trk], writes=[pd_trk])

            emit_s(0)
            emit_s(1)
            for i in range(n):
                if i + 2 < n:
                    emit_s(i + 2)
                emit_pv(i)
            cx.op("dve", lambda: nc.vector.reciprocal(out=recip, in_=PD), reads=[pd_trk], writes=[recip_trk])
            cx.op("dve", lambda: nc.vector.tensor_tensor(out=OT[:, hh, :], in0=PO, in1=recip, op=ALU.mult),
                  reads=[po_trk, recip_trk], writes=[ot_trk])

        load_kv(0)
        for t in range(4):
            q_proj(t)
            nb = 4 * (t + 1)
            cx.op("dve", lambda: nc.vector.tensor_tensor(
                out=bias_t[:, 0:16, :], in0=self.tail_prev, in1=self.Rt[:, t:t + 1, :].broadcast_to([128, 16, NH]),
                op=ALU.add), reads=[self.cs_trk], writes=[bias_trk])
            cx.op("dve", lambda: nc.vector.scalar_tensor_tensor(
                out=bias_t[:, 16:16 + nb, :], in0=self.csum_own[:, 0:nb, :], scalar=-1.0,
                in1=self.Rt[:, t:t + 1, :].broadcast_to([128, nb, NH]), op0=ALU.mult, op1=ALU.add),
                reads=[self.cs_trk], writes=[bias_trk])
            for hh in range(NH):
                u = t * NH + hh
                if u + 1 < len(units):
                    load_kv(u + 1)
                attend(u)
            o_proj(t)

        t = 4
        q_proj(t)
        if self.do_sample_attn:
            self.sample_attention(QT, qt_trk, OT, ot_trk, recip, recip_trk)
        else:
            cx.op("dve", lambda: nc.vector.memset(OT[:, :, 0:NST], 0.0), writes=[ot_trk])
        o_proj(t)
        self.w_release()
        self.w_release()

    def sample_attention(self, QT, qt_trk, OT, ot_trk, recip, recip_trk):
        nc, cx = self.nc, self.cx
        ct = self.const_trk
        PSs = [(self.bank[0], self.bank_trk[0]), (self.bank[1], self.bank_trk[1])]
        PO, po_trk = self.bank[2], self.bank_trk[2]
        PD, pd_trk = self.bank[3], self.bank_trk[3]
        PN, pn_trk = self.bank[4], self.bank_trk[4]
        self.RT.off = self._att_mark
        self._phase_barrier = cx.barrier_events()
        KTn, ktn_trk = self.palloc([128, NH, NST], BF16, "KTn")
        Vn, vn_trk = self.palloc([128, D], BF16, "Vn")
        tn, tn_trk = self.palloc([128, NH, NST], F32, "tn")
        pn, pnb_trk = self.palloc([128, NH, NST], BF16, "pn")
        ktg = [self.palloc([128, PGG, D], BF16, f"ktg{i}") for i in range(2)]
        Vh = [self.palloc([128, PGG, D], BF16, f"Vh{i}") for i in range(2)]
        tsc, tsc_trk = self.palloc([128, PGG, NH, 4], F32, "tsc")
        psb = [self.palloc([128, PGG, NH, 4], BF16, f"psb{i}") for i in range(2)]
        PO3 = PO.rearrange("p (h q) -> p h q", h=NH)
        PD3 = PD.rearrange("p (h q) -> p h q", h=NH)
        self.load(KTn, self.KTown.rearrange("h d t -> d h t")[:, :, NPT:NT], writes=[ktn_trk])
        self.load(Vn[0:NST], self.Vown[NPT:NT, :], writes=[vn_trk])
        for hh in range(NH):
            cx.mm(PN[0:NST, hh * NST:(hh + 1) * NST], KTn[:, hh, :], QT[:, hh, 0:NST], True, True,
                  reads=[ktn_trk, qt_trk], writes=[pn_trk])
        cx.op("dve", lambda: nc.vector.scalar_tensor_tensor(
            out=tn[0:NST], in0=PN[0:NST].rearrange("p (h q) -> p h q", h=NH), scalar=SCALE,
            in1=self.ncn[0:NST].unsqueeze(2).broadcast_to([NST, NH, NST]), op0=ALU.mult, op1=ALU.add),
            reads=[pn_trk, self.cs_trk], writes=[tn_trk])
        cx.op("act", lambda: nc.scalar.activation(out=pn[0:NST], in_=tn[0:NST], func=AF.Exp),
              reads=[tn_trk], writes=[pnb_trk])
        cx.op("dve", lambda: nc.vector.tensor_tensor(
            out=pn[0:NST], in0=pn[0:NST], in1=self.bdm[0:NST, 0:NST].unsqueeze(1).broadcast_to([NST, NH, NST]),
            op=ALU.mult), reads=[pnb_trk, ct], writes=[pnb_trk])
        zt, zt_trk = self.palloc([128, TILE], BF16, "zt")
        cx.op("dve", lambda: nc.vector.memset(zt, 0.0), writes=[zt_trk])
        cx.mm(PO, self.ones_bf, zt, True, False, reads=[ct, zt_trk], writes=[po_trk])
        cx.mm(PD, self.ones_bf, zt, True, False, reads=[ct, zt_trk], writes=[pd_trk])
        for hh in range(NH):
            cx.mm(PO3[:, hh, :], Vn[0:NST, hh * 128:(hh + 1) * 128], pn[0:NST, hh, :], False, False,
                  reads=[vn_trk, pnb_trk], writes=[po_trk])
            cx.mm(PD3[:, hh, :], self.ones_bf[0:NST, :], pn[0:NST, hh, :], False, False,
                  reads=[ct, pnb_trk], writes=[pd_trk])
        gcount = 0
        for i in range(NSQ):
            for half in range(NPG // PGG):
                ps, ps_trk = PSs[gcount % 2]
                vh, vh_trk = Vh[gcount % 2]
                pb, pb_trk = psb[gcount % 2]
                ps4 = ps[:, 0:PGG * NH * 4].rearrange("p (g h q) -> p g h q", g=PGG, h=NH)
                kg, kg_trk = ktg[gcount % 2]
                j0 = i * NPG + half * PGG
                self.load(kg, self.KTpast[j0:j0 + PGG].rearrange("g d x -> d g x"), writes=[kg_trk])
                self.load(vh, self.Vpast[j0:j0 + PGG].rearrange("g t x -> t g x"), writes=[vh_trk])
                for pg in range(PGG):
                    for hh in range(NH):
                        cx.mm(ps4[:, pg, hh, :], kg[:, pg, hh * 128:(hh + 1) * 128], QT[:, hh, 4 * i:4 * i + 4], True, True,
                              reads=[kg_trk, qt_trk], writes=[ps_trk])
                cx.op("dve", lambda: nc.vector.scalar_tensor_tensor(
                    out=tsc, in0=ps4, scalar=SCALE,
                    in1=self.tail_all[:, i, half * PGG:(half + 1) * PGG, :].unsqueeze(3).broadcast_to([128, PGG, NH, 4]),
                    op0=ALU.mult, op1=ALU.add), reads=[ps_trk, self.tail_trk], writes=[tsc_trk])
                cx.op("act", lambda: nc.scalar.activation(out=pb, in_=tsc, func=AF.Exp), reads=[tsc_trk], writes=[pb_trk])
                last = (half == NPG // PGG - 1)
                for pg in range(PGG):
                    for hh in range(NH):
                        cx.mm(PO3[:, hh, 4 * i:4 * i + 4], vh[:, pg, hh * 128:(hh + 1) * 128], pb[:, pg, hh, :],
                              False, last and pg == PGG - 1, reads=[vh_trk, pb_trk], writes=[po_trk])
                    cx.mm(PD3[:, :, 4 * i:4 * i + 4], self.ones_bf, pb[:, pg, :, :], False, last and pg == PGG - 1,
                          reads=[ct, pb_trk], writes=[pd_trk])
                gcount += 1
        cx.op("dve", lambda: nc.vector.reciprocal(out=recip, in_=PD), reads=[pd_trk], writes=[recip_trk])
        cx.op("dve", lambda: nc.vector.tensor_tensor(out=OT[:, :, 0:NST], in0=PO3, in1=recip.rearrange("p (h q) -> p h q", h=NH),
                                                     op=ALU.mult),
              reads=[po_trk, recip_trk], writes=[ot_trk])

    def build(self):
        attn = self.do_attn and self.n_layers > N_A
        self.bg_total = NSQ * NPG if (attn and self.do_sample_attn) else 0
        self.bg_nA = self.bg_nC = 0
        self.bg_bufs = None
        self.plan_weights()
        self.w_issue()
        self.w_issue()
        self.setup_consts()
        if attn:
            self.tiles = PREV_TILES
            self.load_x(self.xprev)
            for l in range(N_A):
                self.ffn(l, 1, "p")
                self.gmlp(l, "p", emit_gv=False)
                self.ffn(l, 2, "p")
                self.ple(l, self.pprev, "p")
            self.shared_kv(True, "p")
        self.tiles = MAIN_TILES
        self.load_x(self.xin)
        for l in range(self.n_layers):
            self.ffn(l, 1)
            if l < N_A:
                self.gmlp(l)
            elif attn:
                self.attention(l)
            self.ffn(l, 2)
            self.ple(l, self.pin)
            if l == N_A - 1:
                self.shared_kv(False)
                if attn:
                    self.prep_forget()
                    if self.do_sample_attn:
                        self.prep_past()
        self.store_y(self.y_out)
        self.cx.finish("sp")
        print(f"[kernel] instructions emitted: {self.cx.n_ins}")


def _consts():
    s = np.arange(128)
    ident = np.eye(128, dtype=np.float32)
    tri = (s[:, None] <= s[None, :]).astype(np.float32)
    bdm = ((s[:, None] // 4 == s[None, :] // 4) & (s[:, None] % 4 <= s[None, :] % 4)).astype(np.float32)
    rep = ((s[None, :] % 4 == s[:, None]) & (s[:, None] < 4)).astype(np.float32)
    su = (s[:, None] > s[None, :]).astype(np.float32)
    sel = np.zeros((128, 128), np.float32)
    sel[127, :] = 1.0
    iota = s.astype(np.int32).reshape(128, 1)
    return ident, tri, bdm, rep, su, sel, iota


def make_in_maps(inputs, n_cores=8, with_cache=True):
    f = lambda a: np.ascontiguousarray(np.asarray(a, dtype=np.float32))
    x_prompt = f(inputs["x_prompt"])
    x_sample = f(inputs["x_sample"]).reshape(128 * 4, D)
    p_prompt = f(inputs["p_prompt"])
    p_sample = f(inputs["p_sample"]).reshape(DEPTH, 128 * 4, PLE)
    norms = np.zeros((17, D), np.float32)
    for l in range(DEPTH):
        norms[4 * l + 0] = inputs["ffn1_norm"][l]
        norms[4 * l + 1] = inputs["mix_norm"][l]
        norms[4 * l + 2] = inputs["ffn2_norm"][l]
        norms[4 * l + 3] = inputs["ple_norm"][l]
    norms[16] = inputs["kv_norm"]
    ident, tri, bdm, rep, su, sel, iota = _consts()
    shared = {
        "norms": norms,
        "ple_w_gate": f(inputs["ple_w_gate"]), "ple_w_proj": f(inputs["ple_w_proj"]),
        "gmlp_w_in": f(inputs["gmlp_w_in"]), "gmlp_v_norm": f(inputs["gmlp_v_norm"]),
        "gmlp_w_s": f(inputs["gmlp_w_s"]), "gmlp_b_s": f(inputs["gmlp_b_s"]),
        "gmlp_w_out": f(inputs["gmlp_w_out"]),
        "w_kvf": f(inputs["w_kvf"]), "b_f": f(inputs["b_f"]).reshape(1, NH),
        "k_norm": f(inputs["k_norm"]).reshape(1, HD),
        "att_w_q": f(inputs["att_w_q"]), "q_norm": f(inputs["q_norm"]), "att_w_o": f(inputs["att_w_o"]),
        "c_ident": ident, "c_tri": tri, "c_bdm": bdm, "c_rep": rep, "c_su": su, "c_sel": sel, "c_iota": iota,
    }
    if with_cache:
        shared["cache_k"] = f(inputs["cache_k"]).reshape(NPOOL * 128, D)
        shared["cache_v"] = f(inputs["cache_v"]).reshape(NPOOL * 128, D)
        shared["cache_lf"] = f(inputs["cache_logf"]).reshape(NPOOL * 128, NH)
    else:
        shared["cache_k"] = np.zeros((128, D), np.float32)
        shared["cache_v"] = np.zeros((128, D), np.float32)
        shared["cache_lf"] = np.zeros((128, NH), np.float32)
    for which in (1, 2):
        for nm in ("w1", "w3", "w2"):
            shared[f"ffn{which}_{nm}"] = f(inputs[f"ffn{which}_{nm}"])
    page_table = np.asarray(inputs["page_table"], dtype=np.int32)
    maps = []
    for c in range(n_cores):
        s, hf = c // 2, c % 2
        m = dict(shared)
        sq = slice(NSQ * 4 * c, NSQ * 4 * (c + 1))
        m["xin"] = np.concatenate([x_prompt[s, hf * NPT:(hf + 1) * NPT], x_sample[sq]], axis=0)
        m["pin"] = np.concatenate([p_prompt[:, s, hf * NPT:(hf + 1) * NPT], p_sample[:, sq]], axis=1)
        m["xprev"] = np.ascontiguousarray(x_prompt[s, 0:NPT])
        m["pprev"] = np.ascontiguousarray(p_prompt[0:N_A, s, 0:NPT])
        m["flag"] = np.full((1, 1), NEG if hf == 0 else 0.0, np.float32)
        m["ptab"] = np.ascontiguousarray(page_table[NSQ * c:NSQ * (c + 1)].reshape(1, NSQ * NPG))
        maps.append(m)
    return maps


def assemble(results, n_cores=8):
    B, S = 4, 4096
    y_prompt = np.zeros((B, S, D), np.float32)
    k_prompt = np.zeros((B, S, NH, HD), np.float32)
    v_prompt = np.zeros((B, S, NH, HD), np.float32)
    lf_prompt = np.zeros((B, S, NH), np.float32)
    y_sample = np.zeros((128, 4, D), np.float32)
    k_sample = np.zeros((128, 4, NH, HD), np.float32)
    v_sample = np.zeros((128, 4, NH, HD), np.float32)
    lf_sample = np.zeros((128, 4, NH), np.float32)
    gv = np.zeros((N_A, 128, 4, D), np.float32)
    for c in range(n_cores):
        s, hf = c // 2, c % 2
        r = results[c]
        sl = slice(hf * NPT, (hf + 1) * NPT)
        bq = slice(NSQ * c, NSQ * (c + 1))
        y_prompt[s, sl] = r["y_out"][:NPT]
        k_prompt[s, sl] = r["k_out"][:NPT].reshape(NPT, NH, HD)
        v_prompt[s, sl] = r["v_out"][:NPT].reshape(NPT, NH, HD)
        lf_prompt[s, sl] = r["lf_out"][:NPT]
        y_sample[bq] = r["y_out"][NPT:].reshape(NSQ, 4, D)
        k_sample[bq] = r["k_out"][NPT:].reshape(NSQ, 4, NH, HD)
        v_sample[bq] = r["v_out"][NPT:].reshape(NSQ, 4, NH, HD)
        lf_sample[bq] = r["lf_out"][NPT:].reshape(NSQ, 4, NH)
        gv[:, bq] = r["gv_out"].reshape(N_A, NSQ, 4, D)
    return (y_prompt, y_sample, k_prompt, v_prompt, lf_prompt, k_sample, v_sample, lf_sample, gv)


def kernel(**inputs):
    n_cores = 8
    nc = bass.Bass("TRN2", target_bir_lowering=False)
    Builder(nc).build()
    in_maps = make_in_maps(inputs, n_cores)
    res = run_bass_kernel_spmd(nc, in_maps, core_ids=list(range(n_cores)))
    return assemble(res.results, n_cores)
```

```python
import numpy as np
import concourse.bass as bass
import concourse.mybir as mybir
from concourse.bass_utils import run_bass_kernel_spmd

F32 = mybir.dt.float32
BF16 = mybir.dt.bfloat16
I32 = mybir.dt.int32
AF = mybir.ActivationFunctionType
ALU = mybir.AluOpType
AX = mybir.AxisListType

ENGS = ("pe", "act", "dve", "pool", "sp")

D = 1024
NCH = 8
DFF = 2816
NFC = 22
PLE = 256
DEPTH = 4
N_A = 2
NH = 8
HD = 128
TILE = 512
NPT = 2048
NSQ = 16
NST = NSQ * 4
NT = NPT + NST
MAIN_TILES = [(0, 512), (512, 512), (1024, 512), (1536, 512), (2048, NST)]
PREV_TILES = [(0, 512), (512, 512), (1024, 512), (1536, 512)]
NBLK = 17
NPG = 16
NPOOL = 2560
SCALE = HD ** -0.5
NEG = -30000.0
PGG = 4
BGD = 3
EPS = 1e-6
SLOT_EL = 12288
FFN_GROUPS = [(0, 4), (4, 4), (8, 4), (12, 4), (16, 4), (20, 2)]


class Trk:
    __slots__ = ("name", "last_w", "reads")

    def __init__(self, name=""):
        self.name = name
        self.last_w = None
        self.reads = []


class Chan:
    def __init__(self, ctx, name):
        self.key = "dma_" + name
        self.sem = ctx.nc.alloc_semaphore(name=self.key)
        self.count = 0
        ctx.semh[self.key] = self.sem


class Ctx:
    def __init__(self, nc, same_engine_sync=True):
        self.nc = nc
        self.eng = {"pe": nc.tensor, "act": nc.scalar, "dve": nc.vector,
                    "pool": nc.gpsimd, "sp": nc.sync}
        self.semh = {}
        self.cnt = {}
        for e in ENGS:
            self.semh[e] = nc.alloc_semaphore(name="sem_" + e)
            self.cnt[e] = 0
        self.seen = {e: {} for e in ENGS}
        self.same_engine_sync = same_engine_sync
        self.chans = []
        self.n_ins = 0
        self.all_trk = []

    def trk(self, name=""):
        t = Trk(name)
        self.all_trk.append(t)
        return t

    def chan(self, name):
        c = Chan(self, name)
        self.chans.append(c)
        return c

    def _wait(self, e, events):
        need = {}
        for ev in events:
            if ev is None:
                continue
            k, v = ev
            if k == e and (e == "pe" or not self.same_engine_sync):
                continue
            if need.get(k, 0) < v:
                need[k] = v
        seen = self.seen[e]
        for k, v in need.items():
            if seen.get(k, 0) < v:
                self.eng[e].wait_ge(self.semh[k], v)
                seen[k] = v

    @staticmethod
    def _deps(reads, writes):
        evs = []
        for t in reads:
            evs.append(t.last_w)
        for t in writes:
            evs.append(t.last_w)
            evs.extend(t.reads)
        return evs

    @staticmethod
    def _commit(ev, reads, writes):
        for t in reads:
            t.reads.append(ev)
            if len(t.reads) > 32:
                mx = {}
                for k, v in t.reads:
                    if mx.get(k, 0) < v:
                        mx[k] = v
                t.reads = list(mx.items())
        for t in writes:
            t.last_w = ev
            t.reads = []

    def op(self, e, fn, reads=(), writes=()):
        self._wait(e, self._deps(reads, writes))
        ins = fn()
        self.cnt[e] += 1
        ins.then_inc(self.semh[e], 1)
        ev = (e, self.cnt[e])
        self._commit(ev, reads, writes)
        self.n_ins += 1
        return ev

    def mm_group(self, out_ap, pairs, reads=(), writes=()):
        e = "pe"
        self._wait(e, self._deps(reads, writes))
        n = len(pairs)
        ins = None
        for i, (l, r) in enumerate(pairs):
            ins = self.nc.tensor.matmul(out_ap, l, r, start=(i == 0), stop=(i == n - 1))
        self.n_ins += n
        self.cnt[e] += 1
        ins.then_inc(self.semh[e], 1)
        ev = (e, self.cnt[e])
        self._commit(ev, reads, writes)
        return ev

    def mm(self, out_ap, lhsT, rhs, start, stop, reads=(), writes=()):
        e = "pe"
        self._wait(e, self._deps(reads, writes))
        ins = self.nc.tensor.matmul(out_ap, lhsT, rhs, start=start, stop=stop)
        self.n_ins += 1
        self.cnt[e] += 1
        ins.then_inc(self.semh[e], 1)
        ev = (e, self.cnt[e])
        self._commit(ev, reads, writes)
        return ev

    def idma(self, chan, out, in_, idx_ap, reads=(), writes=()):
        q = "pool"
        self._wait(q, self._deps(reads, writes))
        ins = self.nc.gpsimd.indirect_dma_start(
            out=out, out_offset=None, in_=in_,
            in_offset=bass.IndirectOffsetOnAxis(ap=idx_ap, axis=0))
        chan.count += 1
        ins.then_inc(chan.sem, 16)
        ev = (chan.key, 16 * chan.count)
        self._commit(ev, reads, writes)
        self.n_ins += 1
        return ev

    def dma(self, q, chan, out, in_, reads=(), writes=()):
        self._wait(q, self._deps(reads, writes))
        ins = self.eng[q].dma_start(out=out, in_=in_)
        chan.count += 1
        ins.then_inc(chan.sem, 16)
        ev = (chan.key, 16 * chan.count)
        self._commit(ev, reads, writes)
        self.n_ins += 1
        return ev

    def barrier_events(self):
        evs = [(e, self.cnt[e]) for e in ENGS if self.cnt[e]]
        evs += [(c.key, 16 * c.count) for c in self.chans if c.count and not c.key.startswith("dma_w")]
        return evs

    def finish(self, e="sp"):
        self._wait(e, self.barrier_events())


class SB:
    def __init__(self, big_ap, base, limit):
        self.big = big_ap
        self.base = base
        self.off = base
        self.limit = limit

    def reset(self):
        self.off = self.base

    def alloc(self, shape, dtype, align=64):
        esz = 4 if dtype in (F32, I32) else 2
        n = int(np.prod(shape[1:]))
        nbytes = esz * n
        self.off = (self.off + align - 1) // align * align
        a = self.big[:, self.off // 2:(self.off + nbytes) // 2]
        if dtype != BF16:
            a = a.bitcast(dtype)
        if shape[0] != 128:
            a = a[0:shape[0]]
        if len(shape) == 3:
            a = a.rearrange("p (a b) -> p a b", a=shape[1])
        elif len(shape) == 4:
            a = a.rearrange("p (a b c) -> p a b c", a=shape[1], b=shape[2])
        self.off += nbytes
        assert self.off <= self.limit, ("SBUF region overflow", self.off, self.limit)
        return a


class Builder:
    def __init__(self, nc, n_layers=DEPTH, do_attn=True, do_sample_attn=True):
        self.nc = nc
        self.n_layers = n_layers
        self.do_attn = do_attn
        self.do_sample_attn = do_sample_attn
        self.cx = Ctx(nc)
        self.tiles = MAIN_TILES
        self.npool = NPOOL if do_sample_attn else 1
        self.declare_dram()
        self.alloc_onchip()

    def declare_dram(self):
        nc = self.nc

        def inp(name, shape, dt=F32):
            return nc.dram_tensor(name, list(shape), dt, kind="ExternalInput").ap()

        def outp(name, shape, dt=F32):
            return nc.dram_tensor(name, list(shape), dt, kind="ExternalOutput").ap()

        def scr(name, shape, dt=BF16):
            return nc.dram_tensor(name, list(shape), dt, kind="Internal").ap()

        self.xin = inp("xin", [NT, D])
        self.pin = inp("pin", [DEPTH, NT, PLE])
        self.xprev = inp("xprev", [NPT, D])
        self.pprev = inp("pprev", [N_A, NPT, PLE])
        self.flag = inp("flag", [1, 1])
        self.ptab = inp("ptab", [1, NSQ * NPG], I32)
        self.cache_k = inp("cache_k", [self.npool * 128, D])
        self.cache_v = inp("cache_v", [self.npool * 128, D])
        self.cache_lf = inp("cache_lf", [self.npool * 128, NH])
        self.norms = inp("norms", [17, D])
        self.w_ffn = {}
        for which in (1, 2):
            for nm in ("w1", "w3", "w2"):
                shp = [DEPTH, D, DFF] if nm != "w2" else [DEPTH, DFF, D]
                self.w_ffn[(which, nm)] = inp(f"ffn{which}_{nm}", shp)
        self.ple_w_gate = inp("ple_w_gate", [DEPTH, D, D])
        self.ple_w_proj = inp("ple_w_proj", [DEPTH, PLE, D])
        self.gmlp_w_in = inp("gmlp_w_in", [N_A, D, 2 * D])
        self.gmlp_v_norm = inp("gmlp_v_norm", [N_A, D])
        self.gmlp_w_s = inp("gmlp_w_s", [N_A, 8, 128, 128])
        self.gmlp_b_s = inp("gmlp_b_s", [N_A, 8, 128])
        self.gmlp_w_out = inp("gmlp_w_out", [N_A, D, D])
        self.w_kvf = inp("w_kvf", [D, 2 * D + NH])
        self.b_f = inp("b_f", [1, NH])
        self.k_norm = inp("k_norm", [1, HD])
        self.att_w_q = inp("att_w_q", [2, D, D])
        self.q_norm = inp("q_norm", [2, HD])
        self.att_w_o = inp("att_w_o", [2, D, D])
        self.c_ident = inp("c_ident", [128, 128])
        self.c_tri = inp("c_tri", [128, 128])
        self.c_bdm = inp("c_bdm", [128, 128])
        self.c_rep = inp("c_rep", [128, 128])
        self.c_su = inp("c_su", [128, 128])
        self.c_sel = inp("c_sel", [128, 128])
        self.c_iota = inp("c_iota", [128, 1], I32)
        self.y_out = outp("y_out", [NT, D])
        self.k_out = outp("k_out", [NT, D])
        self.v_out = outp("v_out", [NT, D])
        self.lf_out = outp("lf_out", [NT, NH])
        self.gv_out = outp("gv_out", [N_A, NST, D])
        self.KTprev = scr("KTprev", [NH, HD, NPT])
        self.Vprev = scr("Vprev", [NPT, D])
        self.KTown = scr("KTown", [NH, HD, NT])
        self.Vown = scr("Vown", [NT, D])
        self.KTpast = scr("KTpast", [NSQ * NPG, HD, D])
        self.Vpast = scr("Vpast", [NSQ * NPG, 128, D])

    def alloc_onchip(self):
        nc, cx = self.nc, self.cx
        total = nc.sbuf_bytes_remaining
        total = (total - 256) // 64 * 64
        big = nc.alloc_sbuf_tensor("big", [128, total // 2], BF16).ap()
        self.big = big
        P = SB(big, 0, total)
        self.h = P.alloc([128, NCH, NT], F32)
        self.h_trk = [[cx.trk(f"h{t}_{c}") for c in range(NCH)] for t in range(len(MAIN_TILES))]
        self.wslot = [P.alloc([128, SLOT_EL], BF16) for _ in range(2)]
        self.wslot_trk = [cx.trk("wslot0"), cx.trk("wslot1")]
        self.ident = P.alloc([128, 128], F32)
        self.identb = P.alloc([128, 128], BF16)
        self.ones_mean = P.alloc([128, 128], BF16)
        self.ones_hd = P.alloc([128, 128], BF16)
        self.ones_bf = P.alloc([128, 128], BF16)
        self.ones_f = P.alloc([128, 128], F32)
        self.tri = P.alloc([128, 128], F32)
        self.bdm = P.alloc([128, 128], F32)
        self.rep = P.alloc([128, 128], F32)
        self.su = P.alloc([128, 128], F32)
        self.sel = P.alloc([128, 128], F32)
        self.gcol = P.alloc([128, 17 * NCH], F32)
        self.gqcol = P.alloc([128, 2], F32)
        self.epsb = P.alloc([128, 1], F32)
        self.oneb = P.alloc([128, 1], F32)
        self.flagB = P.alloc([128, 1], F32)
        self.iota = P.alloc([128, 1], I32)
        self.lf_all = P.alloc([128, NBLK, NH], F32)
        self.lf_prev = P.alloc([128, 16, NH], F32)
        self.csum_own = P.alloc([128, 16, NH], F32)
        self.tail_prev = P.alloc([128, 16, NH], F32)
        self.Rt = P.alloc([128, 4, NH], F32)
        self.ncn = P.alloc([128, NH], F32)
        self.idx_all = P.alloc([128, NSQ * NPG], I32)
        self.tail_all = P.alloc([128, NSQ, NPG, NH], F32)
        self.bfB = P.alloc([128, NH], F32)
        self.gkB = P.alloc([128, HD], F32)
        self.const_trk = cx.trk("const")
        self.lf_trk = [cx.trk("lf") for b in range(NBLK)]
        self.lfp_trk = [cx.trk("lfp") for b in range(16)]
        self.cs_trk = cx.trk("csum")
        self.idx_trk = cx.trk("idx")
        self.tail_trk = cx.trk("tailall")
        self.RT = SB(big, P.off, total)
        self.rt_bytes = total - P.off
        print(f"[kernel] SBUF persistent={P.off} phase_region={self.rt_bytes}")
        self.pp = [nc.alloc_psum_tensor(f"pp{i}", [128, 1024], F32).ap() for i in range(4)]
        self.bank = []
        for i in range(4):
            self.bank += [self.pp[i][:, 0:512], self.pp[i][:, 512:1024]]
        self.bank_trk = [cx.trk(f"bank{i}") for i in range(8)]
        self.ch_w = [cx.chan("w0"), cx.chan("w1")]
        self.ch_const = cx.chan("const")
        self.ch_by_name = {}

    def phase(self):
        self.RT.reset()
        self._phase_barrier = self.cx.barrier_events()

    def ptrk(self, name=""):
        t = self.cx.trk(name)
        t.reads = list(self._phase_barrier)
        return t

    def palloc(self, shape, dtype, name=""):
        return self.RT.alloc(shape, dtype), self.ptrk(name)

    def plan_weights(self):
        pieces = []

        def chunked(ap2d):
            return ap2d.rearrange("(c p) f -> p c f", p=128)

        def ffn_pieces(l, which, tag):
            for (f0, gc) in FFN_GROUPS:
                w1 = self.w_ffn[(which, "w1")][l]
                w3 = self.w_ffn[(which, "w3")][l]
                w2 = self.w_ffn[(which, "w2")][l]
                n1 = NCH * gc * 128
                pieces.append((f"{tag}ffn{which}_{l}_{f0}", [
                    ((0, n1, [NCH, gc * 128]), chunked(w1[:, f0 * 128:(f0 + gc) * 128])),
                    ((4096, n1, [NCH, gc * 128]), chunked(w3[:, f0 * 128:(f0 + gc) * 128])),
                    ((8192, gc * D, [gc, D]), chunked(w2[f0 * 128:(f0 + gc) * 128, :])),
                ]))

        def gmlp_pieces(l, tag):
            wi = self.gmlp_w_in[l]
            pieces.append((f"{tag}gmlpA_{l}", [
                ((0, 8192, [NCH, D]), chunked(wi[:, 0:D])),
                ((8192, 4096, [4, D]), chunked(wi[0:512, D:2 * D])),
            ]))
            pieces.append((f"{tag}gmlpB_{l}", [
                ((0, 4096, [4, D]), chunked(wi[512:1024, D:2 * D])),
                ((4096, 8192, [NCH, D]), chunked(self.gmlp_w_out[l])),
            ]))

        def ple_piece(l, tag):
            pieces.append((f"{tag}ple_{l}", [
                ((0, 8192, [NCH, D]), chunked(self.ple_w_gate[l])),
                ((8192, 2048, [2, D]), chunked(self.ple_w_proj[l])),
            ]))

        def kv_pieces(tag):
            pieces.append((f"{tag}kvA", [((0, 8192, [NCH, D]), chunked(self.w_kvf[:, 0:D]))]))
            pieces.append((f"{tag}kvB", [((0, NCH * (D + NH), [NCH, D + NH]),
                                          chunked(self.w_kvf[:, D:2 * D + NH]))]))

        if self.do_attn and self.n_layers > N_A:
            for l in range(N_A):
                ffn_pieces(l, 1, "p")
                gmlp_pieces(l, "p")
                ffn_pieces(l, 2, "p")
                ple_piece(l, "p")
            kv_pieces("p")
        for l in range(self.n_layers):
            ffn_pieces(l, 1, "")
            if l < N_A:
                gmlp_pieces(l, "")
            elif self.do_attn:
                j = l - N_A
                pieces.append((f"attA_{l}", [((0, 8192, [NCH, D]), chunked(self.att_w_q[j]))]))
                pieces.append((f"attB_{l}", [((0, 8192, [NCH, D]), chunked(self.att_w_o[j]))]))
            ffn_pieces(l, 2, "")
            ple_piece(l, "")
            if l == N_A - 1:
                kv_pieces("")
        self.pieces = pieces
        self.w_cur = 0
        self.w_issued = 0

    def w_view(self, slot, off, n, shp):
        return self.wslot[slot][:, off:off + n].rearrange("p (a b) -> p a b", a=shp[0])

    def w_issue(self):
        i = self.w_issued
        if i >= len(self.pieces):
            return
        slot = i % 2
        name, loads = self.pieces[i]
        for (off, n, shp), src in loads:
            self.cx.dma("pool", self.ch_w[slot], self.w_view(slot, off, n, shp), src,
                        writes=[self.wslot_trk[slot]])
        self.w_issued += 1

    def w_acquire(self, name):
        i = self.w_cur
        assert self.pieces[i][0] == name, (self.pieces[i][0], name)
        assert self.w_issued > i
        self.w_cur += 1
        return i % 2

    def w_release(self):
        self.w_issue()

    def tsl(self, t):
        off, w = self.tiles[t]
        return slice(off, off + w)

    def blocks(self, t):
        off, w = self.tiles[t]
        return [(off + 128 * j, min(128, w - 128 * j)) for j in range((w + 127) // 128)]

    def chan_for(self, trk, kind):
        key = kind + "_" + trk.name
        if key not in self.ch_by_name:
            self.ch_by_name[key] = self.cx.chan(key)
        return self.ch_by_name[key]

    def store(self, out_ap, in_ap, reads, writes=()):
        return self.cx.dma("sp", self.chan_for(reads[0], "st"), out_ap, in_ap, reads=reads, writes=writes)

    def load(self, out_ap, in_ap, writes, reads=(), q="sp"):
        return self.cx.dma(q, self.chan_for(writes[0], "ld"), out_ap, in_ap, reads=reads, writes=writes)

    def setup_consts(self):
        nc, cx = self.nc, self.cx
        ct = self.const_trk
        for dst, src in ((self.ident, self.c_ident), (self.tri, self.c_tri), (self.bdm, self.c_bdm),
                         (self.rep, self.c_rep), (self.su, self.c_su), (self.sel, self.c_sel),
                         (self.iota, self.c_iota)):
            cx.dma("sp", self.ch_const, dst, src, writes=[ct])
        cx.dma("sp", self.ch_const, self.bfB, self.b_f[0].partition_broadcast(128), writes=[ct])
        cx.dma("sp", self.ch_const, self.gkB, self.k_norm[0].partition_broadcast(128), writes=[ct])
        cx.dma("sp", self.ch_const, self.flagB, self.flag[0].partition_broadcast(128), writes=[ct])
        with nc.allow_non_contiguous_dma(reason="256-element gain transpose"):
            cx.dma("sp", self.ch_const, self.gqcol, self.q_norm.rearrange("j d -> d j"), writes=[ct])
        cx.op("dve", lambda: nc.vector.memset(self.ones_mean, 1.0 / D), writes=[ct])
        cx.op("dve", lambda: nc.vector.memset(self.ones_hd, 1.0 / HD), writes=[ct])
        cx.op("dve", lambda: nc.vector.memset(self.ones_bf, 1.0), writes=[ct])
        cx.op("dve", lambda: nc.vector.memset(self.ones_f, 1.0), writes=[ct])
        cx.op("dve", lambda: nc.vector.memset(self.epsb, EPS), writes=[ct])
        cx.op("dve", lambda: nc.vector.memset(self.oneb, 1.0), writes=[ct])
        cx.op("dve", lambda: nc.vector.tensor_copy(out=self.identb, in_=self.ident), reads=[ct], writes=[ct])
        self.phase()
        ptab_b, ptab_trk = self.palloc([128, NSQ * NPG], I32, "ptab_b")
        self.load(ptab_b, self.ptab[0].partition_broadcast(128), writes=[ptab_trk])
        cx.op("dve", lambda: nc.vector.tensor_scalar(out=self.idx_all, in0=ptab_b, scalar1=128, scalar2=self.iota[:, 0:1],
                                                     op0=ALU.mult, op1=ALU.add),
              reads=[ptab_trk, ct], writes=[self.idx_trk])
        rows, rt = self.palloc([17, D], F32, "normrows")
        self.load(rows, self.norms, writes=[rt])
        bt = self.bank_trk[0]
        for c in range(NCH):
            cx.op("pe", lambda: nc.tensor.transpose(self.bank[0][:, c * 17:(c + 1) * 17],
                                                    rows[:, c * 128:(c + 1) * 128], self.ident[0:17, 0:17]),
                  reads=[rt, ct], writes=[bt])
        cx.op("act", lambda: nc.scalar.activation(
            out=self.gcol.rearrange("p (n c) -> p c n", c=NCH),
            in_=self.bank[0][:, 0:NCH * 17].rearrange("p (c n) -> p c n", n=17), func=AF.Copy),
            reads=[bt], writes=[ct])

    def load_x(self, src):
        nc, cx = self.nc, self.cx
        self.phase()
        xr = [self.palloc([128, D], F32, f"xrow{i}") for i in range(2)]
        b = 0
        for t in range(len(self.tiles)):
            for (tok0, bp) in self.blocks(t):
                row, rtrk = xr[b % 2]
                self.load(row[0:bp], src[tok0:tok0 + bp, :], writes=[rtrk])
                for j in range(2):
                    bi = (2 * b + j) % 4
                    for cc in range(4):
                        c = 4 * j + cc
                        cx.op("pe", lambda: nc.tensor.transpose(self.bank[bi][:, cc * 128:cc * 128 + bp],
                                                                row[0:bp, c * 128:(c + 1) * 128],
                                                                self.ident[0:bp, 0:bp]),
                              reads=[rtrk, self.const_trk], writes=[self.bank_trk[bi]])
                    dst = self.h[:, 4 * j:4 * j + 4, tok0:tok0 + bp]
                    srcp = self.bank[bi].rearrange("p (a b) -> p a b", a=4)[:, :, 0:bp]
                    if j == 0:
                        fn = lambda: nc.scalar.activation(out=dst, in_=srcp, func=AF.Copy)
                    else:
                        fn = lambda: nc.vector.tensor_copy(out=dst, in_=srcp)
                    cx.op("act" if j == 0 else "dve", fn, reads=[self.bank_trk[bi]],
                          writes=self.h_trk[t][4 * j:4 * j + 4])
                b += 1

    def store_y(self, dst):
        nc, cx = self.nc, self.cx
        self.phase()
        yr = [self.palloc([128, D], F32, f"yrow{i}") for i in range(2)]
        b = 0
        for t in range(len(self.tiles)):
            for (tok0, bp) in self.blocks(t):
                row, rtrk = yr[b % 2]
                for j in range(2):
                    bi = (2 * b + j) % 4
                    for cc in range(4):
                        c = 4 * j + cc
                        cx.op("pe", lambda: nc.tensor.transpose(self.bank[bi][0:bp, cc * 128:(cc + 1) * 128],
                                                                self.h[:, c, tok0:tok0 + bp], self.ident),
                              reads=[self.h_trk[t][c], self.const_trk], writes=[self.bank_trk[bi]])
                    d_ = row[0:bp, 512 * j:512 * (j + 1)]
                    s_ = self.bank[bi][0:bp, :]
                    if j == 0:
                        fn = lambda: nc.scalar.activation(out=d_, in_=s_, func=AF.Copy)
                    else:
                        fn = lambda: nc.vector.tensor_copy(out=d_, in_=s_)
                    cx.op("act" if j == 0 else "dve", fn, reads=[self.bank_trk[bi]], writes=[rtrk])
                self.store(dst[tok0:tok0 + bp, :], row[0:bp], reads=[rtrk])
                b += 1

    def norm_tile(self, t, gi, xn_dst, xn_trks, tmp):
        nc, cx = self.nc, self.cx
        sq, sq_trk, rs, rs_trk = tmp
        st_bank, st_trk = self.bank[6], self.bank_trk[6]
        ts = self.tsl(t)
        W = self.tiles[t][1]
        for hf in range(2):
            cx.op("act", lambda: nc.scalar.activation(out=sq[hf][:, :, 0:W], in_=self.h[:, 4 * hf:4 * hf + 4, ts],
                                                      func=AF.Square),
                  reads=self.h_trk[t][4 * hf:4 * hf + 4], writes=[sq_trk[hf]])
        cx.mm_group(st_bank[:, 0:W], [(self.ones_mean, sq[c // 4][:, c % 4, 0:W]) for c in range(NCH)],
                    reads=[self.const_trk, sq_trk[0], sq_trk[1]], writes=[st_trk])
        cx.op("act", lambda: nc.scalar.activation(out=rs[:, 0:W], in_=st_bank[:, 0:W], func=AF.Ln,
                                                  bias=self.epsb, scale=1.0),
              reads=[st_trk, self.const_trk], writes=[rs_trk])
        cx.op("act", lambda: nc.scalar.activation(out=rs[:, 0:W], in_=rs[:, 0:W], func=AF.Exp, scale=-0.5),
              reads=[rs_trk], writes=[rs_trk])
        for c in range(NCH):
            cx.op("dve", lambda: nc.vector.scalar_tensor_tensor(
                out=xn_dst[:, c, 0:W], in0=self.h[:, c, ts], scalar=self.gcol[:, gi * NCH + c:gi * NCH + c + 1],
                in1=rs[:, 0:W], op0=ALU.mult, op1=ALU.mult),
                reads=[self.h_trk[t][c], self.const_trk, rs_trk], writes=[xn_trks[c]])

    def norm_tmp(self):
        sq, sq_trk = [], []
        for i in range(2):
            a, tk = self.palloc([128, 4, TILE], BF16, f"sq{i}")
            sq.append(a)
            sq_trk.append(tk)
        rs, rs_trk = self.palloc([128, TILE], F32, "rstd")
        return (sq, sq_trk, rs, rs_trk)

    def bg_alloc(self):
        self.bg_bufs = None
        if self.bg_nC >= self.bg_total:
            return
        kbf = [self.palloc([128, D], BF16, f"bgk{i}") for i in range(BGD)]
        vbf = [self.palloc([128, D], BF16, f"bgv{i}") for i in range(BGD)]
        ktp = [self.palloc([128, D], BF16, f"bgt{i}") for i in range(2)]
        self.bg_bufs = (kbf, vbf, ktp)

    def bg_A(self, p):
        cx = self.cx
        kbf, vbf, ktp = self.bg_bufs
        i, pg = divmod(p, NPG)
        kb, kb_trk = kbf[p % BGD]
        vb, vb_trk = vbf[p % BGD]
        ix = self.idx_all[:, p:p + 1]
        cx.idma(self.chan_for(kb_trk, "ld"), kb, self.cache_k, ix, reads=[self.idx_trk], writes=[kb_trk])
        cx.idma(self.chan_for(vb_trk, "ld"), vb, self.cache_v, ix, reads=[self.idx_trk], writes=[vb_trk])
        cx.idma(self.chan_for(self.tail_trk, "ld"), self.tail_all[:, i, pg, :], self.cache_lf, ix,
                reads=[self.idx_trk], writes=[self.tail_trk])

    def bg_C(self, p):
        nc, cx = self.nc, self.cx
        kbf, vbf, ktp = self.bg_bufs
        kb, kb_trk = kbf[p % BGD]
        vb, vb_trk = vbf[p % BGD]
        kp, kp_trk = ktp[p % 2]
        ptb = self.bank[7].bitcast(BF16)
        ptb_trk = self.bank_trk[7]
        for hh in range(NH):
            cx.op("pe", lambda: nc.tensor.transpose(ptb[:, hh * 128:(hh + 1) * 128], kb[:, hh * 128:(hh + 1) * 128],
                                                    self.identb),
                  reads=[kb_trk, self.const_trk], writes=[ptb_trk])
        if p % 2 == 0:
            cx.op("act", lambda: nc.scalar.activation(out=kp, in_=ptb, func=AF.Copy), reads=[ptb_trk], writes=[kp_trk])
        else:
            cx.op("dve", lambda: nc.vector.tensor_copy(out=kp, in_=ptb), reads=[ptb_trk], writes=[kp_trk])
        self.store(self.KTpast[p], kp, reads=[kp_trk])
        self.store(self.Vpast[p], vb, reads=[vb_trk])

    def bg_hook(self):
        if not self.bg_bufs:
            return
        if self.bg_nA - self.bg_nC >= BGD - 1 or (self.bg_nA >= self.bg_total and self.bg_nC < self.bg_nA):
            self.bg_C(self.bg_nC)
            self.bg_nC += 1
        if self.bg_nA < self.bg_total and self.bg_nA - self.bg_nC < BGD:
            self.bg_A(self.bg_nA)
            self.bg_nA += 1

    def bg_flush(self):
        if not self.bg_bufs:
            return
        while self.bg_nC < self.bg_nA:
            self.bg_C(self.bg_nC)
            self.bg_nC += 1
        self.bg_bufs = None

    def ffn(self, l, which, tag=""):
        nc, cx = self.nc, self.cx
        self.phase()
        ntl = len(self.tiles)
        ntok = self.tiles[-1][0] + self.tiles[-1][1]
        gi = 4 * l + (0 if which == 1 else 2)
        xn, _ = self.palloc([128, NCH, ntok], BF16, "xn_all")
        xn_trk = [[self.ptrk(f"xn{t}_{c}") for c in range(NCH)] for t in range(ntl)]
        tmp = self.norm_tmp()
        hmid = [self.palloc([128, 4, TILE], BF16, f"hmid{i}") for i in range(2)]
        sa = [self.palloc([128, TILE], F32, f"sa{i}") for i in range(2)]
        self.bg_alloc()
        self.norm_tile(0, gi, xn[:, :, self.tsl(0)], xn_trk[0], tmp)
        PA = [(self.bank[0], self.bank_trk[0]), (self.bank[1], self.bank_trk[1])]
        PB = [(self.bank[2], self.bank_trk[2]), (self.bank[3], self.bank_trk[3])]
        PY = [(self.bank[4], self.bank_trk[4]), (self.bank[5], self.bank_trk[5])]
        steps = [(g, t) for g in range(len(FFN_GROUPS)) for t in range(ntl)]
        state = {"k": 0, "j": 0, "slot": {}}

        def emit_ab(i):
            g, t = steps[i]
            f0, gc = FFN_GROUPS[g]
            if t == 0:
                state["slot"][g] = self.w_acquire(f"{tag}ffn{which}_{l}_{f0}")
            slot = state["slot"][g]
            strk = self.wslot_trk[slot]
            w1g = self.w_view(slot, 0, NCH * gc * 128, [NCH, gc * 128])
            w3g = self.w_view(slot, 4096, NCH * gc * 128, [NCH, gc * 128])
            hm, hm_trk = hmid[i % 2]
            ts = self.tsl(t)
            W = self.tiles[t][1]
            for fc in range(gc):
                k = state["k"]
                state["k"] += 1
                (a_ps, a_trk), (b_ps, b_trk) = PA[k % 2], PB[k % 2]
                s_sb, s_trk = sa[k % 2]
                fs = slice(fc * 128, (fc + 1) * 128)
                cx.mm_group(a_ps[:, 0:W], [(w1g[:, c, fs], xn[:, c, ts]) for c in range(NCH)],
                            reads=[strk] + xn_trk[t], writes=[a_trk])
                cx.mm_group(b_ps[:, 0:W], [(w3g[:, c, fs], xn[:, c, ts]) for c in range(NCH)],
                            reads=[strk] + xn_trk[t], writes=[b_trk])
                cx.op("act", lambda: nc.scalar.activation(out=s_sb[:, 0:W], in_=a_ps[:, 0:W], func=AF.Silu),
                      reads=[a_trk], writes=[s_trk])
                cx.op("dve", lambda: nc.vector.tensor_tensor(out=hm[:, fc, 0:W], in0=s_sb[:, 0:W], in1=b_ps[:, 0:W],
                                                             op=ALU.mult),
                      reads=[s_trk, b_trk], writes=[hm_trk])

        def emit_y(i):
            g, t = steps[i]
            f0, gc = FFN_GROUPS[g]
            slot = state["slot"][g]
            strk = self.wslot_trk[slot]
            w2g = self.w_view(slot, 8192, gc * D, [gc, D])
            hm, hm_trk = hmid[i % 2]
            ts = self.tsl(t)
            W = self.tiles[t][1]
            for c in range(NCH):
                j = state["j"]
                state["j"] += 1
                y_ps, y_trk = PY[j % 2]
                cx.mm_group(y_ps[:, 0:W], [(w2g[:, fc, c * 128:(c + 1) * 128], hm[:, fc, 0:W]) for fc in range(gc)],
                            reads=[strk, hm_trk], writes=[y_trk])
                cx.op("dve", lambda: nc.vector.scalar_tensor_tensor(
                    out=self.h[:, c, ts], in0=y_ps[:, 0:W], scalar=0.5, in1=self.h[:, c, ts],
                    op0=ALU.mult, op1=ALU.add),
                    reads=[y_trk, self.h_trk[t][c]], writes=[self.h_trk[t][c]])
            if t == ntl - 1:
                self.w_release()

        emit_ab(0)
        for i in range(len(steps)):
            if i + 1 < len(steps):
                g1, t1 = steps[i + 1]
                if g1 == 0:
                    self.norm_tile(t1, gi, xn[:, :, self.tsl(t1)], xn_trk[t1], tmp)
                emit_ab(i + 1)
            self.bg_hook()
            emit_y(i)
            self.bg_hook()
        self.bg_flush()

    def gmlp(self, l, tag="", emit_gv=True):
        nc, cx = self.nc, self.cx
        self.phase()
        gi = 4 * l + 1
        ct = self.const_trk
        xt, _ = self.palloc([128, NCH, TILE], BF16, "xn_t0")
        xtr = [self.ptrk() for c in range(NCH)]
        uT, uT_trk = self.palloc([128, NCH, TILE], BF16, "uT")
        vn, _ = self.palloc([128, 4, D], BF16, "vn")
        vn_trk = [self.ptrk() for _ in range(4)]
        us, us_trk = self.palloc([128, NCH, TILE], BF16, "us")
        tmp = self.norm_tmp()
        mixp, mixp_trk = self.palloc([128, 8, 128], BF16, "mixp")
        mixs, mixs_trk = self.palloc([128, 8, 128], BF16, "mixs")
        bsp, bsp_trk = self.palloc([128, 8, 128], F32, "bsp")
        bss, bss_trk = self.palloc([128, 8, 128], F32, "bss")
        gvB, gvB_trk = self.palloc([128, D], F32, "gvB")
        wsr, wsr_trk = self.palloc([128, 8, 128], F32, "wsr")
        x44, x44_trk = self.palloc([128, 8, 4], F32, "x44")
        t1s, t1s_trk = self.palloc([128, 8, 4], F32, "t1s")
        vf, vf_trk = self.palloc([128, D], F32, "vnf0")
        junk, junk_trk = self.palloc([128, D], BF16, "junk")
        ssv, ssv_trk = self.palloc([128, 2], F32, "ssv")
        t1 = [self.palloc([128, TILE], F32, f"t1_{i}") for i in range(2)]

        self.load(wsr, self.gmlp_w_s[l].rearrange("g t s -> t g s"), writes=[wsr_trk])
        self.load(bsp, self.gmlp_b_s[l].partition_broadcast(128), writes=[bsp_trk])
        self.load(gvB, self.gmlp_v_norm[l].partition_broadcast(128), writes=[gvB_trk])
        for g in range(8):
            bi = g % 2
            cx.op("pe", lambda: nc.tensor.transpose(self.bank[bi][:, 0:128], wsr[:, g, :], self.ident),
                  reads=[wsr_trk, ct], writes=[self.bank_trk[bi]])
            cx.op("dve", lambda: nc.vector.tensor_tensor(out=mixp[:, g, :], in0=self.bank[bi][:, 0:128],
                                                         in1=self.tri, op=ALU.mult),
                  reads=[self.bank_trk[bi], ct], writes=[mixp_trk])
            cx.op("dve", lambda: nc.vector.tensor_copy(out=x44[:, g, :], in_=self.bank[bi][:, 0:4]),
                  reads=[self.bank_trk[bi]], writes=[x44_trk])
        cx.mm_group(self.bank[2][:, 0:32], [(self.rep, x44.rearrange("a g t -> a (g t)"))],
                    reads=[ct, x44_trk], writes=[self.bank_trk[2]])
        cx.op("dve", lambda: nc.vector.tensor_copy(out=t1s.rearrange("p g t -> p (g t)"), in_=self.bank[2][:, 0:32]),
              reads=[self.bank_trk[2]], writes=[t1s_trk])
        for g in range(8):
            cx.op("dve", lambda: nc.vector.tensor_tensor(
                out=mixs[:, g, :].rearrange("p (a b) -> p a b", b=4),
                in0=t1s[:, g:g + 1, :].broadcast_to([128, 32, 4]),
                in1=self.bdm.rearrange("p (a b) -> p a b", b=4), op=ALU.mult),
                reads=[t1s_trk, ct], writes=[mixs_trk])
        cx.op("dve", lambda: nc.vector.tensor_copy(
            out=bss.rearrange("p g (a b) -> p g a b", b=4),
            in_=bsp[:, :, 0:4].unsqueeze(2).broadcast_to([128, 8, 32, 4])),
            reads=[bsp_trk], writes=[bss_trk])

        slotA = self.w_acquire(f"{tag}gmlpA_{l}")
        slotB = self.w_acquire(f"{tag}gmlpB_{l}")
        sA, sB = self.wslot_trk[slotA], self.wslot_trk[slotB]
        Wu = self.w_view(slotA, 0, 8192, [NCH, D])
        WvA = self.w_view(slotA, 8192, 4096, [4, D])
        WvB = self.w_view(slotB, 0, 4096, [4, D])
        Wo = self.w_view(slotB, 4096, 8192, [NCH, D])
        kk = 0
        for t in range(len(self.tiles)):
            off, W = self.tiles[t]
            sample = (off >= NPT)
            ts = self.tsl(t)
            self.norm_tile(t, gi, xt, xtr, tmp)
            for fc in range(NCH):
                bi = fc % 2
                cx.mm_group(self.bank[bi][:, 0:W], [(Wu[:, c, fc * 128:(fc + 1) * 128], xt[:, c, 0:W]) for c in range(NCH)],
                            reads=[sA] + xtr, writes=[self.bank_trk[bi]])
                cx.op("act", lambda: nc.scalar.activation(out=uT[:, fc, 0:W], in_=self.bank[bi][:, 0:W], func=AF.Copy),
                      reads=[self.bank_trk[bi]], writes=[uT_trk])
            blks = self.blocks(t)
            for blk, (tok0, bp) in enumerate(blks):
                pv = self.pp[1 + (blk % 2)]
                pv_trk = [self.bank_trk[2 + 2 * (blk % 2)], self.bank_trk[3 + 2 * (blk % 2)]]
                bs = slice(blk * 128, blk * 128 + bp)
                for hf in range(2):
                    fs = slice(hf * 512, (hf + 1) * 512)
                    cx.mm_group(pv[0:bp, fs], [(xt[:, c, bs], (WvA if c < 4 else WvB)[:, c % 4, fs]) for c in range(NCH)],
                                reads=[sA, sB] + xtr, writes=[pv_trk[hf]])
                cx.op("act", lambda: nc.scalar.activation(out=junk[0:bp], in_=pv[0:bp], func=AF.Square,
                                                          accum_out=ssv[0:bp, 0:1]),
                      reads=pv_trk, writes=[junk_trk, ssv_trk])
                cx.op("act", lambda: nc.scalar.activation(out=ssv[0:bp, 1:2], in_=ssv[0:bp, 0:1], func=AF.Ln,
                                                          bias=self.epsb[0:bp], scale=1.0 / D),
                      reads=[ssv_trk, ct], writes=[ssv_trk])
                cx.op("act", lambda: nc.scalar.activation(out=ssv[0:bp, 1:2], in_=ssv[0:bp, 1:2], func=AF.Exp, scale=-0.5),
                      reads=[ssv_trk], writes=[ssv_trk])
                if sample:
                    cx.op("dve", lambda: nc.vector.scalar_tensor_tensor(
                        out=vf[0:bp], in0=pv[0:bp], scalar=ssv[0:bp, 1:2], in1=gvB[0:bp], op0=ALU.mult, op1=ALU.mult),
                        reads=pv_trk + [ssv_trk, gvB_trk], writes=[vf_trk])
                    if emit_gv:
                        self.store(self.gv_out[l, tok0 - NPT:tok0 - NPT + bp, :], vf[0:bp], reads=[vf_trk])
                    cx.op("dve", lambda: nc.vector.tensor_copy(out=vn[0:bp, blk, :], in_=vf[0:bp]),
                          reads=[vf_trk], writes=[vn_trk[blk]])
                else:
                    cx.op("dve", lambda: nc.vector.scalar_tensor_tensor(
                        out=vn[0:bp, blk, :], in0=pv[0:bp], scalar=ssv[0:bp, 1:2], in1=gvB[0:bp], op0=ALU.mult, op1=ALU.mult),
                        reads=pv_trk + [ssv_trk, gvB_trk], writes=[vn_trk[blk]])
            mix, mix_trk = (mixs, mixs_trk) if sample else (mixp, mixp_trk)
            bsB, bsB_trk = (bss, bss_trk) if sample else (bsp, bsp_trk)
            nb = len(blks)
            for g in range(8):
                bi = g % 2
                for blk, (tok0, bp) in enumerate(blks):
                    cx.mm_group(self.bank[bi][:, blk * 128:blk * 128 + bp],
                                [(vn[0:bp, blk, g * 128:(g + 1) * 128], mix[0:bp, g, 0:bp])],
                                reads=[vn_trk[blk], mix_trk], writes=[self.bank_trk[bi]])
                tt, tt_trk = t1[kk % 2]
                kk += 1
                if nb == 4:
                    cx.op("dve", lambda: nc.vector.tensor_tensor(
                        out=tt.rearrange("p (a b) -> p a b", a=4),
                        in0=self.bank[bi].rearrange("p (a b) -> p a b", a=4),
                        in1=bsB[:, g:g + 1, :].broadcast_to([128, 4, 128]), op=ALU.add),
                        reads=[self.bank_trk[bi], bsB_trk], writes=[tt_trk])
                else:
                    cx.op("dve", lambda: nc.vector.tensor_tensor(
                        out=tt[:, 0:W], in0=self.bank[bi][:, 0:W], in1=bsB[:, g, 0:W], op=ALU.add),
                        reads=[self.bank_trk[bi], bsB_trk], writes=[tt_trk])
                cx.op("dve", lambda: nc.vector.tensor_tensor(out=us[:, g, 0:W], in0=tt[:, 0:W], in1=uT[:, g, 0:W],
                                                             op=ALU.mult),
                      reads=[tt_trk, uT_trk], writes=[us_trk])
            for fc in range(NCH):
                bi = 6 + fc % 2
                cx.mm_group(self.bank[bi][:, 0:W], [(Wo[:, g, fc * 128:(fc + 1) * 128], us[:, g, 0:W]) for g in range(NCH)],
                            reads=[sB, us_trk], writes=[self.bank_trk[bi]])
                cx.op("dve", lambda: nc.vector.tensor_tensor(out=self.h[:, fc, ts], in0=self.bank[bi][:, 0:W],
                                                             in1=self.h[:, fc, ts], op=ALU.add),
                      reads=[self.bank_trk[bi], self.h_trk[t][fc]], writes=[self.h_trk[t][fc]])
        self.w_release()
        self.w_release()

    def ple(self, l, pe_src, tag=""):
        nc, cx = self.nc, self.cx
        self.phase()
        gi = 4 * l + 3
        ct = self.const_trk
        xn = [self.palloc([128, NCH, TILE], BF16, f"xn_t{i}") for i in range(2)]
        xn_trks = [[self.ptrk() for c in range(NCH)] for i in range(2)]
        tmp = self.norm_tmp()
        peT = [self.palloc([128, 2, TILE], BF16, f"peT{i}") for i in range(2)]
        perow = [self.palloc([128, PLE], F32, f"perow{i}") for i in range(2)]
        gs = [self.palloc([128, TILE], F32, f"gs{i}") for i in range(2)]
        tm = [self.palloc([128, TILE], F32, f"tm{i}") for i in range(2)]
        slot = self.w_acquire(f"{tag}ple_{l}")
        strk = self.wslot_trk[slot]
        Wg = self.w_view(slot, 0, 8192, [NCH, D])
        Wp = self.w_view(slot, 8192, 2048, [2, D])
        kk = 0
        for t in range(len(self.tiles)):
            ts = self.tsl(t)
            W = self.tiles[t][1]
            pT, pT_trk = peT[t % 2]
            for blk, (tok0, bp) in enumerate(self.blocks(t)):
                pr, pr_trk = perow[blk % 2]
                self.load(pr[0:bp], pe_src[l, tok0:tok0 + bp, :], writes=[pr_trk])
                for c2 in range(2):
                    cx.op("pe", lambda: nc.tensor.transpose(self.bank[4 + c2][:, blk * 128:blk * 128 + bp],
                                                            pr[0:bp, c2 * 128:(c2 + 1) * 128], self.ident[0:bp, 0:bp]),
                          reads=[pr_trk, ct], writes=[self.bank_trk[4 + c2]])
            for c2 in range(2):
                cx.op("act", lambda: nc.scalar.activation(out=pT[:, c2, 0:W], in_=self.bank[4 + c2][:, 0:W], func=AF.Copy),
                      reads=[self.bank_trk[4 + c2]], writes=[pT_trk])
            xt, _ = xn[t % 2]
            xtr = xn_trks[t % 2]
            self.norm_tile(t, gi, xt, xtr, tmp)
            for fc in range(NCH):
                bg, bp_ = fc % 2, 2 + fc % 2
                fs = slice(fc * 128, (fc + 1) * 128)
                cx.mm_group(self.bank[bg][:, 0:W], [(Wg[:, c, fs], xt[:, c, 0:W]) for c in range(NCH)],
                            reads=[strk] + xtr, writes=[self.bank_trk[bg]])
                cx.mm_group(self.bank[bp_][:, 0:W], [(Wp[:, c2, fs], pT[:, c2, 0:W]) for c2 in range(2)],
                            reads=[strk, pT_trk], writes=[self.bank_trk[bp_]])
                g_sb, g_trk = gs[kk % 2]
                t_sb, t_trk = tm[kk % 2]
                kk += 1
                cx.op("act", lambda: nc.scalar.activation(out=g_sb[:, 0:W], in_=self.bank[bg][:, 0:W], func=AF.Sigmoid),
                      reads=[self.bank_trk[bg]], writes=[g_trk])
                cx.op("dve", lambda: nc.vector.tensor_tensor(out=t_sb[:, 0:W], in0=g_sb[:, 0:W], in1=self.bank[bp_][:, 0:W],
                                                             op=ALU.mult),
                      reads=[g_trk, self.bank_trk[bp_]], writes=[t_trk])
                cx.op("dve", lambda: nc.vector.tensor_tensor(out=self.h[:, fc, ts], in0=t_sb[:, 0:W], in1=self.h[:, fc, ts],
                                                             op=ALU.add),
                      reads=[t_trk, self.h_trk[t][fc]], writes=[self.h_trk[t][fc]])
        self.w_release()

    def shared_kv(self, prev, tag=""):
        nc, cx = self.nc, self.cx
        self.phase()
        gi = 16
        ct = self.const_trk
        xn = [self.palloc([128, NCH, TILE], BF16, f"xs_t{i}") for i in range(2)]
        xn_trks = [[self.ptrk() for c in range(NCH)] for i in range(2)]
        tmp = self.norm_tmp()
        kf = [self.palloc([128, D], F32, f"kf{i}") for i in range(2)]
        vf = [self.palloc([128, D], F32, f"vf{i}") for i in range(2)]
        kb = [self.palloc([128, D], BF16, f"kb{i}") for i in range(2)]
        vb = [self.palloc([128, D], BF16, f"vb{i}") for i in range(2)]
        kts = [self.palloc([128, NH, 128], BF16, f"kts{i}") for i in range(2)]
        sqj, sqj_trk = self.palloc([128, D], F32, "sqj")
        ss, ss_trk = self.palloc([128, 2 * NH], F32, "ss")
        fx, fx_trk = self.palloc([128, NH], F32, "fx")
        slotA = self.w_acquire(f"{tag}kvA")
        slotB = self.w_acquire(f"{tag}kvB")
        sA, sB = self.wslot_trk[slotA], self.wslot_trk[slotB]
        Wk = self.w_view(slotA, 0, 8192, [NCH, D])
        Wvf = self.w_view(slotB, 0, NCH * (D + NH), [NCH, D + NH])
        KT_dst = self.KTprev if prev else self.KTown
        V_dst = self.Vprev if prev else self.Vown
        lf_dst = self.lf_prev if prev else self.lf_all
        lf_trks = self.lfp_trk if prev else self.lf_trk
        ptb = self.bank[7].bitcast(BF16)
        b = 0
        for t in range(len(self.tiles)):
            xt, _ = xn[t % 2]
            xtr = xn_trks[t % 2]
            self.norm_tile(t, gi, xt, xtr, tmp)
            for blk, (tok0, bp) in enumerate(self.blocks(t)):
                bs = slice(blk * 128, blk * 128 + bp)
                pk, pk_trk = self.pp[0], self.bank_trk[0:2]
                pv, pv_trk = self.pp[1], self.bank_trk[2:4]
                pf, pf_trk = self.bank[4 + (b % 2)], self.bank_trk[4 + (b % 2)]
                for hf in range(2):
                    fs = slice(hf * 512, (hf + 1) * 512)
                    cx.mm_group(pk[0:bp, fs], [(xt[:, c, bs], Wk[:, c, fs]) for c in range(NCH)],
                                reads=[sA] + xtr, writes=[pk_trk[hf]])
                for hf in range(2):
                    fs = slice(hf * 512, (hf + 1) * 512)
                    cx.mm_group(pv[0:bp, fs], [(xt[:, c, bs], Wvf[:, c, fs]) for c in range(NCH)],
                                reads=[sB] + xtr, writes=[pv_trk[hf]])
                cx.mm_group(pf[0:bp, 0:NH], [(xt[:, c, bs], Wvf[:, c, D:D + NH]) for c in range(NCH)],
                            reads=[sB] + xtr, writes=[pf_trk])
                k_sb, k_trk = kf[b % 2]
                v_sb, v_trk = vf[b % 2]
                kb_sb, kb_trk = kb[b % 2]
                vb_sb, vb_trk = vb[b % 2]
                kt_sb, kt_trk = kts[b % 2]
                k_sb, v_sb, kb_sb, vb_sb = k_sb[0:bp], v_sb[0:bp], kb_sb[0:bp], vb_sb[0:bp]
                cx.op("act", lambda: nc.scalar.activation(out=k_sb, in_=pk[0:bp], func=AF.Copy),
                      reads=pk_trk, writes=[k_trk])
                cx.op("dve", lambda: nc.vector.tensor_tensor(out=sqj[0:bp], in0=k_sb, in1=k_sb, op=ALU.mult),
                      reads=[k_trk], writes=[sqj_trk])
                cx.op("dve", lambda: nc.vector.tensor_reduce(out=ss[0:bp, 0:NH],
                                                             in_=sqj[0:bp].rearrange("p (h d) -> p h d", h=NH),
                                                             axis=AX.X, op=ALU.add),
                      reads=[sqj_trk], writes=[ss_trk])
                cx.op("act", lambda: nc.scalar.activation(out=ss[0:bp, NH:2 * NH], in_=ss[0:bp, 0:NH], func=AF.Ln,
                                                          bias=self.epsb[0:bp], scale=1.0 / HD),
                      reads=[ss_trk, ct], writes=[ss_trk])
                cx.op("act", lambda: nc.scalar.activation(out=ss[0:bp, NH:2 * NH], in_=ss[0:bp, NH:2 * NH], func=AF.Exp,
                                                          scale=-0.5),
                      reads=[ss_trk], writes=[ss_trk])
                k3 = k_sb.rearrange("p (h d) -> p h d", h=NH)
                cx.op("dve", lambda: nc.vector.tensor_tensor(
                    out=k3, in0=k3, in1=ss[0:bp, NH:2 * NH].unsqueeze(2).broadcast_to([bp, NH, HD]), op=ALU.mult),
                    reads=[k_trk, ss_trk], writes=[k_trk])
                cx.op("dve", lambda: nc.vector.tensor_tensor(
                    out=k3, in0=k3, in1=self.gkB[0:bp].unsqueeze(1).broadcast_to([bp, NH, HD]), op=ALU.mult),
                    reads=[k_trk, ct], writes=[k_trk])
                if not prev:
                    self.store(self.k_out[tok0:tok0 + bp, :], k_sb, reads=[k_trk])
                cx.op("act", lambda: nc.scalar.activation(out=kb_sb, in_=k_sb, func=AF.Copy),
                      reads=[k_trk], writes=[kb_trk])
                for hh in range(NH):
                    cx.op("pe", lambda: nc.tensor.transpose(ptb[:, hh * 128:hh * 128 + bp],
                                                            kb_sb[:, hh * 128:(hh + 1) * 128], self.identb[0:bp, 0:bp]),
                          reads=[kb_trk, ct], writes=[self.bank_trk[7]])
                cx.op("dve", lambda: nc.vector.tensor_copy(
                    out=kt_sb[:, :, 0:bp], in_=ptb.rearrange("p (h t) -> p h t", h=NH)[:, :, 0:bp]),
                    reads=[self.bank_trk[7]], writes=[kt_trk])
                self.store(KT_dst.rearrange("h d t -> d h t")[:, :, tok0:tok0 + bp], kt_sb[:, :, 0:bp], reads=[kt_trk])
                cx.op("act", lambda: nc.scalar.activation(out=v_sb, in_=pv[0:bp], func=AF.Copy),
                      reads=pv_trk, writes=[v_trk])
                if not prev:
                    self.store(self.v_out[tok0:tok0 + bp, :], v_sb, reads=[v_trk])
                cx.op("dve", lambda: nc.vector.tensor_copy(out=vb_sb, in_=v_sb), reads=[v_trk], writes=[vb_trk])
                self.store(V_dst[tok0:tok0 + bp, :], vb_sb, reads=[vb_trk])
                lf = lf_dst[0:bp, b, :]
                cx.op("dve", lambda: nc.vector.tensor_tensor(out=fx[0:bp], in0=pf[0:bp, 0:NH], in1=self.bfB[0:bp], op=ALU.add),
                      reads=[pf_trk, ct], writes=[fx_trk])
                cx.op("act", lambda: nc.scalar.activation(out=fx[0:bp], in_=fx[0:bp], func=AF.Exp, scale=-1.0),
                      reads=[fx_trk], writes=[fx_trk])
                cx.op("act", lambda: nc.scalar.activation(out=fx[0:bp], in_=fx[0:bp], func=AF.Ln, bias=self.oneb[0:bp],
                                                          scale=1.0),
                      reads=[fx_trk, ct], writes=[fx_trk])
                cx.op("dve", lambda: nc.vector.tensor_scalar(out=lf, in0=fx[0:bp], scalar1=-1.0, scalar2=None, op0=ALU.mult),
                      reads=[fx_trk], writes=[lf_trks[b]])
                if not prev:
                    self.store(self.lf_out[tok0:tok0 + bp, :], lf, reads=[lf_trks[b]])
                b += 1
        self.w_release()
        self.w_release()

    def prep_forget(self):
        nc, cx = self.nc, self.cx
        ct = self.const_trk
        self.phase()
        tot, tot_trk = self.palloc([128, NH], F32, "tot")
        cn, cn_trk = self.palloc([128, NH], F32, "cn")
        b0, b1, b2 = self.bank[0], self.bank[1], self.bank[2]
        t0, t1, t2 = self.bank_trk[0], self.bank_trk[1], self.bank_trk[2]
        for i in range(16):
            pairs = [(self.tri, self.lf_all[:, i, :])] + [(self.ones_f, self.lf_all[:, j, :]) for j in range(i)]
            cx.mm_group(b0[:, i * NH:(i + 1) * NH], pairs, reads=[ct] + self.lf_trk[0:i + 1], writes=[t0])
        cx.op("dve", lambda: nc.vector.tensor_copy(out=self.csum_own.rearrange("p b h -> p (b h)"), in_=b0[:, 0:16 * NH]),
              reads=[t0], writes=[self.cs_trk])
        for i in range(16):
            pairs = [(self.tri, self.lf_prev[:, i, :])] + [(self.ones_f, self.lf_prev[:, j, :]) for j in range(i)]
            cx.mm_group(b1[:, i * NH:(i + 1) * NH], pairs, reads=[ct] + self.lfp_trk[0:i + 1], writes=[t1])
        cx.mm_group(b2[:, 0:NH], [(self.ones_f, self.lf_prev[:, j, :]) for j in range(16)],
                    reads=[ct] + self.lfp_trk, writes=[t2])
        cx.op("dve", lambda: nc.vector.tensor_scalar(out=tot, in0=b2[:, 0:NH], scalar1=self.flagB[:, 0:1], scalar2=None,
                                                     op0=ALU.add),
              reads=[t2, ct], writes=[tot_trk])
        cx.op("dve", lambda: nc.vector.scalar_tensor_tensor(
            out=self.tail_prev, in0=b1[:, 0:16 * NH].rearrange("p (b h) -> p b h", h=NH), scalar=-1.0,
            in1=tot.unsqueeze(1).broadcast_to([128, 16, NH]), op0=ALU.mult, op1=ALU.add),
            reads=[t1, tot_trk], writes=[self.cs_trk])
        for t in range(4):
            cx.mm_group(b2[:, 64 + t * NH:64 + (t + 1) * NH], [(self.sel, self.csum_own[:, 4 * t + 1, :])],
                        reads=[ct, self.cs_trk], writes=[t2])
        cx.op("dve", lambda: nc.vector.tensor_copy(out=self.Rt.rearrange("p t h -> p (t h)"), in_=b2[:, 64:64 + 4 * NH]),
              reads=[t2], writes=[self.cs_trk])
        cx.mm_group(b2[0:NST, 128:128 + NH], [(self.bdm[0:NST, 0:NST], self.lf_all[0:NST, 16, :])],
                    reads=[ct, self.lf_trk[16]], writes=[t2])
        cx.op("dve", lambda: nc.vector.tensor_scalar(out=self.ncn[0:NST], in0=b2[0:NST, 128:128 + NH], scalar1=-1.0,
                                                     scalar2=None, op0=ALU.mult),
              reads=[t2], writes=[self.cs_trk])

    def prep_past(self):
        nc, cx = self.nc, self.cx
        ct = self.const_trk
        self.phase()
        assert self.bg_nC >= self.bg_total
        tots, tots_trk = self.palloc([128, NPG, NH], F32, "tots")
        suf, suf_trk = self.palloc([128, NPG, NH], F32, "suf")
        for i in range(NSQ):
            lf2 = self.tail_all[:, i].rearrange("p g h -> p (g h)")
            cx.mm_group(self.bank[0][:, 0:128], [(self.su, lf2)], reads=[ct, self.tail_trk], writes=[self.bank_trk[0]])
            cx.mm_group(self.bank[1][:, 0:128], [(self.ones_f, lf2)], reads=[ct, self.tail_trk], writes=[self.bank_trk[1]])
            cx.op("act", lambda: nc.scalar.activation(out=tots.rearrange("p g h -> p (g h)"), in_=self.bank[1][:, 0:128],
                                                      func=AF.Copy),
                  reads=[self.bank_trk[1]], writes=[tots_trk])
            cx.op("dve", lambda: nc.vector.memset(suf[:, NPG - 1, :], 0.0), writes=[suf_trk])
            for pg in range(NPG - 2, -1, -1):
                cx.op("dve", lambda: nc.vector.tensor_tensor(out=suf[:, pg, :], in0=suf[:, pg + 1, :], in1=tots[:, pg + 1, :],
                                                             op=ALU.add),
                      reads=[suf_trk, tots_trk], writes=[suf_trk])
            cx.op("dve", lambda: nc.vector.tensor_tensor(
                out=lf2, in0=self.bank[0][:, 0:128], in1=suf.rearrange("p g h -> p (g h)"), op=ALU.add),
                reads=[self.bank_trk[0], self.bank_trk[1], suf_trk], writes=[self.tail_trk])

    def attention(self, l):
        nc, cx = self.nc, self.cx
        self.phase()
        j = l - N_A
        gi = 4 * l + 1
        ct = self.const_trk
        QT, qt_trk = self.palloc([128, NH, TILE], BF16, "QT")
        OT, ot_trk = self.palloc([128, NH, TILE], BF16, "OT")
        recip, recip_trk = self.palloc([128, TILE], F32, "recip")
        self._att_mark = self.RT.off
        xt, _ = self.palloc([128, NCH, TILE], BF16, "xn_t0")
        xtr = [self.ptrk() for c in range(NCH)]
        tmp = self.norm_tmp()
        sq, sq_trk, rs, rs_trk = tmp
        KTb = [self.palloc([128, 4096], BF16, f"KTb{i}") for i in range(2)]
        Vb = [self.palloc([128, 32, HD], BF16, f"Vb{i}") for i in range(2)]
        pts = [self.palloc([128, TILE], BF16, f"pt{i}") for i in range(4)]
        bias_t, bias_trk = self.palloc([128, 32, NH], F32, "bias_t")
        slotA = self.w_acquire(f"attA_{l}")
        slotB = self.w_acquire(f"attB_{l}")
        sA, sB = self.wslot_trk[slotA], self.wslot_trk[slotB]
        Wq = self.w_view(slotA, 0, 8192, [NCH, D])
        Wo = self.w_view(slotB, 0, 8192, [NCH, D])
        PS = [(self.bank[0], self.bank_trk[0]), (self.bank[1], self.bank_trk[1]), (self.bank[7], self.bank_trk[7])]
        POs = [(self.bank[2], self.bank_trk[2]), (self.bank[4], self.bank_trk[4])]
        PDs = [(self.bank[3], self.bank_trk[3]), (self.bank[5], self.bank_trk[5])]

        def q_proj(t):
            W = self.tiles[t][1]
            self.norm_tile(t, gi, xt, xtr, tmp)
            for hh in range(NH):
                bi = hh % 2
                qp, qp_trk = self.bank[bi], self.bank_trk[bi]
                cx.mm_group(qp[:, 0:W], [(Wq[:, c, hh * 128:(hh + 1) * 128], xt[:, c, 0:W]) for c in range(NCH)],
                            reads=[sA] + xtr, writes=[qp_trk])
                sqh = sq[hh % 2][:, 0, 0:W]
                cx.op("act", lambda: nc.scalar.activation(out=sqh, in_=qp[:, 0:W], func=AF.Square),
                      reads=[qp_trk], writes=[sq_trk[hh % 2]])
                cx.mm_group(self.bank[6][:, 0:W], [(self.ones_hd, sqh)], reads=[ct, sq_trk[hh % 2]],
                            writes=[self.bank_trk[6]])
                cx.op("act", lambda: nc.scalar.activation(out=rs[:, 0:W], in_=self.bank[6][:, 0:W], func=AF.Ln,
                                                          bias=self.epsb, scale=1.0),
                      reads=[self.bank_trk[6], ct], writes=[rs_trk])
                cx.op("act", lambda: nc.scalar.activation(out=rs[:, 0:W], in_=rs[:, 0:W], func=AF.Exp, scale=-0.5),
                      reads=[rs_trk], writes=[rs_trk])
                cx.op("dve", lambda: nc.vector.scalar_tensor_tensor(
                    out=QT[:, hh, 0:W], in0=qp[:, 0:W], scalar=self.gqcol[:, j:j + 1], in1=rs[:, 0:W],
                    op0=ALU.mult, op1=ALU.mult),
                    reads=[qp_trk, ct, rs_trk], writes=[qt_trk])

        def o_proj(t):
            W = self.tiles[t][1]
            ts = self.tsl(t)
            for fc in range(NCH):
                bi = fc % 2
                cx.mm_group(self.bank[bi][:, 0:W], [(Wo[:, hh, fc * 128:(fc + 1) * 128], OT[:, hh, 0:W]) for hh in range(NH)],
                            reads=[sB, ot_trk], writes=[self.bank_trk[bi]])
                cx.op("dve", lambda: nc.vector.tensor_tensor(out=self.h[:, fc, ts], in0=self.bank[bi][:, 0:W],
                                                             in1=self.h[:, fc, ts], op=ALU.add),
                      reads=[self.bank_trk[bi], self.h_trk[t][fc]], writes=[self.h_trk[t][fc]])

        units = [(t, hh) for t in range(4) for hh in range(NH)]

        def load_kv(u):
            t, hh = units[u]
            kt, kt_trk = KTb[u % 2]
            vv, vv_trk = Vb[u % 2]
            nown = (t + 1) * 512
            self.load(kt[:, 0:NPT], self.KTprev[hh], writes=[kt_trk])
            self.load(kt[:, NPT:NPT + nown], self.KTown[hh, :, 0:nown], writes=[kt_trk])
            self.load(vv[:, 0:16, :],
                      self.Vprev.rearrange("(b p) (h d) -> p b h d", p=128, h=NH)[:, :, hh, :], writes=[vv_trk])
            self.load(vv[:, 16:16 + 4 * (t + 1), :],
                      self.Vown[0:nown].rearrange("(b p) (h d) -> p b h d", p=128, h=NH)[:, :, hh, :], writes=[vv_trk])

        def attend(u):
            t, hh = units[u]
            kt, kt_trk = KTb[u % 2]
            vv, vv_trk = Vb[u % 2]
            PO, po_trk = POs[u % 2]
            PD, pd_trk = PDs[u % 2]
            blocks = [(b, 0) for b in range(16 + 4 * t)] + [(16 + 4 * t + m, 128 * m) for m in range(4)]
            n = len(blocks)

            def emit_s(i):
                b, qo = blocks[i]
                N = TILE - qo
                ps, ps_trk = PS[i % 3]
                pt, pt_trk = pts[i % 4]
                cx.mm(ps[:, 0:N], kt[:, b * 128:(b + 1) * 128], QT[:, hh, qo:TILE], True, True,
                      reads=[kt_trk, qt_trk], writes=[ps_trk])
                cx.op("act", lambda: nc.scalar.activation(out=pt[:, 0:N], in_=ps[:, 0:N], func=AF.Exp,
                                                          bias=bias_t[:, b, hh:hh + 1], scale=SCALE),
                      reads=[ps_trk, bias_trk], writes=[pt_trk])
                if b >= 16 + 4 * t:
                    cx.op("dve", lambda: nc.vector.tensor_tensor(out=pt[:, 0:128], in0=pt[:, 0:128], in1=self.tri,
                                                                 op=ALU.mult),
                          reads=[pt_trk, ct], writes=[pt_trk])

            def emit_pv(i):
                b, qo = blocks[i]
                N = TILE - qo
                pt, pt_trk = pts[i % 4]
                cx.mm(PO[:, qo:TILE], vv[:, b, :], pt[:, 0:N], i == 0, i == n - 1,
                      reads=[vv_trk, pt_trk], writes=[po_trk])
                cx.mm(PD[:, qo:TILE], self.ones_bf, pt[:, 0:N], i == 0, i == n - 1,
                      reads=[ct, pt_trk], writes=[pd_trk])

            emit_s(0)
            emit_s(1)
            for i in range(n):
                if i + 2 < n:
                    emit_s(i + 2)
                emit_pv(i)
            cx.op("dve", lambda: nc.vector.reciprocal(out=recip, in_=PD), reads=[pd_trk], writes=[recip_trk])
            cx.op("dve", lambda: nc.vector.tensor_tensor(out=OT[:, hh, :], in0=PO, in1=recip, op=ALU.mult),
                  reads=[po_trk, recip_trk], writes=[ot_trk])

        load_kv(0)
        for t in range(4):
            q_proj(t)
            nb = 4 * (t + 1)
            cx.op("dve", lambda: nc.vector.tensor_tensor(
                out=bias_t[:, 0:16, :], in0=self.tail_prev, in1=self.Rt[:, t:t + 1, :].broadcast_to([128, 16, NH]),
                op=ALU.add), reads=[self.cs_trk], writes=[bias_trk])
            cx.op("dve", lambda: nc.vector.scalar_tensor_tensor(
                out=bias_t[:, 16:16 + nb, :], in0=self.csum_own[:, 0:nb, :], scalar=-1.0,
                in1=self.Rt[:, t:t + 1, :].broadcast_to([128, nb, NH]), op0=ALU.mult, op1=ALU.add),
                reads=[self.cs_trk], writes=[bias_trk])
            for hh in range(NH):
                u = t * NH + hh
                if u + 1 < len(units):
                    load_kv(u + 1)
                attend(u)
            o_proj(t)

        t = 4
        q_proj(t)
        if self.do_sample_attn:
            self.sample_attention(QT, qt_trk, OT, ot_trk, recip, recip_trk)
        else:
            cx.op("dve", lambda: nc.vector.memset(OT[:, :, 0:NST], 0.0), writes=[ot_trk])
        o_proj(t)
        self.w_release()
        self.w_release()

    def sample_attention(self, QT, qt_trk, OT, ot_trk, recip, recip_trk):
        nc, cx = self.nc, self.cx
        ct = self.const_trk
        PSs = [(self.bank[0], self.bank_trk[0]), (self.bank[1], self.bank_trk[1])]
        PO, po_trk = self.bank[2], self.bank_trk[2]
        PD, pd_trk = self.bank[3], self.bank_trk[3]
        PN, pn_trk = self.bank[4], self.bank_trk[4]
        self.RT.off = self._att_mark
        self._phase_barrier = cx.barrier_events()
        KTn, ktn_trk = self.palloc([128, NH, NST], BF16, "KTn")
        Vn, vn_trk = self.palloc([128, D], BF16, "Vn")
        tn, tn_trk = self.palloc([128, NH, NST], F32, "tn")
        pn, pnb_trk = self.palloc([128, NH, NST], BF16, "pn")
        ktg = [self.palloc([128, PGG, D], BF16, f"ktg{i}") for i in range(2)]
        Vh = [self.palloc([128, PGG, D], BF16, f"Vh{i}") for i in range(2)]
        tsc, tsc_trk = self.palloc([128, PGG, NH, 4], F32, "tsc")
        psb = [self.palloc([128, PGG, NH, 4], BF16, f"psb{i}") for i in range(2)]
        PO3 = PO.rearrange("p (h q) -> p h q", h=NH)
        PD3 = PD.rearrange("p (h q) -> p h q", h=NH)
        self.load(KTn, self.KTown.rearrange("h d t -> d h t")[:, :, NPT:NT], writes=[ktn_trk])
        self.load(Vn[0:NST], self.Vown[NPT:NT, :], writes=[vn_trk])
        for hh in range(NH):
            cx.mm(PN[0:NST, hh * NST:(hh + 1) * NST], KTn[:, hh, :], QT[:, hh, 0:NST], True, True,
                  reads=[ktn_trk, qt_trk], writes=[pn_trk])
        cx.op("dve", lambda: nc.vector.scalar_tensor_tensor(
            out=tn[0:NST], in0=PN[0:NST].rearrange("p (h q) -> p h q", h=NH), scalar=SCALE,
            in1=self.ncn[0:NST].unsqueeze(2).broadcast_to([NST, NH, NST]), op0=ALU.mult, op1=ALU.add),
            reads=[pn_trk, self.cs_trk], writes=[tn_trk])
        cx.op("act", lambda: nc.scalar.activation(out=pn[0:NST], in_=tn[0:NST], func=AF.Exp),
              reads=[tn_trk], writes=[pnb_trk])
        cx.op("dve", lambda: nc.vector.tensor_tensor(
            out=pn[0:NST], in0=pn[0:NST], in1=self.bdm[0:NST, 0:NST].unsqueeze(1).broadcast_to([NST, NH, NST]),
            op=ALU.mult), reads=[pnb_trk, ct], writes=[pnb_trk])
        zt, zt_trk = self.palloc([128, TILE], BF16, "zt")
        cx.op("dve", lambda: nc.vector.memset(zt, 0.0), writes=[zt_trk])
        cx.mm(PO, self.ones_bf, zt, True, False, reads=[ct, zt_trk], writes=[po_trk])
        cx.mm(PD, self.ones_bf, zt, True, False, reads=[ct, zt_trk], writes=[pd_trk])
        for hh in range(NH):
            cx.mm(PO3[:, hh, :], Vn[0:NST, hh * 128:(hh + 1) * 128], pn[0:NST, hh, :], False, False,
                  reads=[vn_trk, pnb_trk], writes=[po_trk])
            cx.mm(PD3[:, hh, :], self.ones_bf[0:NST, :], pn[0:NST, hh, :], False, False,
                  reads=[ct, pnb_trk], writes=[pd_trk])
        groups = [(i, half) for i in range(NSQ) for half in range(NPG // PGG)]

        def emit_scores(g):
            i, half = groups[g]
            ps, ps_trk = PSs[g % 2]
            vh, vh_trk = Vh[g % 2]
            kg, kg_trk = ktg[g % 2]
            ps4 = ps[:, 0:PGG * NH * 4].rearrange("p (g h q) -> p g h q", g=PGG, h=NH)
            j0 = i * NPG + half * PGG
            self.load(kg, self.KTpast[j0:j0 + PGG].rearrange("g d x -> d g x"), writes=[kg_trk])
            self.load(vh, self.Vpast[j0:j0 + PGG].rearrange("g t x -> t g x"), writes=[vh_trk])
            for pg in range(PGG):
                for hh in range(NH):
                    cx.mm(ps4[:, pg, hh, :], kg[:, pg, hh * 128:(hh + 1) * 128], QT[:, hh, 4 * i:4 * i + 4], True, True,
                          reads=[kg_trk, qt_trk], writes=[ps_trk])

        def emit_pv(g):
            i, half = groups[g]
            ps, ps_trk = PSs[g % 2]
            vh, vh_trk = Vh[g % 2]
            pb, pb_trk = psb[g % 2]
            ps4 = ps[:, 0:PGG * NH * 4].rearrange("p (g h q) -> p g h q", g=PGG, h=NH)
            cx.op("dve", lambda: nc.vector.scalar_tensor_tensor(
                out=tsc, in0=ps4, scalar=SCALE,
                in1=self.tail_all[:, i, half * PGG:(half + 1) * PGG, :].unsqueeze(3).broadcast_to([128, PGG, NH, 4]),
                op0=ALU.mult, op1=ALU.add), reads=[ps_trk, self.tail_trk], writes=[tsc_trk])
            cx.op("act", lambda: nc.scalar.activation(out=pb, in_=tsc, func=AF.Exp), reads=[tsc_trk], writes=[pb_trk])
            last = (half == NPG // PGG - 1)
            for pg in range(PGG):
                for hh in range(NH):
                    cx.mm(PO3[:, hh, 4 * i:4 * i + 4], vh[:, pg, hh * 128:(hh + 1) * 128], pb[:, pg, hh, :],
                          False, last and pg == PGG - 1, reads=[vh_trk, pb_trk], writes=[po_trk])
                cx.mm(PD3[:, :, 4 * i:4 * i + 4], self.ones_bf, pb[:, pg, :, :], False, last and pg == PGG - 1,
                      reads=[ct, pb_trk], writes=[pd_trk])

        emit_scores(0)
        for g in range(len(groups)):
            if g + 1 < len(groups):
                emit_scores(g + 1)
            emit_pv(g)
        cx.op("dve", lambda: nc.vector.reciprocal(out=recip, in_=PD), reads=[pd_trk], writes=[recip_trk])
        cx.op("dve", lambda: nc.vector.tensor_tensor(out=OT[:, :, 0:NST], in0=PO3, in1=recip.rearrange("p (h q) -> p h q", h=NH),
                                                     op=ALU.mult),
              reads=[po_trk, recip_trk], writes=[ot_trk])

    def build(self):
        attn = self.do_attn and self.n_layers > N_A
        self.bg_total = NSQ * NPG if (attn and self.do_sample_attn) else 0
        self.bg_nA = self.bg_nC = 0
        self.bg_bufs = None
        self.plan_weights()
        self.w_issue()
        self.w_issue()
        self.setup_consts()
        if attn:
            self.tiles = PREV_TILES
            self.load_x(self.xprev)
            for l in range(N_A):
                self.ffn(l, 1, "p")
                self.gmlp(l, "p", emit_gv=False)
                self.ffn(l, 2, "p")
                self.ple(l, self.pprev, "p")
            self.shared_kv(True, "p")
        self.tiles = MAIN_TILES
        self.load_x(self.xin)
        for l in range(self.n_layers):
            self.ffn(l, 1)
            if l < N_A:
                self.gmlp(l)
            elif attn:
                self.attention(l)
            self.ffn(l, 2)
            self.ple(l, self.pin)
            if l == N_A - 1:
                self.shared_kv(False)
                if attn:
                    self.prep_forget()
                    if self.do_sample_attn:
                        self.prep_past()
        self.store_y(self.y_out)
        self.cx.finish("sp")
        print(f"[kernel] instructions emitted: {self.cx.n_ins}")


def _consts():
    s = np.arange(128)
    ident = np.eye(128, dtype=np.float32)
    tri = (s[:, None] <= s[None, :]).astype(np.float32)
    bdm = ((s[:, None] // 4 == s[None, :] // 4) & (s[:, None] % 4 <= s[None, :] % 4)).astype(np.float32)
    rep = ((s[None, :] % 4 == s[:, None]) & (s[:, None] < 4)).astype(np.float32)
    su = (s[:, None] > s[None, :]).astype(np.float32)
    sel = np.zeros((128, 128), np.float32)
    sel[127, :] = 1.0
    iota = s.astype(np.int32).reshape(128, 1)
    return ident, tri, bdm, rep, su, sel, iota


def make_in_maps(inputs, n_cores=8, with_cache=True):
    f = lambda a: np.ascontiguousarray(np.asarray(a, dtype=np.float32))
    x_prompt = f(inputs["x_prompt"])
    x_sample = f(inputs["x_sample"]).reshape(128 * 4, D)
    p_prompt = f(inputs["p_prompt"])
    p_sample = f(inputs["p_sample"]).reshape(DEPTH, 128 * 4, PLE)
    norms = np.zeros((17, D), np.float32)
    for l in range(DEPTH):
        norms[4 * l + 0] = inputs["ffn1_norm"][l]
        norms[4 * l + 1] = inputs["mix_norm"][l]
        norms[4 * l + 2] = inputs["ffn2_norm"][l]
        norms[4 * l + 3] = inputs["ple_norm"][l]
    norms[16] = inputs["kv_norm"]
    ident, tri, bdm, rep, su, sel, iota = _consts()
    shared = {
        "norms": norms,
        "ple_w_gate": f(inputs["ple_w_gate"]), "ple_w_proj": f(inputs["ple_w_proj"]),
        "gmlp_w_in": f(inputs["gmlp_w_in"]), "gmlp_v_norm": f(inputs["gmlp_v_norm"]),
        "gmlp_w_s": f(inputs["gmlp_w_s"]), "gmlp_b_s": f(inputs["gmlp_b_s"]),
        "gmlp_w_out": f(inputs["gmlp_w_out"]),
        "w_kvf": f(inputs["w_kvf"]), "b_f": f(inputs["b_f"]).reshape(1, NH),
        "k_norm": f(inputs["k_norm"]).reshape(1, HD),
        "att_w_q": f(inputs["att_w_q"]), "q_norm": f(inputs["q_norm"]), "att_w_o": f(inputs["att_w_o"]),
        "c_ident": ident, "c_tri": tri, "c_bdm": bdm, "c_rep": rep, "c_su": su, "c_sel": sel, "c_iota": iota,
    }
    if with_cache:
        shared["cache_k"] = f(inputs["cache_k"]).reshape(NPOOL * 128, D)
        shared["cache_v"] = f(inputs["cache_v"]).reshape(NPOOL * 128, D)
        shared["cache_lf"] = f(inputs["cache_logf"]).reshape(NPOOL * 128, NH)
    else:
        shared["cache_k"] = np.zeros((128, D), np.float32)
        shared["cache_v"] = np.zeros((128, D), np.float32)
        shared["cache_lf"] = np.zeros((128, NH), np.float32)
    for which in (1, 2):
        for nm in ("w1", "w3", "w2"):
            shared[f"ffn{which}_{nm}"] = f(inputs[f"ffn{which}_{nm}"])
    page_table = np.asarray(inputs["page_table"], dtype=np.int32)
    maps = []
    for c in range(n_cores):
        s, hf = c // 2, c % 2
        m = dict(shared)
        sq = slice(NSQ * 4 * c, NSQ * 4 * (c + 1))
        m["xin"] = np.concatenate([x_prompt[s, hf * NPT:(hf + 1) * NPT], x_sample[sq]], axis=0)
        m["pin"] = np.concatenate([p_prompt[:, s, hf * NPT:(hf + 1) * NPT], p_sample[:, sq]], axis=1)
        m["xprev"] = np.ascontiguousarray(x_prompt[s, 0:NPT])
        m["pprev"] = np.ascontiguousarray(p_prompt[0:N_A, s, 0:NPT])
        m["flag"] = np.full((1, 1), NEG if hf == 0 else 0.0, np.float32)
        m["ptab"] = np.ascontiguousarray(page_table[NSQ * c:NSQ * (c + 1)].reshape(1, NSQ * NPG))
        maps.append(m)
    return maps


def assemble(results, n_cores=8):
    B, S = 4, 4096
    y_prompt = np.zeros((B, S, D), np.float32)
    k_prompt = np.zeros((B, S, NH, HD), np.float32)
    v_prompt = np.zeros((B, S, NH, HD), np.float32)
    lf_prompt = np.zeros((B, S, NH), np.float32)
    y_sample = np.zeros((128, 4, D), np.float32)
    k_sample = np.zeros((128, 4, NH, HD), np.float32)
    v_sample = np.zeros((128, 4, NH, HD), np.float32)
    lf_sample = np.zeros((128, 4, NH), np.float32)
    gv = np.zeros((N_A, 128, 4, D), np.float32)
    for c in range(n_cores):
        s, hf = c // 2, c % 2
        r = results[c]
        sl = slice(hf * NPT, (hf + 1) * NPT)
        bq = slice(NSQ * c, NSQ * (c + 1))
        y_prompt[s, sl] = r["y_out"][:NPT]
        k_prompt[s, sl] = r["k_out"][:NPT].reshape(NPT, NH, HD)
        v_prompt[s, sl] = r["v_out"][:NPT].reshape(NPT, NH, HD)
        lf_prompt[s, sl] = r["lf_out"][:NPT]
        y_sample[bq] = r["y_out"][NPT:].reshape(NSQ, 4, D)
        k_sample[bq] = r["k_out"][NPT:].reshape(NSQ, 4, NH, HD)
        v_sample[bq] = r["v_out"][NPT:].reshape(NSQ, 4, NH, HD)
        lf_sample[bq] = r["lf_out"][NPT:].reshape(NSQ, 4, NH)
        gv[:, bq] = r["gv_out"].reshape(N_A, NSQ, 4, D)
    return (y_prompt, y_sample, k_prompt, v_prompt, lf_prompt, k_sample, v_sample, lf_sample, gv)


def kernel(**inputs):
    n_cores = 8
    nc = bass.Bass("TRN2", target_bir_lowering=False)
    Builder(nc).build()
    in_maps = make_in_maps(inputs, n_cores)
    res = run_bass_kernel_spmd(nc, in_maps, core_ids=list(range(n_cores)))
    return assemble(res.results, n_cores)
```
